# Optimizing a Trainium2 kernel written in Bass

```python
import math
import jax, jax.numpy as jnp
from jax import lax
import numpy as np

D_MODEL = 2048
BATCH = 2
SEQ = 8192
DEPTH = 4

GRID_W = 64
CTX_LEN = 256

HEAD_DIM = 128
ATT_W = D_MODEL // 2
N_Q_HEADS = ATT_W // HEAD_DIM
N_KV_HEADS = N_Q_HEADS // 4
Q_GROUP = N_Q_HEADS // N_KV_HEADS
KV_W = N_KV_HEADS * HEAD_DIM
AXIS_DIM = HEAD_DIM // 2
ROPE_THETA = 10000.0
Q_BLOCK = 128
SSM_W = D_MODEL // 2
SSM_P = 16
SSM_G = SSM_W // SSM_P
SSM_N = 64
DT_MIN = 0.001
DT_MAX = 0.1
CONV_W = D_MODEL // 2
CONV_K = 31
D_FF = 4 * D_MODEL
N_BRANCH = 3
EPS = 1e-6

Q_END = ATT_W
K_END = Q_END + KV_W
V_END = K_END + KV_W
SSM_END = V_END + SSM_W
CONV_END = SSM_END + 2 * CONV_W
IN_COLS = CONV_END + N_BRANCH * D_MODEL
SPLITS = [Q_END, K_END, V_END, SSM_END, CONV_END]

kernel_name = 'hybrid_gated_s5_conformer_gqa_dit_block'


def rms_norm(x, g):
    xf = x.astype(jnp.float32)
    y = xf * lax.rsqrt(jnp.mean(xf * xf, axis=-1, keepdims=True) + EPS)
    return (y * g.astype(jnp.float32)).astype(x.dtype)


def layer_norm(x, g, b):
    xf = x.astype(jnp.float32)
    mu = jnp.mean(xf, axis=-1, keepdims=True)
    var = jnp.mean(jnp.square(xf - mu), axis=-1, keepdims=True)
    y = (xf - mu) * lax.rsqrt(var + EPS) * g.astype(jnp.float32) + b.astype(jnp.float32)
    return y.astype(x.dtype)


def modulate(x, g, shift, scale):
    return rms_norm(x, g) * (1.0 + scale) + shift


def axial_rope_tables(n_tokens):
    rows = n_tokens // GRID_W
    row_idx = jnp.repeat(jnp.arange(rows, dtype=jnp.float32), GRID_W)
    col_idx = jnp.tile(jnp.arange(GRID_W, dtype=jnp.float32), rows)
    inv_freq = ROPE_THETA ** (-jnp.arange(0, AXIS_DIM, 2, dtype=jnp.float32) / AXIS_DIM)
    ang = jnp.concatenate([row_idx[:, None] * inv_freq, col_idx[:, None] * inv_freq], axis=-1)
    return jnp.cos(ang), jnp.sin(ang)


def apply_axial_rope(x, cos, sin):
    L = x.shape[1]
    nf = AXIS_DIM // 2
    xf = x.astype(jnp.float32).reshape(x.shape[:-1] + (2, 2, nf))
    x1, x2 = xf[..., 0, :], xf[..., 1, :]
    c = cos.reshape(L, 1, 2, nf)
    s = sin.reshape(L, 1, 2, nf)
    out = jnp.stack([x1 * c - x2 * s, x2 * c + x1 * s], axis=-2)
    return out.reshape(x.shape).astype(x.dtype)


def blocked_gqa(q, k, v):
    B, Lq = q.shape[:2]
    nb = Lq // Q_BLOCK
    qb = jnp.moveaxis(q.reshape(B, nb, Q_BLOCK, N_KV_HEADS, Q_GROUP, HEAD_DIM), 1, 0)
    scale = HEAD_DIM ** -0.5

    def one_block(qi):
        s = jnp.einsum('bqhgd,bkhd->bhgqk', qi, k).astype(jnp.float32) * scale
        p = jax.nn.softmax(s, axis=-1).astype(v.dtype)
        return jnp.einsum('bhgqk,bkhd->bqhgd', p, v)

    o = lax.map(one_block, qb)
    return jnp.moveaxis(o, 0, 1).reshape(B, Lq, N_Q_HEADS * HEAD_DIM)


def ssm_discretise(a_re, a_im, log_dt, b_re, b_im):
    a_re = a_re.astype(jnp.float32)
    a_im = a_im.astype(jnp.float32)
    b_re = b_re.astype(jnp.float32)
    b_im = b_im.astype(jnp.float32)
    dt = jnp.exp(log_dt.astype(jnp.float32))[:, None]
    mag = jnp.exp(a_re * dt)
    lam_re = mag * jnp.cos(a_im * dt)
    lam_im = mag * jnp.sin(a_im * dt)
    den = a_re * a_re + a_im * a_im
    f_re = ((lam_re - 1.0) * a_re + lam_im * a_im) / den
    f_im = (lam_im * a_re - (lam_re - 1.0) * a_im) / den
    bb_re = f_re[..., None] * b_re - f_im[..., None] * b_im
    bb_im = f_re[..., None] * b_im + f_im[..., None] * b_re
    return lam_re, lam_im, bb_re, bb_im


def _complex_affine_combine(left, right):
    ar1, ai1, br1, bi1 = left
    ar2, ai2, br2, bi2 = right
    return (ar1 * ar2 - ai1 * ai2,
            ar1 * ai2 + ai1 * ar2,
            ar2 * br1 - ai2 * bi1 + br2,
            ar2 * bi1 + ai2 * br1 + bi2)


def ssm_states(u4, lam_re, lam_im, bb_re, bb_im, h0_re, h0_im):
    L = u4.shape[1]
    bu_re = jnp.einsum('blgp,gnp->blgn', u4, bb_re)
    bu_im = jnp.einsum('blgp,gnp->blgn', u4, bb_im)
    bu_re = bu_re.at[:, 0].add(lam_re * h0_re - lam_im * h0_im)
    bu_im = bu_im.at[:, 0].add(lam_re * h0_im + lam_im * h0_re)
    shape = (1, L) + lam_re.shape
    a_re = jnp.broadcast_to(lam_re, shape)
    a_im = jnp.broadcast_to(lam_im, shape)
    _, _, h_re, h_im = lax.associative_scan(_complex_affine_combine, (a_re, a_im, bu_re, bu_im), axis=1)
    return h_re, h_im


def ssm_readout(h_re, h_im, c_re, c_im):
    return (jnp.einsum('blgn,gpn->blgp', h_re, c_re.astype(jnp.float32))
            - jnp.einsum('blgn,gpn->blgp', h_im, c_im.astype(jnp.float32)))


def to_groups(u):
    B, L, _ = u.shape
    return u.astype(jnp.float32).reshape(B, L, SSM_G, SSM_P)


def s5_bidir_states(u4, disc_f, disc_b, h0_f, h0_b):
    hf = ssm_states(u4, *disc_f, *h0_f)
    hb_rev = ssm_states(u4[:, ::-1], *disc_b, *h0_b)
    return hf, hb_rev


def s5_bidir_output(u, hf, hb_rev, c_f, c_b, d):
    B, L, _ = u.shape
    y = ssm_readout(*hf, *c_f) + ssm_readout(*hb_rev, *c_b)[:, ::-1]
    return y.reshape(B, L, SSM_W) + d.astype(jnp.float32) * u.astype(jnp.float32)


def s5_glu(y, w_glu, b_glu):
    g = jax.nn.gelu(y).astype(w_glu.dtype)
    return g * jax.nn.sigmoid(g @ w_glu + b_glu)


def conformer_conv(z, conv_w, conv_b, ln_g, ln_b):
    a, gt = jnp.split(z, 2, axis=-1)
    u = a * jax.nn.sigmoid(gt)
    u = lax.conv_general_dilated(
        u, conv_w[:, None, :].astype(u.dtype), window_strides=(1,),
        padding=[(CONV_K // 2, CONV_K // 2)], dimension_numbers=('NWC', 'WIO', 'NWC'),
        feature_group_count=CONV_W) + conv_b
    return jax.nn.silu(layer_norm(u, ln_g, ln_b))


def merge_branches(att, y_ssm, z, gate_pre, b_gate, w_attn_o, w_glu, b_glu, w_ssm_o,
                   conv_w, conv_b, conv_ln_g, conv_ln_b, w_conv_o, w_out):
    g_att, g_ssm, g_conv = jnp.split(jax.nn.sigmoid(gate_pre + b_gate), N_BRANCH, axis=-1)
    y_att = att @ w_attn_o
    y_s = s5_glu(y_ssm, w_glu, b_glu).astype(att.dtype) @ w_ssm_o
    y_c = conformer_conv(z, conv_w, conv_b, conv_ln_g, conv_ln_b) @ w_conv_o
    return (g_att * y_att + g_ssm * y_s + g_conv * y_c) @ w_out


def sqrelu_mlp(h, w1, w2):
    return jnp.square(jax.nn.relu(h @ w1)) @ w2


def setup_inputs(seed: int = 0) -> dict:
    key = jax.random.key(seed)
    keys = iter(jax.random.split(key, 40))

    def nrm(shape, scale):
        return jax.random.normal(next(keys), shape, jnp.float32) * scale

    D = D_MODEL
    x = nrm((BATCH, SEQ, D), 1.0)
    c = nrm((BATCH, D), 1.0)
    ctx = nrm((BATCH, CTX_LEN, D), 1.0)
    c_ctx = nrm((D,), 1.0)
    norm1_g = 1.0 + nrm((DEPTH, D), 0.02)
    norm2_g = 1.0 + nrm((DEPTH, D), 0.02)
    w_mod = nrm((DEPTH, D, 6 * D), 0.5 * D ** -0.5)
    b_mod = nrm((DEPTH, 6 * D), 0.01)
    w_in = nrm((DEPTH, D, IN_COLS), D ** -0.5)
    b_gate = nrm((DEPTH, N_BRANCH * D), 0.01)
    q_norm_g = 1.0 + nrm((DEPTH, HEAD_DIM), 0.02)
    k_norm_g = 1.0 + nrm((DEPTH, HEAD_DIM), 0.02)
    w_attn_o = nrm((DEPTH, ATT_W, D), ATT_W ** -0.5)
    ssm_a_re = -0.5 + nrm((DEPTH, 2, SSM_G, SSM_N), 0.01)
    ssm_a_im = jnp.pi * jnp.arange(SSM_N, dtype=jnp.float32) + nrm((DEPTH, 2, SSM_G, SSM_N), 0.01)
    ssm_log_dt = jax.random.uniform(next(keys), (DEPTH, 2, SSM_G), jnp.float32,
                                    math.log(DT_MIN), math.log(DT_MAX))
    ssm_b_re = nrm((DEPTH, 2, SSM_G, SSM_N, SSM_P), (2 * SSM_P) ** -0.5)
    ssm_b_im = nrm((DEPTH, 2, SSM_G, SSM_N, SSM_P), (2 * SSM_P) ** -0.5)
    ssm_c_re = nrm((DEPTH, 2, SSM_G, SSM_P, SSM_N), SSM_N ** -0.5)
    ssm_c_im = nrm((DEPTH, 2, SSM_G, SSM_P, SSM_N), SSM_N ** -0.5)
    ssm_d = nrm((DEPTH, SSM_W), 0.5)
    w_glu = nrm((DEPTH, SSM_W, SSM_W), SSM_W ** -0.5)
    b_glu = nrm((DEPTH, SSM_W), 0.01)
    w_ssm_o = nrm((DEPTH, SSM_W, D), SSM_W ** -0.5)
    conv_w = nrm((DEPTH, CONV_K, CONV_W), CONV_K ** -0.5)
    conv_b = nrm((DEPTH, CONV_W), 0.01)
    conv_ln_g = 1.0 + nrm((DEPTH, CONV_W), 0.02)
    conv_ln_b = nrm((DEPTH, CONV_W), 0.01)
    w_conv_o = nrm((DEPTH, CONV_W, D), CONV_W ** -0.5)
    w_out = nrm((DEPTH, D, D), D ** -0.5)
    w_mlp1 = nrm((DEPTH, D, D_FF), D ** -0.5)
    w_mlp2 = nrm((DEPTH, D_FF, D), D_FF ** -0.5)
    final_g = 1.0 + nrm((D,), 0.02)
    return {'x': x, 'c': c, 'ctx': ctx, 'c_ctx': c_ctx,
            'norm1_g': norm1_g, 'norm2_g': norm2_g, 'w_mod': w_mod, 'b_mod': b_mod,
            'w_in': w_in, 'b_gate': b_gate, 'q_norm_g': q_norm_g, 'k_norm_g': k_norm_g,
            'w_attn_o': w_attn_o, 'ssm_a_re': ssm_a_re, 'ssm_a_im': ssm_a_im,
            'ssm_log_dt': ssm_log_dt, 'ssm_b_re': ssm_b_re, 'ssm_b_im': ssm_b_im,
            'ssm_c_re': ssm_c_re, 'ssm_c_im': ssm_c_im, 'ssm_d': ssm_d,
            'w_glu': w_glu, 'b_glu': b_glu, 'w_ssm_o': w_ssm_o,
            'conv_w': conv_w, 'conv_b': conv_b, 'conv_ln_g': conv_ln_g, 'conv_ln_b': conv_ln_b,
            'w_conv_o': w_conv_o, 'w_out': w_out, 'w_mlp1': w_mlp1, 'w_mlp2': w_mlp2,
            'final_g': final_g}


def reference(x, c, ctx, c_ctx, norm1_g, norm2_g, w_mod, b_mod, w_in, b_gate,
              q_norm_g, k_norm_g, w_attn_o,
              ssm_a_re, ssm_a_im, ssm_log_dt, ssm_b_re, ssm_b_im, ssm_c_re, ssm_c_im,
              ssm_d, w_glu, b_glu, w_ssm_o,
              conv_w, conv_b, conv_ln_g, conv_ln_b, w_conv_o,
              w_out, w_mlp1, w_mlp2, final_g):
    B, L, _ = x.shape
    C = ctx.shape[1]
    cos, sin = axial_rope_tables(L)
    silu_c = jax.nn.silu(c)
    silu_cc = jax.nn.silu(c_ctx)
    zero = jnp.zeros((B, SSM_G, SSM_N), jnp.float32)
    xc = ctx
    for l in range(DEPTH):
        ctx_out = l < DEPTH - 1
        mod = jnp.split((silu_c @ w_mod[l] + b_mod[l])[:, None, :], 6, axis=-1)
        modc = jnp.split(silu_cc @ w_mod[l] + b_mod[l], 6, axis=-1)

        h = modulate(x, norm1_g[l], mod[0], mod[1])
        hc = modulate(xc, norm1_g[l], modc[0], modc[1])
        q, k, v, u, z, gate_pre = jnp.split(h @ w_in[l], SPLITS, axis=-1)
        if ctx_out:
            qc, kc, vc, uc, zc, gate_prec = jnp.split(hc @ w_in[l], SPLITS, axis=-1)
        else:
            kc, vc, uc = jnp.split(hc @ w_in[l, :, Q_END:SSM_END], [KV_W, 2 * KV_W], axis=-1)

        q = apply_axial_rope(rms_norm(q.reshape(B, L, N_Q_HEADS, HEAD_DIM), q_norm_g[l]), cos, sin)
        k = apply_axial_rope(rms_norm(k.reshape(B, L, N_KV_HEADS, HEAD_DIM), k_norm_g[l]), cos, sin)
        v = v.reshape(B, L, N_KV_HEADS, HEAD_DIM)
        kc = rms_norm(kc.reshape(B, C, N_KV_HEADS, HEAD_DIM), k_norm_g[l])
        vc = vc.reshape(B, C, N_KV_HEADS, HEAD_DIM)
        att = blocked_gqa(q, jnp.concatenate([k, kc], axis=1), jnp.concatenate([v, vc], axis=1))

        disc_f = ssm_discretise(ssm_a_re[l, 0], ssm_a_im[l, 0], ssm_log_dt[l, 0], ssm_b_re[l, 0], ssm_b_im[l, 0])
        disc_b = ssm_discretise(ssm_a_re[l, 1], ssm_a_im[l, 1], ssm_log_dt[l, 1], ssm_b_re[l, 1], ssm_b_im[l, 1])
        c_f = (ssm_c_re[l, 0], ssm_c_im[l, 0])
        c_b = (ssm_c_re[l, 1], ssm_c_im[l, 1])
        hcf, hcb_rev = s5_bidir_states(to_groups(uc), disc_f, disc_b, (zero, zero), (zero, zero))
        h0_f = (hcf[0][:, -1], hcf[1][:, -1])
        h0_b = (hcb_rev[0][:, -1], hcb_rev[1][:, -1])
        hf, hb_rev = s5_bidir_states(to_groups(u), disc_f, disc_b, h0_f, h0_b)
        y_ssm = s5_bidir_output(u, hf, hb_rev, c_f, c_b, ssm_d[l])

        mix = merge_branches(att, y_ssm, z, gate_pre, b_gate[l], w_attn_o[l], w_glu[l], b_glu[l],
                             w_ssm_o[l], conv_w[l], conv_b[l], conv_ln_g[l], conv_ln_b[l],
                             w_conv_o[l], w_out[l])
        x = x + mod[2] * mix

        if ctx_out:
            qc = rms_norm(qc.reshape(B, C, N_Q_HEADS, HEAD_DIM), q_norm_g[l])
            attc = blocked_gqa(qc, kc, vc)
            y_ssmc = s5_bidir_output(uc, hcf, hcb_rev, c_f, c_b, ssm_d[l])
            mixc = merge_branches(attc, y_ssmc, zc, gate_prec, b_gate[l], w_attn_o[l], w_glu[l], b_glu[l],
                                  w_ssm_o[l], conv_w[l], conv_b[l], conv_ln_g[l], conv_ln_b[l],
                                  w_conv_o[l], w_out[l])
            xc = xc + modc[2] * mixc

        x = x + mod[5] * sqrelu_mlp(modulate(x, norm2_g[l], mod[3], mod[4]), w_mlp1[l], w_mlp2[l])
        if ctx_out:
            xc = xc + modc[5] * sqrelu_mlp(modulate(xc, norm2_g[l], modc[3], modc[4]), w_mlp1[l], w_mlp2[l])

    return rms_norm(x, final_g)
```

```python
import contextlib, math
import numpy as np
import concourse.bass as bass
import concourse.mybir as mybir
from concourse.bass_utils import run_bass_kernel_spmd

F32 = mybir.dt.float32
BF16 = mybir.dt.bfloat16
AF = mybir.ActivationFunctionType
ALU = mybir.AluOpType
AX = mybir.AxisListType

ENGS = ("pe", "act", "dve", "pool", "sp")
CONSERVATIVE = False


class Buf:
    __slots__ = ("name", "lw", "rd", "dsem", "dcnt", "lw_dma")

    def __init__(self, name):
        self.name = name
        self.lw = None
        self.rd = []
        self.dsem = None
        self.dcnt = 0
        self.lw_dma = False


class Prog:
    def __init__(self, nc):
        self.nc = nc
        self.ops = {e: [] for e in ENGS}
        self.cnt = {e: 0 for e in ENGS}
        self.waited = {e: {} for e in ENGS}
        self.nsem_dma = 0
        self.sems = {}
        self.final_waits = []

    def _need(self, eng, ev, waits):
        if ev is None:
            return
        k, v = ev
        if self.waited[eng].get(k, 0) >= v:
            return
        self.waited[eng][k] = v
        waits[k] = max(waits.get(k, 0), v)

    def _deps(self, eng, reads, writes, dma_dst=None, is_dma=False):
        waits = {}
        for b in reads:
            self._need(eng, b.lw, waits)
        for b in writes:
            if dma_dst is b and b.lw_dma and not b.rd:
                pass
            elif b.lw is not None and (is_dma or b.lw[0] != eng or (CONSERVATIVE and eng != "pe")):
                self._need(eng, b.lw, waits)
            for ev in b.rd:
                if is_dma or ev[0] != eng or (CONSERVATIVE and eng != "pe"):
                    self._need(eng, ev, waits)
        return list(waits.items())

    def op(self, eng, fn, reads=(), writes=()):
        waits = self._deps(eng, reads, writes)
        self.cnt[eng] += 1
        ev = (eng, self.cnt[eng])
        self.ops[eng].append((waits, fn, (eng, 1)))
        for b in reads:
            b.rd.append(ev)
        for b in writes:
            b.lw = ev
            b.rd = []
            b.lw_dma = False
        return ev

    def dma(self, eng, fn, dst, reads=(), extra_writes=()):
        waits = self._deps(eng, reads, (dst,) + tuple(extra_writes), dma_dst=dst, is_dma=True)
        if dst.dsem is None:
            dst.dsem = "d%d" % self.nsem_dma
            self.nsem_dma += 1
        dst.dcnt += 16
        ev = (dst.dsem, dst.dcnt)
        self.ops[eng].append((waits, fn, (dst.dsem, 16)))
        for b in reads:
            b.rd.append(ev)
        for b in (dst,) + tuple(extra_writes):
            b.lw = ev
            b.rd = []
            b.lw_dma = True
        return ev

    def wait_final(self, eng, bufs):
        waits = {}
        for b in bufs:
            self._need(eng, b.lw, waits)
        self.ops[eng].append((list(waits.items()), None, None))

    def emit(self):
        nc = self.nc
        import contextlib
        with contextlib.ExitStack() as st:
            for e in ENGS:
                self.sems[e] = st.enter_context(nc.semaphore("s_" + e))
            for i in range(self.nsem_dma):
                self.sems["d%d" % i] = st.enter_context(nc.semaphore("sd%d" % i))
            block = st.enter_context(nc.Block())
            sems = self.sems

            def runner(ename):
                def run(eng):
                    for waits, fn, inc in self.ops[ename]:
                        for k, v in waits:
                            eng.wait_ge(sems[k], v)
                        if fn is not None:
                            ins = fn(eng)
                            ins.then_inc(sems[inc[0]], inc[1])
                return run

            block.tensor(runner("pe"))
            block.scalar(runner("act"))
            block.vector(runner("dve"))
            block.gpsimd(runner("pool"))
            block.sync(runner("sp"))

    def stats(self):
        return {e: len(self.ops[e]) for e in ENGS}


D = 2048; NT = 2112; NLAT = 2048; NCTX = 64; SEG = 352; NSEG = 6
KC = 16
IN_COLS = 10752
EPS = 1e-6


class Ctx:
    def __init__(self, nc):
        self.nc = nc; self.P = Prog(nc); self.st = contextlib.ExitStack(); self.n = 0

    def sb(self, shape, dt, name=None):
        self.n += 1
        t = self.st.enter_context(self.nc.sbuf_tensor(name or ("t%d" % self.n), list(shape), dt))
        return t, Buf(name or ("t%d" % self.n))

    def ps(self, shape, dt=F32, name=None):
        self.n += 1
        t = self.st.enter_context(self.nc.psum_tensor(name or ("p%d" % self.n), list(shape), dt))
        return t, Buf(name or ("p%d" % self.n))

    def ring(self, n, shape, dt, psum=False):
        return Ring([self.ps(shape, dt) if psum else self.sb(shape, dt) for _ in range(n)])

    def din(self, name, shape, dt=F32):
        return self.nc.dram_tensor(name, list(shape), dt, kind="ExternalInput").ap()

    def dout(self, name, shape, dt=F32):
        return self.nc.dram_tensor(name, list(shape), dt, kind="ExternalOutput").ap(), Buf(name)


class Ring:
    def __init__(self, items):
        self.items = items; self.i = 0

    def next(self):
        it = self.items[self.i % len(self.items)]; self.i += 1
        return it


def make_ident(cx, dt=BF16):
    P = cx.P
    idf, Bidf = cx.sb([128, 128], F32)
    P.op("pool", lambda e: e.memset(idf[:, :], 0.0), writes=[Bidf])
    P.op("pool", lambda e: e.affine_select(out=idf[:, :], in_=idf[:, :], pattern=[[-1, 128]],
                                           compare_op=ALU.not_equal, fill=1.0, base=0, channel_multiplier=1),
         reads=[Bidf], writes=[Bidf])
    if dt == F32:
        return idf, Bidf
    idb, Bidb = cx.sb([128, 128], dt)
    P.op("dve", lambda e: e.tensor_copy(out=idb[:, :], in_=idf[:, :]), reads=[Bidf], writes=[Bidb])
    return idb, Bidb


def build_stage_m(nunits=48):
    nc = bass.Bass("TRN2", target_bir_lowering=False)
    cx = Ctx(nc); P = cx.P
    wm = cx.din("wm", [nunits, 128, KC, 128])
    cT = cx.din("cT", [128, KC, 3])
    bm = cx.din("bm", [128, nunits])
    out, Bout = cx.dout("modT", [128, nunits, 3])
    with cx.st:
        cs, Bcs = cx.sb([128, KC, 3], F32)
        cb, Bcb = cx.sb([128, KC, 3], BF16)
        bms, Bbms = cx.sb([128, nunits], F32)
        res, Bres = cx.sb([128, nunits, 3], F32)
        wr = cx.ring(3, [128, KC, 128], BF16)
        pr = cx.ring(2, [128, 4], F32, psum=True)
        P.dma("sp", lambda e: e.dma_start(out=cs[:, :, :], in_=cT[:, :, :]), Bcs)
        P.dma("sp", lambda e: e.dma_start(out=bms[:, :], in_=bm[:, :]), Bbms)
        P.op("act", lambda e: e.activation(out=cb[:, :, :], in_=cs[:, :, :], func=AF.Silu), reads=[Bcs], writes=[Bcb])
        for u in range(nunits):
            wt, Bw = wr.next()
            P.dma("pool", lambda e, wt=wt, u=u: e.dma_start(out=wt[:, :, :], in_=wm[u, :, :, :]), Bw)
            pt, Bp = pr.next()
            for kc in range(KC):
                P.op("pe", lambda e, wt=wt, pt=pt, kc=kc: e.matmul(pt[:, 0:3], lhsT=wt[:, kc, :], rhs=cb[:, kc, :],
                                                                   start=(kc == 0), stop=(kc == KC - 1)),
                     reads=[Bw, Bcb], writes=[Bp])
            P.op("dve", lambda e, pt=pt, u=u: e.tensor_scalar(out=res[:, u, :], in0=pt[:, 0:3], scalar1=bms[:, u:u + 1],
                                                              scalar2=None, op0=ALU.add),
                 reads=[Bp, Bbms], writes=[Bres])
        P.dma("sp", lambda e: e.dma_start(out=out[:, :, :], in_=res[:, :, :]), Bout, reads=[Bres])
        P.wait_final("sp", [Bout])
        P.emit()
    return nc


def rmsnorm_mod_segment(cx, eps_t, xs, Bxs, ones_f, Bones, sc_all, Bsc, hT, BhT, seg, tmp_ring, ps_ring, rstd_ring, local=False):
    P = cx.P
    pt, Bp = ps_ring.next()
    for kc in range(KC):
        sq, Bsq = tmp_ring.next()
        P.op("act", lambda e, sq=sq, kc=kc: e.activation(out=sq[:, :], in_=xs[:, kc, :], func=AF.Square),
             reads=[Bxs], writes=[Bsq])
        P.op("pe", lambda e, sq=sq, pt=pt, kc=kc: e.matmul(pt[:, 0:SEG], lhsT=ones_f[:, :], rhs=sq[:, :],
                                                           start=(kc == 0), stop=(kc == KC - 1)),
             reads=[Bsq, Bones], writes=[Bp])
    rstd, Br = rstd_ring.next()
    P.op("act", lambda e: e.activation(out=rstd[:, :], in_=pt[:, 0:SEG], func=AF.Sqrt, scale=1.0 / D, bias=eps_t[:, 0:1]),
         reads=[Bp], writes=[Br])
    P.op("dve", lambda e: e.reciprocal(out=rstd[:, :], in_=rstd[:, :]), reads=[Br], writes=[Br])
    t0 = seg * SEG
    ho = 0 if local else t0
    if seg == NSEG - 1:
        parts = [(0, NLAT - t0, 0), (NLAT - t0, SEG, 1)]
    else:
        parts = [(0, SEG, 0)]
    for kc in range(KC):
        tm, Bt = tmp_ring.next()
        eng = "dve" if kc % 2 == 0 else "pool"
        P.op(eng, lambda e, tm=tm, kc=kc: e.tensor_tensor(out=tm[:, :], in0=xs[:, kc, :], in1=rstd[:, :], op=ALU.mult),
             reads=[Bxs, Br], writes=[Bt])
        for (a, b, var) in parts:
            P.op(eng, lambda e, tm=tm, kc=kc, a=a, b=b, var=var: e.tensor_scalar(
                out=hT[:, kc, ho + a:ho + b], in0=tm[:, a:b], scalar1=sc_all[:, kc, var, 0:1], scalar2=sc_all[:, kc, var, 1:2],
                op0=ALU.mult, op1=ALU.add), reads=[Bt, Bsc], writes=[BhT])


def build_stage_a():
    nc = bass.Bass("TRN2", target_bir_lowering=False)
    cx = Ctx(nc); P = cx.P
    xT = cx.din("xT", [KC, 128, NT])
    w_in = cx.din("w_in", [D, IN_COLS])
    gT = cx.din("g1T", [128, KC])
    modss = cx.din("modss", [128, KC, 2, 2])
    qkg = cx.din("qkg", [128, 2])
    cosT = cx.din("cosT", [128, NT]); sinT = cx.din("sinT", [128, NT])
    permM = cx.din("permM", [128, 128])
    bgT = cx.din("bgT", [128, 48])
    o_q, Boq = cx.dout("qT", [8, 128, NT], BF16)
    o_k, Bok = cx.dout("kT", [2, 128, NT], BF16)
    o_v, Bov = cx.dout("v", [NT, 256], BF16)
    o_u, Bou = cx.dout("uT", [8, 128, NT], F32)
    o_uc, Bouc = cx.dout("ucT", [8, 128, NT], F32)
    o_g, Bog = cx.dout("gT", [48, 128, NT], BF16)
    w_v = w_in.rearrange("(kc p) n -> p kc n", p=128)
    with cx.st:
        ones_f, Bones = cx.sb([128, 128], F32)
        P.op("pool", lambda e: e.memset(ones_f[:, :], 1.0), writes=[Bones])
        eps_t, Beps = cx.sb([128, 1], F32)
        P.op("pool", lambda e: e.memset(eps_t[:, :], EPS), writes=[Beps])
        g1, Bg1 = cx.sb([128, KC], F32)
        ms, Bms = cx.sb([128, KC, 2, 2], F32)
        sc_all, Bsc = cx.sb([128, KC, 2, 2], F32)
        qk, Bqk = cx.sb([128, 2], F32)
        cs, Bcs = cx.sb([128, NT], F32); sn, Bsn = cx.sb([128, NT], F32)
        pm, Bpm = cx.sb([128, 128], F32)
        bg, Bbg = cx.sb([128, 48], F32)
        hT, BhT = cx.sb([128, KC, NT], BF16)
        P.dma("sp", lambda e: e.dma_start(out=g1[:, :], in_=gT[:, :]), Bg1)
        P.dma("sp", lambda e: e.dma_start(out=ms[:, :, :, :], in_=modss[:, :, :, :]), Bms)
        P.dma("sp", lambda e: e.dma_start(out=qk[:, :], in_=qkg[:, :]), Bqk)
        P.dma("sp", lambda e: e.dma_start(out=cs[:, :], in_=cosT[:, :]), Bcs)
        P.dma("sp", lambda e: e.dma_start(out=sn[:, :], in_=sinT[:, :]), Bsn)
        P.dma("sp", lambda e: e.dma_start(out=pm[:, :], in_=permM[:, :]), Bpm)
        P.dma("sp", lambda e: e.dma_start(out=bg[:, :], in_=bgT[:, :]), Bbg)
        for var in range(2):
            P.op("dve", lambda e, var=var: e.scalar_tensor_tensor(out=sc_all[:, :, var, 0], in0=ms[:, :, var, 1], scalar=1.0,
                                                                  in1=g1[:, :], op0=ALU.add, op1=ALU.mult),
                 reads=[Bms, Bg1], writes=[Bsc])
            P.op("dve", lambda e, var=var: e.tensor_copy(out=sc_all[:, :, var, 1], in_=ms[:, :, var, 0]),
                 reads=[Bms], writes=[Bsc])
        xr = cx.ring(1, [128, KC, SEG], F32)
        tmp_ring = cx.ring(4, [128, SEG], F32)
        rstd_ring = cx.ring(2, [128, SEG], F32)
        ps_aux = cx.ring(2, [128, 512], F32, psum=True)
        for seg in range(NSEG):
            xs, Bxs = xr.next()
            P.dma("sp", lambda e, xs=xs, seg=seg: e.dma_start(
                out=xs[:, :, :], in_=xT[:, :, seg * SEG:(seg + 1) * SEG].rearrange("k p t -> p k t")), Bxs)
            rmsnorm_mod_segment(cx, eps_t, xs, Bxs, ones_f, Bones, sc_all, Bsc, hT, BhT, seg, tmp_ring, ps_aux, rstd_ring)
        wr = cx.ring(2, [128, KC, 256], BF16)
        ps_mm = cx.ring(4, [128, 512], F32, psum=True)
        stg_b = cx.ring(3, [128, SEG], BF16)
        stg_f = cx.ring(3, [128, SEG], F32)
        sig_ring = cx.ring(2, [128, NT], F32)
        groups = [0, 2, 4, 6, 8, 10, 12, 14, 16, 18]
        for i in (0, 2, 4, 6):
            groups += [28 + i, 20 + i]
        groups += list(range(36, 84, 2))
        sig_tiles = {}
        for g0 in groups:
            wt, Bw = wr.next()
            P.dma("pool", lambda e, wt=wt, g0=g0: e.dma_start(out=wt[:, :, :], in_=w_v[:, :, g0 * 128:g0 * 128 + 256]), Bw)
            for j in range(2):
                ct = g0 + j
                if ct in (10, 11):
                    continue
                if 28 <= ct < 36:
                    sg, Bsg = sig_ring.next(); sig_tiles[ct - 8] = (sg, Bsg)
                for seg in range(NSEG):
                    t0 = seg * SEG
                    pt, Bp = ps_mm.next()
                    for kc in range(KC):
                        P.op("pe", lambda e, pt=pt, wt=wt, j=j, kc=kc, t0=t0: e.matmul(
                            pt[:, 0:SEG], lhsT=wt[:, kc, j * 128:(j + 1) * 128], rhs=hT[:, kc, t0:t0 + SEG],
                            start=(kc == 0), stop=(kc == KC - 1)), reads=[Bw, BhT], writes=[Bp])
                    if ct < 10:
                        gi = 0 if ct < 8 else 1
                        sq, Bsq = tmp_ring.next()
                        P.op("act", lambda e, sq=sq, pt=pt: e.activation(out=sq[:, :], in_=pt[:, 0:SEG], func=AF.Square),
                             reads=[Bp], writes=[Bsq])
                        pa, Bpa = ps_aux.next()
                        P.op("pe", lambda e, pa=pa, sq=sq: e.matmul(pa[:, 0:SEG], lhsT=ones_f[:, :], rhs=sq[:, :],
                                                                    start=True, stop=True), reads=[Bsq, Bones], writes=[Bpa])
                        rstd, Br = rstd_ring.next()
                        P.op("act", lambda e, rstd=rstd, pa=pa: e.activation(out=rstd[:, :], in_=pa[:, 0:SEG], func=AF.Sqrt,
                                                                             scale=1.0 / 128, bias=eps_t[:, 0:1]), reads=[Bpa], writes=[Br])
                        P.op("dve", lambda e, rstd=rstd: e.reciprocal(out=rstd[:, :], in_=rstd[:, :]), reads=[Br], writes=[Br])
                        xn, Bxn = tmp_ring.next()
                        P.op("dve", lambda e, xn=xn, pt=pt, rstd=rstd, gi=gi: e.scalar_tensor_tensor(
                            out=xn[:, :], in0=pt[:, 0:SEG], scalar=qk[:, gi:gi + 1], in1=rstd[:, :], op0=ALU.mult, op1=ALU.mult),
                            reads=[Bp, Bqk, Br], writes=[Bxn])
                        pa2, Bpa2 = ps_aux.next()
                        P.op("pe", lambda e, pa2=pa2, xn=xn: e.matmul(pa2[:, 0:SEG], lhsT=pm[:, :], rhs=xn[:, :],
                                                                      start=True, stop=True), reads=[Bxn, Bpm], writes=[Bpa2])
                        t1, Bt1 = tmp_ring.next()
                        P.op("pool", lambda e, t1=t1, xn=xn, t0=t0: e.tensor_tensor(out=t1[:, :], in0=xn[:, :], in1=cs[:, t0:t0 + SEG],
                                                                                   op=ALU.mult), reads=[Bxn, Bcs], writes=[Bt1])
                        t2, Bt2 = tmp_ring.next()
                        P.op("dve", lambda e, t2=t2, pa2=pa2, t0=t0: e.tensor_tensor(out=t2[:, :], in0=pa2[:, 0:SEG], in1=sn[:, t0:t0 + SEG],
                                                                                    op=ALU.mult), reads=[Bpa2, Bsn], writes=[Bt2])
                        ob, Bob = stg_b.next()
                        P.op("pool", lambda e, ob=ob, t1=t1, t2=t2: e.tensor_tensor(out=ob[:, :], in0=t1[:, :], in1=t2[:, :], op=ALU.add),
                             reads=[Bt1, Bt2], writes=[Bob])
                        if ct < 8:
                            P.dma("sp", lambda e, ob=ob, ct=ct, t0=t0: e.dma_start(out=o_q[ct, :, t0:t0 + SEG], in_=ob[:, :]), Boq, reads=[Bob])
                        else:
                            P.dma("sp", lambda e, ob=ob, ct=ct, t0=t0: e.dma_start(out=o_k[ct - 8, :, t0:t0 + SEG], in_=ob[:, :]), Bok, reads=[Bob])
                    elif ct < 20:
                        of, Bof = stg_f.next()
                        P.op("act", lambda e, of=of, pt=pt: e.activation(out=of[:, :], in_=pt[:, 0:SEG], func=AF.Copy),
                             reads=[Bp], writes=[Bof])
                        P.dma("sp", lambda e, of=of, ct=ct, t0=t0: e.dma_start(out=o_u[ct - 12, :, t0:t0 + SEG], in_=of[:, :]), Bou, reads=[Bof])
                    elif ct < 28:
                        sg, Bsg = sig_tiles[ct]
                        of, Bof = stg_f.next()
                        P.op("dve", lambda e, of=of, pt=pt, sg=sg, t0=t0: e.tensor_tensor(out=of[:, :], in0=pt[:, 0:SEG], in1=sg[:, t0:t0 + SEG],
                                                                                         op=ALU.mult), reads=[Bp, Bsg], writes=[Bof])
                        P.dma("sp", lambda e, of=of, ct=ct, t0=t0: e.dma_start(out=o_uc[ct - 20, :, t0:t0 + SEG], in_=of[:, :]), Bouc, reads=[Bof])
                    elif ct < 36:
                        P.op("act", lambda e, sg=sg, pt=pt, t0=t0: e.activation(out=sg[:, t0:t0 + SEG], in_=pt[:, 0:SEG], func=AF.Sigmoid),
                             reads=[Bp], writes=[Bsg])
                    else:
                        ob, Bob = stg_b.next()
                        P.op("act", lambda e, ob=ob, pt=pt, ct=ct: e.activation(out=ob[:, :], in_=pt[:, 0:SEG], func=AF.Sigmoid,
                                                                               bias=bg[:, ct - 36:ct - 35]), reads=[Bp, Bbg], writes=[Bob])
                        P.dma("sp", lambda e, ob=ob, ct=ct, t0=t0: e.dma_start(out=o_g[ct - 36, :, t0:t0 + SEG], in_=ob[:, :]), Bog, reads=[Bob])
            if g0 == 10:
                vstg = cx.ring(2, [128, 256], BF16)
                for tb in range(17):
                    n = 128 if tb < 16 else 64
                    pt, Bp = ps_mm.next()
                    for kc in range(KC):
                        P.op("pe", lambda e, pt=pt, wt=wt, kc=kc, tb=tb, n=n: e.matmul(
                            pt[0:n, 0:256], lhsT=hT[:, kc, tb * 128:tb * 128 + n], rhs=wt[:, kc, 0:256],
                            start=(kc == 0), stop=(kc == KC - 1)), reads=[Bw, BhT], writes=[Bp])
                    vs, Bvs = vstg.next()
                    P.op("act", lambda e, vs=vs, pt=pt, n=n: e.activation(out=vs[0:n, :], in_=pt[0:n, 0:256], func=AF.Copy),
                         reads=[Bp], writes=[Bvs])
                    P.dma("sp", lambda e, vs=vs, tb=tb, n=n: e.dma_start(out=o_v[tb * 128:tb * 128 + n, :], in_=vs[0:n, :]), Bov, reads=[Bvs])
        P.wait_final("sp", [Boq, Bok, Bov, Bou, Bouc, Bog])
        P.emit()
    return nc, P


NKEY = 8448; NKT = 66

def build_stage_b1():
    nc = bass.Bass("TRN2", target_bir_lowering=False)
    cx = Ctx(nc); P = cx.P
    qT = cx.din("qT", [8, 128, NT], BF16)
    kT = cx.din("kT", [2, 128, NKEY], BF16)
    vv = cx.din("v", [NKEY, 256], BF16)
    o_a, Boa = cx.dout("attT", [8, 128, NT], BF16)
    scale = 1.0 / math.sqrt(128.0)
    with cx.st:
        qs, Bqs = cx.sb([128, 8, NT], BF16)
        ks, Bks = cx.sb([128, 2, NKEY], BF16)
        vs, Bvs = cx.sb([128, NKT, 256], BF16)
        ones_b, Bones = cx.sb([128, 128], BF16)
        P.op("pool", lambda e: e.memset(ones_b[:, :], 1.0), writes=[Bones])
        P.dma("sp", lambda e: e.dma_start(out=qs[:, :, :], in_=qT.rearrange("h p t -> p h t")), Bqs)
        for h in range(2):
            P.dma("sp", lambda e, h=h: e.dma_start(out=ks[:, h, :], in_=kT[h, :, :]), Bks)
        P.dma("sp", lambda e: e.dma_start(out=vs[:, :, :], in_=vv.rearrange("(kt p) c -> p kt c", p=128)), Bvs)
        ps_s = cx.ring(3, [128, 512], F32, psum=True)
        ps_o = cx.ring(2, [128, 512], F32, psum=True)
        ps_d = cx.ring(2, [128, 512], F32, psum=True)
        pT = cx.ring(3, [128, 512], BF16)
        rc = cx.ring(2, [128, 512], F32)
        ob = cx.ring(2, [128, 512], BF16)
        qtiles = [(i * 512, 512, list(range(NKT))) for i in range(4)] + [(NLAT, NCTX, [64, 65])]
        for h in range(8):
            kv = h // 4
            for (q0, ql, kts) in qtiles:
                po, Bpo = ps_o.next(); pd, Bpd = ps_d.next()
                for i, kt in enumerate(kts):
                    pss, Bps = ps_s.next()
                    P.op("pe", lambda e, pss=pss, kt=kt, q0=q0, ql=ql, h=h, kv=kv: e.matmul(
                        pss[:, 0:ql], lhsT=ks[:, kv, kt * 128:(kt + 1) * 128], rhs=qs[:, h, q0:q0 + ql], start=True, stop=True),
                        reads=[Bks, Bqs], writes=[Bps])
                    pt, Bpt = pT.next()
                    P.op("act", lambda e, pt=pt, pss=pss, ql=ql: e.activation(out=pt[:, 0:ql], in_=pss[:, 0:ql], func=AF.Exp, scale=scale),
                         reads=[Bps], writes=[Bpt])
                    first = (i == 0); last = (i == len(kts) - 1)
                    P.op("pe", lambda e, po=po, pt=pt, kt=kt, kv=kv, ql=ql, first=first, last=last: e.matmul(
                        po[:, 0:ql], lhsT=vs[:, kt, kv * 128:(kv + 1) * 128], rhs=pt[:, 0:ql], start=first, stop=last),
                        reads=[Bvs, Bpt], writes=[Bpo])
                    P.op("pe", lambda e, pd=pd, pt=pt, ql=ql, first=first, last=last: e.matmul(
                        pd[:, 0:ql], lhsT=ones_b[:, :], rhs=pt[:, 0:ql], start=first, stop=last),
                        reads=[Bones, Bpt], writes=[Bpd])
                r, Br = rc.next()
                P.op("dve", lambda e, r=r, pd=pd, ql=ql: e.reciprocal(out=r[:, 0:ql], in_=pd[:, 0:ql]), reads=[Bpd], writes=[Br])
                o, Bo = ob.next()
                P.op("dve", lambda e, o=o, po=po, r=r, ql=ql: e.tensor_tensor(out=o[:, 0:ql], in0=po[:, 0:ql], in1=r[:, 0:ql], op=ALU.mult),
                     reads=[Bpo, Br], writes=[Bo])
                P.dma("sp", lambda e, o=o, h=h, q0=q0, ql=ql: e.dma_start(out=o_a[h, :, q0:q0 + ql], in_=o[:, 0:ql]), Boa, reads=[Bo])
        P.wait_final("sp", [Boa])
        P.emit()
    return nc, P


TT = 512

def build_stage_b2():
    nc = bass.Bass("TRN2", target_bir_lowering=False)
    cx = Ctx(nc); P = cx.P
    uT = cx.din("uT", [128, 2, NKEY])
    rowp = cx.din("rowp", [2, 3, 128, 128])
    colp = cx.din("colp", [2, 3, 128, 4])
    BT = cx.din("BT", [2, 2, 128, 128])
    CT = cx.din("CT", [2, 2, 128, 4, 128])
    dvec = cx.din("dvec", [128, 1])
    o_y, Boy = cx.dout("ysT", [128, 2, NKEY])
    with cx.st:
        halfpi, Bhp = cx.sb([128, 1], F32)
        P.op("pool", lambda e: e.memset(halfpi[:, :], math.pi / 2), writes=[Bhp])
        dv, Bdv = cx.sb([128, 1], F32)
        P.dma("sp", lambda e: e.dma_start(out=dv[:, :], in_=dvec[:, :]), Bdv)
        cnt = [0]
        def tmp(shape, dt=F32):
            return cx.sb(shape, dt)

        def ew(eng, fn, reads, writes):
            P.op(eng, fn, reads=reads, writes=writes)

        def cos_sin(theta, Bth, W):
            s, Bs = tmp([128, W]); c, Bc = tmp([128, W]); t1, Bt1 = tmp([128, W]); t2, Bt2 = tmp([128, W])
            ew("act", lambda e: e.activation(out=s[:, :], in_=theta[:, :], func=AF.Sin, scale=1.0 / 16), [Bth], [Bs])
            ew("act", lambda e: e.activation(out=c[:, :], in_=theta[:, :], func=AF.Sin, scale=1.0 / 16, bias=halfpi[:, 0:1]), [Bth, Bhp], [Bc])
            for _ in range(4):
                ew("dve", lambda e: e.tensor_tensor(out=t1[:, :], in0=c[:, :], in1=c[:, :], op=ALU.mult), [Bc], [Bt1])
                ew("dve", lambda e: e.tensor_tensor(out=t2[:, :], in0=s[:, :], in1=s[:, :], op=ALU.mult), [Bs], [Bt2])
                ew("dve", lambda e: e.scalar_tensor_tensor(out=s[:, :], in0=s[:, :], scalar=2.0, in1=c[:, :], op0=ALU.mult, op1=ALU.mult),
                   [Bs, Bc], [Bs])
                ew("dve", lambda e: e.tensor_tensor(out=c[:, :], in0=t1[:, :], in1=t2[:, :], op=ALU.subtract), [Bt1, Bt2], [Bc])
            return (c, Bc), (s, Bs)

        WBre = []; WBim = []; CTre = []; CTimN = []; Ec = []; Es = []; rcol = []
        for d in range(2):
            pr_, Bpr_ = tmp([128, 3, 128])
            P.dma("sp", lambda e, d=d, pr_=pr_: e.dma_start(out=pr_[:, :, :], in_=rowp[d].rearrange("k p n -> p k n")), Bpr_)
            bt, Bbt = tmp([128, 2, 128])
            P.dma("sp", lambda e, d=d, bt=bt: e.dma_start(out=bt[:, :, :], in_=BT[d].rearrange("k p n -> p k n")), Bbt)
            dt_, Bdt = tmp([128, 128]); ard, Bard = tmp([128, 128]); th, Bth = tmp([128, 128]); mag, Bmag = tmp([128, 128])
            ew("act", lambda e, dt_=dt_, pr_=pr_: e.activation(out=dt_[:, :], in_=pr_[:, 2, :], func=AF.Exp), [Bpr_], [Bdt])
            ew("dve", lambda e, ard=ard, pr_=pr_, dt_=dt_: e.tensor_tensor(out=ard[:, :], in0=pr_[:, 0, :], in1=dt_[:, :], op=ALU.mult), [Bpr_, Bdt], [Bard])
            ew("dve", lambda e, th=th, pr_=pr_, dt_=dt_: e.tensor_tensor(out=th[:, :], in0=pr_[:, 1, :], in1=dt_[:, :], op=ALU.mult), [Bpr_, Bdt], [Bth])
            ew("act", lambda e, mag=mag, ard=ard: e.activation(out=mag[:, :], in_=ard[:, :], func=AF.Exp), [Bard], [Bmag])
            (c, Bc), (s, Bs) = cos_sin(th, Bth, 128)
            lr, Blr = tmp([128, 128]); li, Bli = tmp([128, 128]); den, Bden = tmp([128, 128]); t1, Bt1 = tmp([128, 128]); t2, Bt2 = tmp([128, 128])
            fre, Bfre = tmp([128, 128]); fim, Bfim = tmp([128, 128])
            ew("dve", lambda e, lr=lr, mag=mag, c=c: e.tensor_tensor(out=lr[:, :], in0=mag[:, :], in1=c[:, :], op=ALU.mult), [Bmag, Bc], [Blr])
            ew("dve", lambda e, li=li, mag=mag, s=s: e.tensor_tensor(out=li[:, :], in0=mag[:, :], in1=s[:, :], op=ALU.mult), [Bmag, Bs], [Bli])
            ew("dve", lambda e, lr=lr: e.tensor_scalar(out=lr[:, :], in0=lr[:, :], scalar1=-1.0, scalar2=None, op0=ALU.add), [Blr], [Blr])
            ew("dve", lambda e, t1=t1, pr_=pr_: e.tensor_tensor(out=t1[:, :], in0=pr_[:, 0, :], in1=pr_[:, 0, :], op=ALU.mult), [Bpr_], [Bt1])
            ew("dve", lambda e, t2=t2, pr_=pr_: e.tensor_tensor(out=t2[:, :], in0=pr_[:, 1, :], in1=pr_[:, 1, :], op=ALU.mult), [Bpr_], [Bt2])
            ew("dve", lambda e, den=den, t1=t1, t2=t2: e.tensor_tensor(out=den[:, :], in0=t1[:, :], in1=t2[:, :], op=ALU.add), [Bt1, Bt2], [Bden])
            ew("dve", lambda e, den=den: e.reciprocal(out=den[:, :], in_=den[:, :]), [Bden], [Bden])
            ew("dve", lambda e, t1=t1, lr=lr, pr_=pr_: e.tensor_tensor(out=t1[:, :], in0=lr[:, :], in1=pr_[:, 0, :], op=ALU.mult), [Blr, Bpr_], [Bt1])
            ew("dve", lambda e, t2=t2, li=li, pr_=pr_: e.tensor_tensor(out=t2[:, :], in0=li[:, :], in1=pr_[:, 1, :], op=ALU.mult), [Bli, Bpr_], [Bt2])
            ew("dve", lambda e, t1=t1, t2=t2: e.tensor_tensor(out=t1[:, :], in0=t1[:, :], in1=t2[:, :], op=ALU.add), [Bt1, Bt2], [Bt1])
            ew("dve", lambda e, fre=fre, t1=t1, den=den: e.tensor_tensor(out=fre[:, :], in0=t1[:, :], in1=den[:, :], op=ALU.mult), [Bt1, Bden], [Bfre])
            ew("dve", lambda e, t1=t1, li=li, pr_=pr_: e.tensor_tensor(out=t1[:, :], in0=li[:, :], in1=pr_[:, 0, :], op=ALU.mult), [Bli, Bpr_], [Bt1])
            ew("dve", lambda e, t2=t2, lr=lr, pr_=pr_: e.tensor_tensor(out=t2[:, :], in0=lr[:, :], in1=pr_[:, 1, :], op=ALU.mult), [Blr, Bpr_], [Bt2])
            ew("dve", lambda e, t1=t1, t2=t2: e.tensor_tensor(out=t1[:, :], in0=t1[:, :], in1=t2[:, :], op=ALU.subtract), [Bt1, Bt2], [Bt1])
            ew("dve", lambda e, fim=fim, t1=t1, den=den: e.tensor_tensor(out=fim[:, :], in0=t1[:, :], in1=den[:, :], op=ALU.mult), [Bt1, Bden], [Bfim])
            wre, Bwre = tmp([128, 128], BF16); wim, Bwim = tmp([128, 128], BF16)
            ew("dve", lambda e, t1=t1, fre=fre, bt=bt: e.tensor_tensor(out=t1[:, :], in0=fre[:, :], in1=bt[:, 0, :], op=ALU.mult), [Bfre, Bbt], [Bt1])
            ew("dve", lambda e, t2=t2, fim=fim, bt=bt: e.tensor_tensor(out=t2[:, :], in0=fim[:, :], in1=bt[:, 1, :], op=ALU.mult), [Bfim, Bbt], [Bt2])
            ew("dve", lambda e, wre=wre, t1=t1, t2=t2: e.tensor_tensor(out=wre[:, :], in0=t1[:, :], in1=t2[:, :], op=ALU.subtract), [Bt1, Bt2], [Bwre])
            ew("dve", lambda e, t1=t1, fre=fre, bt=bt: e.tensor_tensor(out=t1[:, :], in0=fre[:, :], in1=bt[:, 1, :], op=ALU.mult), [Bfre, Bbt], [Bt1])
            ew("dve", lambda e, t2=t2, fim=fim, bt=bt: e.tensor_tensor(out=t2[:, :], in0=fim[:, :], in1=bt[:, 0, :], op=ALU.mult), [Bfim, Bbt], [Bt2])
            ew("dve", lambda e, wim=wim, t1=t1, t2=t2: e.tensor_tensor(out=wim[:, :], in0=t1[:, :], in1=t2[:, :], op=ALU.add), [Bt1, Bt2], [Bwim])
            WBre.append((wre, Bwre)); WBim.append((wim, Bwim))
            ctf, Bctf = tmp([128, 2, 4, 128])
            P.dma("sp", lambda e, d=d, ctf=ctf: e.dma_start(out=ctf[:, :, :, :], in_=CT[d].rearrange("k p q n -> p k q n")), Bctf)
            cre, Bcre = tmp([128, 4, 128], BF16); cim, Bcim = tmp([128, 4, 128], BF16)
            ew("dve", lambda e, cre=cre, ctf=ctf: e.tensor_copy(out=cre[:, :, :], in_=ctf[:, 0, :, :]), [Bctf], [Bcre])
            ew("dve", lambda e, cim=cim, ctf=ctf: e.tensor_scalar(out=cim[:, :, :], in0=ctf[:, 1, :, :], scalar1=-1.0, scalar2=None, op0=ALU.mult), [Bctf], [Bcim])
            CTre.append((cre, Bcre)); CTimN.append((cim, Bcim))
            pc, Bpc = tmp([128, 3, 4])
            P.dma("sp", lambda e, d=d, pc=pc: e.dma_start(out=pc[:, :, :], in_=colp[d].rearrange("k p q -> p k q")), Bpc)
            dtc, Bdtc = tmp([128, 4]); a2, Ba2 = tmp([128, 4]); thc, Bthc = tmp([128, 4]); r_, Br_ = tmp([128, 4])
            ew("act", lambda e, dtc=dtc, pc=pc: e.activation(out=dtc[:, :], in_=pc[:, 2, :], func=AF.Exp), [Bpc], [Bdtc])
            ew("dve", lambda e, a2=a2, pc=pc, dtc=dtc: e.tensor_tensor(out=a2[:, :], in0=pc[:, 0, :], in1=dtc[:, :], op=ALU.mult), [Bpc, Bdtc], [Ba2])
            ew("dve", lambda e, thc=thc, pc=pc, dtc=dtc: e.tensor_tensor(out=thc[:, :], in0=pc[:, 1, :], in1=dtc[:, :], op=ALU.mult), [Bpc, Bdtc], [Bthc])
            ew("act", lambda e, r_=r_, a2=a2: e.activation(out=r_[:, :], in_=a2[:, :], func=AF.Exp), [Ba2], [Br_])
            rcol.append((r_, Br_))
            (cc, Bcc), (sc, Bsc) = cos_sin(thc, Bthc, 4)
            ec, Bec = tmp([128, 4, TT]); es, Bes = tmp([128, 4, TT]); tt, Btt = tmp([128, TT])
            for q in range(4):
                ew("dve", lambda e, ec=ec, cc=cc, q=q: e.tensor_copy(out=ec[:, q, 0:1], in_=cc[:, q:q + 1]), [Bcc], [Bec])
                ew("dve", lambda e, es=es, sc=sc, q=q: e.tensor_scalar(out=es[:, q, 0:1], in0=sc[:, q:q + 1], scalar1=-1.0, scalar2=None, op0=ALU.mult), [Bsc], [Bes])
                m = 1
                while m < TT:
                    ew("dve", lambda e, q=q, m=m, tt=tt, es=es, ec=ec: e.tensor_scalar(out=tt[:, 0:m], in0=es[:, q, 0:m], scalar1=es[:, q, m - 1:m], scalar2=None, op0=ALU.mult), [Bes], [Btt])
                    ew("dve", lambda e, q=q, m=m, tt=tt, es=es, ec=ec: e.scalar_tensor_tensor(out=ec[:, q, m:2 * m], in0=ec[:, q, 0:m], scalar=ec[:, q, m - 1:m], in1=tt[:, 0:m],
                                                                         op0=ALU.mult, op1=ALU.subtract), [Bec, Btt], [Bec])
                    ew("dve", lambda e, q=q, m=m, tt=tt, es=es, ec=ec: e.tensor_scalar(out=tt[:, 0:m], in0=es[:, q, 0:m], scalar1=ec[:, q, m - 1:m], scalar2=None, op0=ALU.mult), [Bes, Bec], [Btt])
                    ew("dve", lambda e, q=q, m=m, tt=tt, es=es, ec=ec: e.scalar_tensor_tensor(out=es[:, q, m:2 * m], in0=ec[:, q, 0:m], scalar=es[:, q, m - 1:m], in1=tt[:, 0:m],
                                                                         op0=ALU.mult, op1=ALU.add), [Bec, Bes, Btt], [Bes])
                    m *= 2
            Ec.append((ec, Bec)); Es.append((es, Bes))
        yf, Byf = cx.sb([128, NKEY], F32)
        u_ring = cx.ring(2, [128, TT], F32); ub_ring = cx.ring(2, [128, TT], BF16)
        ps_r = cx.ring(2, [128, 512], F32, psum=True); ps_i = cx.ring(2, [128, 512], F32, psum=True)
        ps_y = cx.ring(2, [128, 512], F32, psum=True)
        tr = {k: cx.ring(2, [128, TT], F32) for k in ("m1", "m2", "m3", "m4", "gpr", "gpi", "gr", "gi")}
        hb_r = cx.ring(2, [128, TT], BF16); hb_i = cx.ring(2, [128, TT], BF16)
        ostg = cx.ring(2, [128, TT], F32)
        tiles = [(0, 256)] + [(256 + 512 * i, 512) for i in range(16)]
        for b in range(2):
            for d in range(2):
                car_r, Bcr = cx.sb([128, 4], F32); car_i, Bci = cx.sb([128, 4], F32)
                P.op("pool", lambda e, car_r=car_r: e.memset(car_r[:, :], 0.0), writes=[Bcr])
                P.op("pool", lambda e, car_i=car_i: e.memset(car_i[:, :], 0.0), writes=[Bci])
                order = tiles if d == 0 else [tiles[0]] + tiles[:0:-1]
                (wre, Bwre), (wim, Bwim) = WBre[d], WBim[d]
                (cre, Bcre), (cim, Bcim) = CTre[d], CTimN[d]
                (ec, Bec), (es, Bes) = Ec[d], Es[d]
                (r_, Br_) = rcol[d]
                for (t0, n) in order:
                    uf, Buf_ = u_ring.next()
                    P.dma("sp", lambda e, uf=uf, b=b, t0=t0, n=n: e.dma_start(out=uf[:, 0:n], in_=uT[:, b, t0:t0 + n]), Buf_)
                    ub, Bub = ub_ring.next()
                    if d == 0:
                        P.op("act", lambda e, ub=ub, uf=uf, n=n: e.activation(out=ub[:, 0:n], in_=uf[:, 0:n], func=AF.Copy), reads=[Buf_], writes=[Bub])
                    else:
                        P.op("dve", lambda e, ub=ub, uf=uf, n=n: e.tensor_copy(out=ub[:, 0:n], in_=uf[:, n - 1::-1] if False else uf[:, 0:n][:, ::-1]),
                             reads=[Buf_], writes=[Bub])
                    py, Bpy = ps_y.next()
                    for q in range(4):
                        pr, Bpr = ps_r.next(); pi, Bpi = ps_i.next()
                        P.op("pe", lambda e, pr=pr, q=q, ub=ub, n=n, wre=wre: e.matmul(pr[:, 0:n], lhsT=wre[32 * q:32 * q + 32, :], rhs=ub[32 * q:32 * q + 32, 0:n],
                                                                                   start=True, stop=True, tile_position=(32 * q, 0)), reads=[Bwre, Bub], writes=[Bpr])
                        P.op("pe", lambda e, pi=pi, q=q, ub=ub, n=n, wim=wim: e.matmul(pi[:, 0:n], lhsT=wim[32 * q:32 * q + 32, :], rhs=ub[32 * q:32 * q + 32, 0:n],
                                                                                   start=True, stop=True, tile_position=(32 * q, 0)), reads=[Bwim, Bub], writes=[Bpi])
                        T = {k: tr[k].next() for k in tr}
                        def tt_(eng, o, a, bb, op, q=q, n=n):
                            (ot, Bo), (at, Ba), (bt_, Bb) = o, a, bb
                            P.op(eng, lambda e: e.tensor_tensor(out=ot[:, 0:n], in0=at[:, 0:n], in1=bt_, op=op), reads=[Ba, Bb], writes=[Bo])
                        ecq = (ec[:, q, 0:n], Bec); esq = (es[:, q, 0:n], Bes)
                        def tt2(eng, o, a, tab, op, n=n):
                            (ot, Bo), (at, Ba), (tb, Bt) = o, a, tab
                            P.op(eng, lambda e: e.tensor_tensor(out=ot[:, 0:n], in0=at[:, 0:n], in1=tb, op=op), reads=[Ba, Bt], writes=[Bo])
                        def tt3(eng, o, a, bsrc, op, n=n):
                            (ot, Bo), (at, Ba), (bt2, Bb2) = o, a, bsrc
                            P.op(eng, lambda e: e.tensor_tensor(out=ot[:, 0:n], in0=at[:, 0:n], in1=bt2[:, 0:n], op=op), reads=[Ba, Bb2], writes=[Bo])
                        tt2("dve", T["m1"], (pr, Bpr), ecq, ALU.mult)
                        tt2("dve", T["m2"], (pi, Bpi), esq, ALU.mult)
                        tt3("pool", T["gpr"], T["m1"], T["m2"], ALU.subtract)
                        tt2("dve", T["m3"], (pi, Bpi), ecq, ALU.mult)
                        tt2("dve", T["m4"], (pr, Bpr), esq, ALU.mult)
                        tt3("pool", T["gpi"], T["m3"], T["m4"], ALU.add)
                        for (go, gp, car, Bcar) in ((T["gr"], T["gpr"], car_r, Bcr), (T["gi"], T["gpi"], car_i, Bci)):
                            (got, Bgo), (gpt, Bgp) = go, gp
                            P.op("dve", lambda e, got=got, gpt=gpt, car=car, q=q, n=n, r_=r_: e.tensor_tensor_scan(
                                out=got[:, 0:n], data0=r_[:, q:q + 1].to_broadcast([128, n]), data1=gpt[:, 0:n], initial=car[:, q:q + 1],
                                op0=ALU.mult, op1=ALU.add), reads=[Br_, Bgp, Bcar], writes=[Bgo])
                        tt2("pool", T["m1"], T["gr"], ecq, ALU.mult)
                        tt2("pool", T["m2"], T["gi"], esq, ALU.mult)
                        tt2("pool", T["m3"], T["gi"], ecq, ALU.mult)
                        tt2("pool", T["m4"], T["gr"], esq, ALU.mult)
                        hr, Bhr = hb_r.next(); hi, Bhi = hb_i.next()
                        tt3("dve", (hr, Bhr), T["m1"], T["m2"], ALU.add)
                        tt3("dve", (hi, Bhi), T["m3"], T["m4"], ALU.subtract)
                        (m1, Bm1), (m2, Bm2), (m3, Bm3), (m4, Bm4) = T["m1"], T["m2"], T["m3"], T["m4"]
                        P.op("pool", lambda e, car_r=car_r, m1=m1, m2=m2, q=q, n=n: e.tensor_tensor(out=car_r[:, q:q + 1], in0=m1[:, n - 1:n], in1=m2[:, n - 1:n], op=ALU.add),
                             reads=[Bm1, Bm2], writes=[Bcr])
                        P.op("pool", lambda e, car_i=car_i, m3=m3, m4=m4, q=q, n=n: e.tensor_tensor(out=car_i[:, q:q + 1], in0=m3[:, n - 1:n], in1=m4[:, n - 1:n], op=ALU.subtract),
                             reads=[Bm3, Bm4], writes=[Bci])
                        P.op("pe", lambda e, py=py, cre=cre, hr=hr, q=q, n=n: e.matmul(py[:, 0:n], lhsT=cre[:, q, :], rhs=hr[:, 0:n], start=(q == 0), stop=False),
                             reads=[Bcre, Bhr], writes=[Bpy])
                        P.op("pe", lambda e, py=py, cim=cim, hi=hi, q=q, n=n: e.matmul(py[:, 0:n], lhsT=cim[:, q, :], rhs=hi[:, 0:n], start=False, stop=(q == 3)),
                             reads=[Bcim, Bhi], writes=[Bpy])
                    if d == 0:
                        P.op("act", lambda e, py=py, t0=t0, n=n: e.activation(out=yf[:, t0:t0 + n], in_=py[:, 0:n], func=AF.Copy), reads=[Bpy], writes=[Byf])
                    else:
                        o1, Bo1 = ostg.next()
                        P.op("dve", lambda e, o1=o1, uf=uf, t0=t0, n=n: e.scalar_tensor_tensor(out=o1[:, 0:n], in0=uf[:, 0:n], scalar=dv[:, 0:1], in1=yf[:, t0:t0 + n],
                                                                                            op0=ALU.mult, op1=ALU.add), reads=[Buf_, Bdv, Byf], writes=[Bo1])
                        P.op("dve", lambda e, o1=o1, py=py, n=n: e.tensor_tensor(out=o1[:, 0:n], in0=o1[:, 0:n], in1=py[:, 0:n][:, ::-1], op=ALU.add),
                             reads=[Bo1, Bpy], writes=[Bo1])
                        P.dma("sp", lambda e, o1=o1, b=b, t0=t0, n=n: e.dma_start(out=o_y[:, b, t0:t0 + n], in_=o1[:, 0:n]), Boy, reads=[Bo1])
        P.wait_final("sp", [Boy])
        P.emit()
    return nc, P


CONV_PIECES = [(i * SEG, SEG) for i in range(NSEG - 1)] + [((NSEG - 1) * SEG, NLAT - (NSEG - 1) * SEG), (NLAT, NCTX)]
CONV_WOFF = []
_o = 0
for (_s, _n) in CONV_PIECES:
    CONV_WOFF.append(_o); _o += _n + 30
WTOT = _o


def build_stage_c(final=False, debug_seg=None):
    nc = bass.Bass("TRN2", target_bir_lowering=False)
    cx = Ctx(nc); P = cx.P
    xT = cx.din("xT", [KC, 128, NT])
    attT = cx.din("attT", [8, 128, NT], BF16)
    ysT = cx.din("ysT", [8, 128, NT])
    ucH = cx.din("ucH", [8, 128, WTOT])
    gT = cx.din("gT", [48, 128, NT], BF16)
    w_glu = cx.din("w_glu", [1024, 1024]); w_ssm_o = cx.din("w_ssm_o", [1024, D]); w_conv_o = cx.din("w_conv_o", [1024, D])
    w_attn_o = cx.din("w_attn_o", [1024, D]); w_out = cx.din("w_out", [D, D]); w1 = cx.din("w_mlp1", [D, 4 * D]); w2 = cx.din("w_mlp2", [4 * D, D])
    vec8 = cx.din("vec8", [128, 8, 4])
    convw = cx.din("convw", [128, 8, 31])
    g2T = cx.din("g2T", [128, KC])
    fgT = cx.din("fgT", [128, KC])
    mod2 = cx.din("mod2", [128, KC, 2, 4])
    o_x, Box = cx.dout("xT_out", [KC, 128, NT])
    if debug_seg is not None:
        dbg = {n: cx.dout("dbg_" + n, [128, k, SEG], dt) for n, k, dt in (("cacc", 8, F32), ("cv", 8, BF16), ("gs", 8, BF16), ("hT", 16, BF16), ("g32", 8, F32))}
    wv = lambda w: w.rearrange("(kc p) n -> p kc n", p=128)
    with cx.st:
        ones_f, Bones = cx.sb([128, 128], F32)
        P.op("pool", lambda e: e.memset(ones_f[:, :], 1.0), writes=[Bones])
        eps_t, Beps = cx.sb([128, 1], F32)
        P.op("pool", lambda e: e.memset(eps_t[:, :], EPS), writes=[Beps])
        v8, Bv8 = cx.sb([128, 8, 4], F32); cw, Bcw = cx.sb([128, 8, 31], F32)
        g2, Bg2 = cx.sb([128, KC], F32); fg, Bfg = cx.sb([128, KC], F32); m2, Bm2 = cx.sb([128, KC, 2, 4], F32)
        sc2, Bsc2 = cx.sb([128, KC, 2, 2], F32)
        for (t, B_, src) in ((v8, Bv8, vec8), (cw, Bcw, convw), (g2, Bg2, g2T), (fg, Bfg, fgT), (m2, Bm2, mod2)):
            P.dma("sp", lambda e, t=t, src=src: e.dma_start(out=t[:], in_=src), B_)
        for var in range(2):
            P.op("dve", lambda e, var=var: e.scalar_tensor_tensor(out=sc2[:, :, var, 0], in0=m2[:, :, var, 2], scalar=1.0, in1=g2[:, :],
                                                                  op0=ALU.add, op1=ALU.mult), reads=[Bm2, Bg2], writes=[Bsc2])
            P.op("dve", lambda e, var=var: e.tensor_copy(out=sc2[:, :, var, 1], in_=m2[:, :, var, 1]), reads=[Bm2], writes=[Bsc2])
        if final:
            scf, Bscf = cx.sb([128, KC, 2, 2], F32)
            P.op("pool", lambda e: e.memset(scf[:, :, :, :], 0.0), writes=[Bscf])
            for var in range(2):
                P.op("dve", lambda e, var=var: e.tensor_copy(out=scf[:, :, var, 0], in_=fg[:, :]), reads=[Bfg, Bscf], writes=[Bscf])
        xs, Bxs = cx.sb([128, KC, SEG], F32)
        hT, BhT = cx.sb([128, KC, SEG], BF16)
        hh, Bhh = cx.sb([128, 64, SEG], BF16)
        ys, Bys = cx.sb([128, 8, SEG], F32)
        gb, Bgb = cx.sb([128, 8, SEG], BF16); gs, Bgs = cx.sb([128, 8, SEG], BF16)
        cv, Bcv = cx.sb([128, 8, SEG], BF16); at, Bat = cx.sb([128, 8, SEG], BF16)
        cacc, Bcacc = cx.sb([128, 8, SEG], F32)
        ucw_ring = cx.ring(2, [128, SEG + 30], F32)
        gate_ring = cx.ring(6, [128, SEG], BF16)
        w_ring = cx.ring(4, [128, 16, 256], BF16)
        tmp_ring = cx.ring(4, [128, SEG], F32)
        rstd_ring = cx.ring(2, [128, SEG], F32)
        ps_mm = cx.ring(6, [128, 512], F32, psum=True)
        ps_aux = cx.ring(2, [128, 512], F32, psum=True)
        ostg = cx.ring(2, [128, SEG], F32) if final else None

        def mm_group(W, nK, n_oc, rhs_of, Brhs, epilogue):
            for oc2 in range(0, n_oc, 2):
                pss = [ps_mm.next(), ps_mm.next()]
                for k0 in range(0, nK, 16):
                    nk = min(16, nK - k0)
                    wt, Bw = w_ring.next()
                    P.dma("pool", lambda e, wt=wt, W=W, k0=k0, nk=nk, oc2=oc2: e.dma_start(
                        out=wt[:, 0:nk, :], in_=wv(W)[:, k0:k0 + nk, oc2 * 128:oc2 * 128 + 256]), Bw)
                    for j in range(2):
                        pt, Bp = pss[j]
                        for kc in range(nk):
                            P.op("pe", lambda e, pt=pt, wt=wt, j=j, kc=kc, k0=k0: e.matmul(
                                pt[:, 0:SEG], lhsT=wt[:, kc, j * 128:(j + 1) * 128], rhs=rhs_of(k0 + kc),
                                start=(k0 + kc == 0), stop=(k0 + kc == nK - 1)), reads=[Bw, Brhs], writes=[Bp])
                for j in range(2):
                    epilogue(oc2 + j, pss[j][0], pss[j][1])

        for seg in (range(NSEG) if debug_seg is None else [debug_seg]):
            t0 = seg * SEG
            parts = [(0, NLAT - t0, 0), (NLAT - t0, SEG, 1)] if seg == NSEG - 1 else [(0, SEG, 0)]
            P.dma("sp", lambda e, t0=t0: e.dma_start(out=xs[:, :, :], in_=xT[:, :, t0:t0 + SEG].rearrange("k p t -> p k t")), Bxs)
            P.dma("sp", lambda e, t0=t0: e.dma_start(out=ys[:, :, :], in_=ysT[:, :, t0:t0 + SEG].rearrange("k p t -> p k t")), Bys)
            P.dma("sp", lambda e, t0=t0: e.dma_start(out=at[:, :, :], in_=attT[:, :, t0:t0 + SEG].rearrange("k p t -> p k t")), Bat)
            for c in range(8):
                t1, Bt1 = tmp_ring.next()
                P.op("pool", lambda e, t1=t1, c=c: e.tensor_tensor(out=t1[:, :], in0=ys[:, c, :], in1=ys[:, c, :], op=ALU.mult), reads=[Bys], writes=[Bt1])
                P.op("pool", lambda e, t1=t1: e.tensor_scalar(out=t1[:, :], in0=t1[:, :], scalar1=0.044715, scalar2=1.0, op0=ALU.mult, op1=ALU.add),
                     reads=[Bt1], writes=[Bt1])
                P.op("pool", lambda e, t1=t1, c=c: e.tensor_tensor(out=t1[:, :], in0=t1[:, :], in1=ys[:, c, :], op=ALU.mult), reads=[Bt1, Bys], writes=[Bt1])
                P.op("act", lambda e, t1=t1: e.activation(out=t1[:, :], in_=t1[:, :], func=AF.Sigmoid, scale=1.5957691216), reads=[Bt1], writes=[Bt1])
                P.op("dve", lambda e, t1=t1, c=c: e.tensor_tensor(out=ys[:, c, :], in0=ys[:, c, :], in1=t1[:, :], op=ALU.mult), reads=[Bys, Bt1], writes=[Bys])
                P.op("act", lambda e, c=c: e.activation(out=gb[:, c, :], in_=ys[:, c, :], func=AF.Copy), reads=[Bys], writes=[Bgb])
            def ep_glu(oc, pt, Bp):
                t1, Bt1 = tmp_ring.next()
                P.op("act", lambda e: e.activation(out=t1[:, :], in_=pt[:, 0:SEG], func=AF.Sigmoid, bias=v8[:, oc, 0:1]), reads=[Bp, Bv8], writes=[Bt1])
                P.op("dve", lambda e: e.tensor_tensor(out=gs[:, oc, :], in0=ys[:, oc, :], in1=t1[:, :], op=ALU.mult), reads=[Bys, Bt1], writes=[Bgs])
            mm_group(w_glu, 8, 8, lambda k: gb[:, k, :], Bgb, ep_glu)
            for c in range(8):
                for pi_, (ps0, pn) in enumerate(CONV_PIECES):
                    if not (t0 <= ps0 < t0 + SEG):
                        continue
                    o0 = ps0 - t0
                    win, Bwin = ucw_ring.next()
                    P.dma("sp", lambda e, win=win, c=c, pi_=pi_, pn=pn: e.dma_start(out=win[:, 0:pn + 30], in_=ucH[c, :, CONV_WOFF[pi_]:CONV_WOFF[pi_] + pn + 30]), Bwin)
                    a2, Ba2 = tmp_ring.next(); tm, Btm = tmp_ring.next()
                    P.op("dve", lambda e, win=win, c=c, o0=o0, pn=pn: e.tensor_scalar(out=cacc[:, c, o0:o0 + pn], in0=win[:, 0:pn], scalar1=cw[:, c, 0:1],
                                                                                     scalar2=v8[:, c, 1:2], op0=ALU.mult, op1=ALU.add), reads=[Bwin, Bcw, Bv8], writes=[Bcacc])
                    for k in range(1, 16):
                        P.op("dve", lambda e, win=win, c=c, o0=o0, pn=pn, k=k: e.scalar_tensor_tensor(
                            out=cacc[:, c, o0:o0 + pn], in0=win[:, k:k + pn], scalar=cw[:, c, k:k + 1], in1=cacc[:, c, o0:o0 + pn],
                            op0=ALU.mult, op1=ALU.add), reads=[Bwin, Bcw, Bcacc], writes=[Bcacc])
                    P.op("pool", lambda e, win=win, a2=a2, c=c, pn=pn: e.tensor_scalar(out=a2[:, 0:pn], in0=win[:, 16:16 + pn], scalar1=cw[:, c, 16:17], scalar2=None,
                                                                                      op0=ALU.mult), reads=[Bwin, Bcw], writes=[Ba2])
                    for k in range(17, 31):
                        P.op("pool", lambda e, win=win, tm=tm, c=c, pn=pn, k=k: e.tensor_scalar(out=tm[:, 0:pn], in0=win[:, k:k + pn], scalar1=cw[:, c, k:k + 1], scalar2=None,
                                                                                              op0=ALU.mult), reads=[Bwin, Bcw], writes=[Btm])
                        P.op("pool", lambda e, a2=a2, tm=tm, pn=pn: e.tensor_tensor(out=a2[:, 0:pn], in0=a2[:, 0:pn], in1=tm[:, 0:pn], op=ALU.add), reads=[Ba2, Btm], writes=[Ba2])
                    P.op("pool", lambda e, a2=a2, c=c, o0=o0, pn=pn: e.tensor_tensor(out=cacc[:, c, o0:o0 + pn], in0=cacc[:, c, o0:o0 + pn], in1=a2[:, 0:pn], op=ALU.add),
                         reads=[Bcacc, Ba2], writes=[Bcacc])
            p1, Bp1 = ps_aux.next(); p2, Bp2 = ps_aux.next()
            for c in range(8):
                P.op("pe", lambda e, c=c, p1=p1: e.matmul(p1[:, 0:SEG], lhsT=ones_f[:, :], rhs=cacc[:, c, :], start=(c == 0), stop=(c == 7)), reads=[Bones, Bcacc], writes=[Bp1])
            for c in range(8):
                sq, Bsq = tmp_ring.next()
                P.op("act", lambda e, sq=sq, c=c: e.activation(out=sq[:, :], in_=cacc[:, c, :], func=AF.Square), reads=[Bcacc], writes=[Bsq])
                P.op("pe", lambda e, sq=sq, c=c, p2=p2: e.matmul(p2[:, 0:SEG], lhsT=ones_f[:, :], rhs=sq[:, :], start=(c == 0), stop=(c == 7)), reads=[Bones, Bsq], writes=[Bp2])
            mean, Bmean = rstd_ring.next(); rstd, Brstd = rstd_ring.next(); msq, Bmsq = tmp_ring.next()
            P.op("dve", lambda e, mean=mean, p1=p1: e.tensor_scalar(out=mean[:, :], in0=p1[:, 0:SEG], scalar1=1.0 / 1024, scalar2=None, op0=ALU.mult), reads=[Bp1], writes=[Bmean])
            P.op("pool", lambda e, msq=msq, mean=mean: e.tensor_tensor(out=msq[:, :], in0=mean[:, :], in1=mean[:, :], op=ALU.mult), reads=[Bmean], writes=[Bmsq])
            P.op("dve", lambda e, rstd=rstd, p2=p2, msq=msq: e.scalar_tensor_tensor(out=rstd[:, :], in0=p2[:, 0:SEG], scalar=1.0 / 1024, in1=msq[:, :], op0=ALU.mult, op1=ALU.subtract),
                 reads=[Bp2, Bmsq], writes=[Brstd])
            P.op("act", lambda e, rstd=rstd: e.activation(out=rstd[:, :], in_=rstd[:, :], func=AF.Sqrt, bias=eps_t[:, 0:1]), reads=[Brstd, Beps], writes=[Brstd])
            P.op("dve", lambda e, rstd=rstd: e.reciprocal(out=rstd[:, :], in_=rstd[:, :]), reads=[Brstd], writes=[Brstd])
            for c in range(8):
                t1, Bt1 = tmp_ring.next()
                P.op("pool", lambda e, t1=t1, c=c, mean=mean: e.tensor_tensor(out=t1[:, :], in0=cacc[:, c, :], in1=mean[:, :], op=ALU.subtract), reads=[Bcacc, Bmean], writes=[Bt1])
                P.op("pool", lambda e, t1=t1, rstd=rstd: e.tensor_tensor(out=t1[:, :], in0=t1[:, :], in1=rstd[:, :], op=ALU.mult), reads=[Bt1, Brstd], writes=[Bt1])
                P.op("pool", lambda e, t1=t1, c=c: e.tensor_scalar(out=t1[:, :], in0=t1[:, :], scalar1=v8[:, c, 2:3], scalar2=v8[:, c, 3:4], op0=ALU.mult, op1=ALU.add),
                     reads=[Bt1, Bv8], writes=[Bt1])
                P.op("act", lambda e, t1=t1, c=c: e.activation(out=cv[:, c, :], in_=t1[:, :], func=AF.Silu), reads=[Bt1], writes=[Bcv])
            branches = [(w_attn_o, at, Bat, 0), (w_ssm_o, gs, Bgs, 16), (w_conv_o, cv, Bcv, 32)]
            for oc2 in range(0, 16, 2):
                wts = []
                for (W, src, Bsrc, goff) in branches:
                    wt, Bw = w_ring.next()
                    P.dma("pool", lambda e, wt=wt, W=W, oc2=oc2: e.dma_start(out=wt[:, 0:8, :], in_=wv(W)[:, 0:8, oc2 * 128:oc2 * 128 + 256]), Bw)
                    wts.append((wt, Bw))
                for j in range(2):
                    oc = oc2 + j
                    acc = None
                    for bi, (W, src, Bsrc, goff) in enumerate(branches):
                        wt, Bw = wts[bi]
                        pt, Bp = ps_mm.next()
                        for kc in range(8):
                            P.op("pe", lambda e, pt=pt, wt=wt, j=j, kc=kc, src=src: e.matmul(pt[:, 0:SEG], lhsT=wt[:, kc, j * 128:(j + 1) * 128], rhs=src[:, kc, :],
                                                                                            start=(kc == 0), stop=(kc == 7)), reads=[Bw, Bsrc], writes=[Bp])
                        gt_, Bgt = gate_ring.next()
                        P.dma("sp", lambda e, gt_=gt_, goff=goff, oc=oc, t0=t0: e.dma_start(out=gt_[:, :], in_=gT[goff + oc, :, t0:t0 + SEG]), Bgt)
                        t1, Bt1 = tmp_ring.next()
                        P.op("dve", lambda e, t1=t1, pt=pt, gt_=gt_: e.tensor_tensor(out=t1[:, :], in0=pt[:, 0:SEG], in1=gt_[:, :], op=ALU.mult), reads=[Bp, Bgt], writes=[Bt1])
                        if acc is None:
                            acc = (t1, Bt1)
                        elif bi == 1:
                            a_, Ba_ = acc
                            P.op("pool", lambda e, a_=a_, t1=t1: e.tensor_tensor(out=a_[:, :], in0=a_[:, :], in1=t1[:, :], op=ALU.add), reads=[Ba_, Bt1], writes=[Ba_])
                        else:
                            a_, Ba_ = acc
                            P.op("pool", lambda e, a_=a_, t1=t1, oc=oc: e.tensor_tensor(out=hT[:, oc, :], in0=a_[:, :], in1=t1[:, :], op=ALU.add), reads=[Ba_, Bt1], writes=[BhT])
            if debug_seg is not None:
                for n, (t_, B_) in (("cacc", (cacc, Bcacc)), ("cv", (cv, Bcv)), ("gs", (gs, Bgs)), ("hT", (hT, BhT)), ("g32", (ys, Bys))):
                    P.dma("sp", lambda e, n=n, t_=t_: e.dma_start(out=dbg[n][0][:, :, :], in_=t_[:, :, :]), dbg[n][1], reads=[B_])
            def ep_res(gidx):
                def ep(oc, pt, Bp):
                    for (a, b, var) in parts:
                        P.op("dve", lambda e, a=a, b=b, var=var: e.scalar_tensor_tensor(out=xs[:, oc, a:b], in0=pt[:, a:b], scalar=m2[:, oc, var, gidx:gidx + 1],
                                                                                       in1=xs[:, oc, a:b], op0=ALU.mult, op1=ALU.add), reads=[Bp, Bm2, Bxs], writes=[Bxs])
                return ep
            mm_group(w_out, 16, 16, lambda k: hT[:, k, :], BhT, ep_res(0))
            rmsnorm_mod_segment(cx, eps_t, xs, Bxs, ones_f, Bones, sc2, Bsc2, hT, BhT, seg, tmp_ring, ps_aux, rstd_ring, local=True)
            def ep_mlp1(oc, pt, Bp):
                t1, Bt1 = tmp_ring.next()
                P.op("act", lambda e: e.activation(out=t1[:, :], in_=pt[:, 0:SEG], func=AF.Relu), reads=[Bp], writes=[Bt1])
                P.op("pool", lambda e: e.tensor_tensor(out=hh[:, oc, :], in0=t1[:, :], in1=t1[:, :], op=ALU.mult), reads=[Bt1], writes=[Bhh])
            mm_group(w1, 16, 64, lambda k: hT[:, k, :], BhT, ep_mlp1)
            mm_group(w2, 64, 16, lambda k: hh[:, k, :], Bhh, ep_res(3))
            if not final:
                P.dma("sp", lambda e, t0=t0: e.dma_start(out=o_x[:, :, t0:t0 + SEG].rearrange("k p t -> p k t"), in_=xs[:, :, :]), Box, reads=[Bxs])
            else:
                pt, Bp = ps_aux.next()
                for kc in range(KC):
                    sq, Bsq = tmp_ring.next()
                    P.op("act", lambda e, sq=sq, kc=kc: e.activation(out=sq[:, :], in_=xs[:, kc, :], func=AF.Square), reads=[Bxs], writes=[Bsq])
                    P.op("pe", lambda e, sq=sq, kc=kc, pt=pt: e.matmul(pt[:, 0:SEG], lhsT=ones_f[:, :], rhs=sq[:, :], start=(kc == 0), stop=(kc == KC - 1)),
                         reads=[Bsq, Bones], writes=[Bp])
                rs, Brs = rstd_ring.next()
                P.op("act", lambda e, rs=rs, pt=pt: e.activation(out=rs[:, :], in_=pt[:, 0:SEG], func=AF.Sqrt, scale=1.0 / D, bias=eps_t[:, 0:1]), reads=[Bp, Beps], writes=[Brs])
                P.op("dve", lambda e, rs=rs: e.reciprocal(out=rs[:, :], in_=rs[:, :]), reads=[Brs], writes=[Brs])
                for kc in range(KC):
                    og, Bog_ = ostg.next()
                    P.op("dve", lambda e, og=og, kc=kc, rs=rs: e.scalar_tensor_tensor(out=og[:, :], in0=xs[:, kc, :], scalar=fg[:, kc:kc + 1], in1=rs[:, :],
                                                                                     op0=ALU.mult, op1=ALU.mult), reads=[Bxs, Bfg, Brs], writes=[Bog_])
                    P.dma("sp", lambda e, og=og, kc=kc, t0=t0: e.dma_start(out=o_x[kc, :, t0:t0 + SEG], in_=og[:, :]), Box, reads=[Bog_])
        P.wait_final("sp", [Box] + ([dbg[n][1] for n in dbg] if debug_seg is not None else []))
        P.emit()
    return nc, P


L = 8192; C = 256; B = 2


def core_tokens_T(x_lat, x_ctx, core):
    b, s = core // 4, core % 4
    t = np.concatenate([x_lat[b, s * NLAT:(s + 1) * NLAT], x_ctx[b, s * NCTX:(s + 1) * NCTX]], axis=0)
    return np.ascontiguousarray(t.T)


def rope_tables(core):
    s = core % 4
    t = np.arange(s * NLAT, (s + 1) * NLAT, dtype=np.float32)
    inv_freq = (10000.0 ** (-np.arange(0, 64, 2, dtype=np.float32) / 64)).astype(np.float32)
    d = np.arange(128)
    a = d // 64; i = d % 32; half = (d % 64) // 32
    pos = np.where(a[:, None] == 0, np.floor(t / 64)[None, :], np.mod(t, 64)[None, :]).astype(np.float32)
    ang = pos * inv_freq[i][:, None]
    cosT = np.ones((128, NT), np.float32); sinT = np.zeros((128, NT), np.float32)
    cosT[:, :NLAT] = np.cos(ang)
    sinT[:, :NLAT] = np.sin(ang) * np.where(half == 0, -1.0, 1.0)[:, None]
    return cosT, sinT


def perm_matrix():
    d = np.arange(128)
    partner = np.where((d % 64) < 32, d + 32, d - 32)
    Pm = np.zeros((128, 128), np.float32)
    Pm[partner, d] = 1.0
    return Pm


def colT(v, nchunks):
    return np.ascontiguousarray(v.reshape(nchunks, 128).T)


def ssm_params(d, l, chunk):
    rowp = np.zeros((2, 3, 128, 128), np.float32); colp = np.zeros((2, 3, 128, 4), np.float32)
    BT = np.zeros((2, 2, 128, 128), np.float32); CT = np.zeros((2, 2, 128, 4, 128), np.float32)
    for dr in range(2):
        srcs = [d["ssm_a_re"][l, dr], d["ssm_a_im"][l, dr], np.repeat(d["ssm_log_dt"][l, dr][:, None], 64, axis=1)]
        bs = [d["ssm_b_re"][l, dr], d["ssm_b_im"][l, dr]]
        cs = [d["ssm_c_re"][l, dr], d["ssm_c_im"][l, dr]]
        for q in range(4):
            gA = 8 * chunk + 2 * q; gB = gA + 1
            for k in range(3):
                row = np.concatenate([srcs[k][gA], srcs[k][gB]])
                rowp[dr, k, 32 * q:32 * q + 32, :] = row[None, :]
                colp[dr, k, :, q] = row
            for k in range(2):
                BT[dr, k, 32 * q:32 * q + 16, 0:64] = bs[k][gA].T
                BT[dr, k, 32 * q + 16:32 * q + 32, 64:128] = bs[k][gB].T
                CT[dr, k, 0:64, q, 32 * q:32 * q + 16] = cs[k][gA].T
                CT[dr, k, 64:128, q, 32 * q + 16:32 * q + 32] = cs[k][gB].T
    dvec = np.ascontiguousarray(d["ssm_d"][l][128 * chunk:128 * chunk + 128, None])
    return {"rowp": rowp, "colp": colp, "BT": BT, "CT": CT, "dvec": dvec}


def conv_windows(uc_lat, uc_ctx, core, pieces, wtot):
    b, s = core // 4, core % 4
    out = np.zeros((wtot, 1024), np.float32)
    o = 0
    for (ps0, pn) in pieces:
        if ps0 < NLAT:
            src = uc_lat[b]; g0 = s * NLAT + ps0
        else:
            src = uc_ctx[b]; g0 = s * NCTX + (ps0 - NLAT)
        lo, hi = g0 - 15, g0 + pn + 15
        a, e = max(lo, 0), min(hi, src.shape[0])
        out[o + (a - lo):o + (e - lo)] = src[a:e]
        o += pn + 30
    return np.ascontiguousarray(out.T).reshape(8, 128, wtot)


_PROGS = {}


def _prog(name):
    if name not in _PROGS:
        if name == "m":
            _PROGS[name] = build_stage_m()
        elif name == "a":
            _PROGS[name] = build_stage_a()[0]
        elif name == "b1":
            _PROGS[name] = build_stage_b1()[0]
        elif name == "b2":
            _PROGS[name] = build_stage_b2()[0]
        elif name == "c":
            _PROGS[name] = build_stage_c(False)[0]
        elif name == "cf":
            _PROGS[name] = build_stage_c(True)[0]
    return _PROGS[name]


def _run(name, in_maps):
    res = run_bass_kernel_spmd(_prog(name), in_maps, core_ids=list(range(8)))
    return [{k: np.asarray(v) for k, v in r.items()} for r in res.results]


def kernel(x, c, ctx, c_ctx, norm1_g, norm2_g, w_mod, b_mod, w_in, b_gate, q_norm_g, k_norm_g, w_attn_o,
           ssm_a_re, ssm_a_im, ssm_log_dt, ssm_b_re, ssm_b_im, ssm_c_re, ssm_c_im, ssm_d, w_glu, b_glu, w_ssm_o,
           conv_w, conv_b, conv_ln_g, conv_ln_b, w_conv_o, w_out, w_mlp1, w_mlp2, final_g):
    f32 = np.float32
    d = dict(ssm_a_re=np.asarray(ssm_a_re, f32), ssm_a_im=np.asarray(ssm_a_im, f32), ssm_log_dt=np.asarray(ssm_log_dt, f32),
             ssm_b_re=np.asarray(ssm_b_re, f32), ssm_b_im=np.asarray(ssm_b_im, f32), ssm_c_re=np.asarray(ssm_c_re, f32),
             ssm_c_im=np.asarray(ssm_c_im, f32), ssm_d=np.asarray(ssm_d, f32))
    x = np.asarray(x, f32); ctx = np.asarray(ctx, f32); w_mod = np.asarray(w_mod, f32); b_mod = np.asarray(b_mod, f32)
    DEPTH = 4
    c_all = np.concatenate([np.asarray(c, f32), np.asarray(c_ctx, f32)[None, :]], axis=0)
    cT = np.ascontiguousarray(c_all.reshape(3, 16, 128).transpose(2, 1, 0))
    in_maps = []
    for core in range(8):
        wm = np.empty((48, 128, 16, 128), f32); bm = np.empty((128, 48), f32)
        for u in range(48):
            l, j = divmod(core * 48 + u, 96)
            wm[u] = w_mod[l][:, j * 128:(j + 1) * 128].reshape(16, 128, 128).transpose(1, 0, 2)
            bm[:, u] = b_mod[l][j * 128:(j + 1) * 128]
        in_maps.append({"wm": wm, "cT": cT, "bm": bm})
    outs = _run("m", in_maps)
    mod = np.empty((DEPTH, 3, 12288), f32)
    for core in range(8):
        for u in range(48):
            l, j = divmod(core * 48 + u, 96)
            mod[l, :, j * 128:(j + 1) * 128] = outs[core]["modT"][:, u, :].T
    del in_maps
    xT = [core_tokens_T(x, ctx, core).reshape(16, 128, NT) for core in range(8)]
    Pm = perm_matrix()
    ropes = [rope_tables(core) for core in range(8)]
    for l in range(DEPTH):
        w_in_l = np.ascontiguousarray(np.asarray(w_in[l], f32))
        g1T = colT(np.asarray(norm1_g[l], f32), 16)
        qkg = np.stack([np.asarray(q_norm_g[l], f32), np.asarray(k_norm_g[l], f32)], axis=1)
        bgT = colT(np.asarray(b_gate[l], f32), 48)
        in_maps = []
        for core in range(8):
            b = core // 4
            modss = np.empty((128, 16, 2, 2), f32)
            for vi, v in enumerate((b, 2)):
                modss[:, :, vi, 0] = colT(mod[l, v, 0:2048], 16)
                modss[:, :, vi, 1] = colT(mod[l, v, 2048:4096], 16)
            in_maps.append({"xT": xT[core], "w_in": w_in_l, "g1T": g1T, "modss": modss, "qkg": qkg,
                            "cosT": ropes[core][0], "sinT": ropes[core][1], "permM": Pm, "bgT": bgT})
        oa = _run("a", in_maps)
        del in_maps, w_in_l
        kT_all = []; v_all = []
        for b in range(2):
            kk = np.empty((2, 128, NKEY), oa[0]["kT"].dtype); vv = np.empty((NKEY, 256), oa[0]["v"].dtype)
            for s in range(4):
                o = oa[4 * b + s]
                kk[:, :, s * NLAT:(s + 1) * NLAT] = o["kT"][:, :, :NLAT]; kk[:, :, L + s * NCTX:L + (s + 1) * NCTX] = o["kT"][:, :, NLAT:]
                vv[s * NLAT:(s + 1) * NLAT] = o["v"][:NLAT]; vv[L + s * NCTX:L + (s + 1) * NCTX] = o["v"][NLAT:]
            kT_all.append(kk); v_all.append(vv)
        ob1 = _run("b1", [{"qT": oa[core]["qT"], "kT": kT_all[core // 4], "v": v_all[core // 4]} for core in range(8)])
        del kT_all, v_all
        in_maps = []
        for k in range(8):
            m = ssm_params(d, l, k)
            uT = np.empty((128, 2, NKEY), f32)
            for b in range(2):
                for s in range(4):
                    o = oa[4 * b + s]["uT"][k]
                    uT[:, b, C + s * NLAT:C + (s + 1) * NLAT] = o[:, :NLAT]; uT[:, b, s * NCTX:(s + 1) * NCTX] = o[:, NLAT:]
            m["uT"] = uT
            in_maps.append(m)
        ob2 = _run("b2", in_maps)
        del in_maps
        uc_lat = np.empty((2, L, 1024), f32); uc_ctx = np.empty((2, C, 1024), f32)
        for core in range(8):
            b, s = core // 4, core % 4
            u2 = oa[core]["ucT"].reshape(1024, NT)
            uc_lat[b, s * NLAT:(s + 1) * NLAT] = u2[:, :NLAT].T; uc_ctx[b, s * NCTX:(s + 1) * NCTX] = u2[:, NLAT:].T
        Wl = {k: np.ascontiguousarray(np.asarray(v[l], f32)) for k, v in (("w_glu", w_glu), ("w_ssm_o", w_ssm_o), ("w_conv_o", w_conv_o),
                                                                          ("w_attn_o", w_attn_o), ("w_out", w_out), ("w_mlp1", w_mlp1), ("w_mlp2", w_mlp2))}
        vec8 = np.ascontiguousarray(np.stack([colT(np.asarray(t[l], f32), 8) for t in (b_glu, conv_b, conv_ln_g, conv_ln_b)], axis=2))
        convw = np.ascontiguousarray(np.asarray(conv_w[l], f32).T.reshape(8, 128, 31).transpose(1, 0, 2))
        g2T = colT(np.asarray(norm2_g[l], f32), 16); fgT = colT(np.asarray(final_g, f32), 16)
        in_maps = []
        for core in range(8):
            b, s = core // 4, core % 4
            m = dict(Wl)
            m["xT"] = xT[core]; m["attT"] = ob1[core]["attT"]; m["gT"] = oa[core]["gT"]
            ys = np.empty((8, 128, NT), f32)
            for k in range(8):
                ys[k, :, :NLAT] = ob2[k]["ysT"][:, b, C + s * NLAT:C + (s + 1) * NLAT]; ys[k, :, NLAT:] = ob2[k]["ysT"][:, b, s * NCTX:(s + 1) * NCTX]
            m["ysT"] = ys
            m["ucH"] = conv_windows(uc_lat, uc_ctx, core, CONV_PIECES, WTOT)
            m["vec8"] = vec8; m["convw"] = convw; m["g2T"] = g2T; m["fgT"] = fgT
            mod2 = np.empty((128, 16, 2, 4), f32)
            for vi, v in enumerate((b, 2)):
                for i, mi in enumerate((2, 3, 4, 5)):
                    mod2[:, :, vi, i] = colT(mod[l, v, mi * 2048:(mi + 1) * 2048], 16)
            m["mod2"] = mod2
            in_maps.append(m)
        oc = _run("cf" if l == DEPTH - 1 else "c", in_maps)
        del in_maps, oa, ob1, ob2
        xT = [oc[core]["xT_out"] for core in range(8)]
    out = np.empty((2, L, D), f32)
    for core in range(8):
        b, s = core // 4, core % 4
        out[b, s * NLAT:(s + 1) * NLAT] = xT[core].reshape(D, NT)[:, :NLAT].T
    return out
```

```python
import contextlib, math
import numpy as np
import concourse.bass as bass
import concourse.mybir as mybir
from concourse.bass_utils import run_bass_kernel_spmd

F32 = mybir.dt.float32
BF16 = mybir.dt.bfloat16
AF = mybir.ActivationFunctionType
ALU = mybir.AluOpType
AX = mybir.AxisListType

ENGS = ("pe", "act", "dve", "pool", "sp")
CONSERVATIVE = False


class Buf:
    __slots__ = ("name", "lw", "rd", "dsem", "dcnt", "lw_dma")

    def __init__(self, name):
        self.name = name
        self.lw = None
        self.rd = []
        self.dsem = None
        self.dcnt = 0
        self.lw_dma = False


class Prog:
    def __init__(self, nc):
        self.nc = nc
        self.ops = {e: [] for e in ENGS}
        self.cnt = {e: 0 for e in ENGS}
        self.waited = {e: {} for e in ENGS}
        self.nsem_dma = 0
        self.sems = {}
        self.final_waits = []
        self.dma_final = {}

    def _need(self, eng, ev, waits):
        if ev is None:
            return
        k, v = ev
        if self.waited[eng].get(k, 0) >= v:
            return
        self.waited[eng][k] = v
        waits[k] = max(waits.get(k, 0), v)

    def _deps(self, eng, reads, writes, dma_dst=None, is_dma=False):
        waits = {}
        for b in reads:
            self._need(eng, b.lw, waits)
        for b in writes:
            if dma_dst is b and b.lw_dma and not b.rd:
                pass
            elif b.lw is not None and (is_dma or b.lw[0] != eng or (CONSERVATIVE and eng != "pe")):
                self._need(eng, b.lw, waits)
            for ev in b.rd:
                if is_dma or ev[0] != eng or (CONSERVATIVE and eng != "pe"):
                    self._need(eng, ev, waits)
        return list(waits.items())

    def op(self, eng, fn, reads=(), writes=()):
        waits = self._deps(eng, reads, writes)
        self.cnt[eng] += 1
        ev = (eng, self.cnt[eng])
        self.ops[eng].append((waits, fn, (eng, 1)))
        for b in reads:
            b.rd.append(ev)
        for b in writes:
            b.lw = ev
            b.rd = []
            b.lw_dma = False
        return ev

    def dma(self, eng, fn, dst, reads=(), extra_writes=(), inc=16):
        waits = self._deps(eng, reads, (dst,) + tuple(extra_writes), dma_dst=dst, is_dma=True)
        if dst.dsem is None:
            dst.dsem = "d%d" % self.nsem_dma
            self.nsem_dma += 1
        dst.dcnt += inc
        ev = (dst.dsem, dst.dcnt)
        self.dma_final[dst.dsem] = dst.dcnt
        self.ops[eng].append((waits, fn, (dst.dsem, inc)))
        for b in reads:
            b.rd.append(ev)
        for b in (dst,) + tuple(extra_writes):
            b.lw = ev
            b.rd = []
            b.lw_dma = True
        return ev

    def wait_final(self, eng, bufs):
        waits = {}
        for b in bufs:
            self._need(eng, b.lw, waits)
        self.ops[eng].append((list(waits.items()), None, None))

    def barrier(self):
        evs = [(e, self.cnt[e]) for e in ENGS if self.cnt[e] > 0] + list(self.dma_final.items())
        for eng in ENGS:
            waits = {}
            for ev in evs:
                if ev[0] != eng:
                    self._need(eng, ev, waits)
            self.ops[eng].append((list(waits.items()), None, None))

    def emit_pooled(self, pool):
        nc = self.nc
        assert self.nsem_dma <= len(pool["dma"]), (self.nsem_dma, len(pool["dma"]))
        hs = {}; base = {}
        for e in ENGS:
            hs[e], base[e] = pool["eng"][e]
        for i in range(self.nsem_dma):
            hs["d%d" % i], base["d%d" % i] = pool["dma"][i]
        with nc.Block() as block:
            def runner(ename):
                def run(eng):
                    for waits, fn, inc in self.ops[ename]:
                        for k, v in waits:
                            eng.wait_ge(hs[k], base[k] + v)
                        if fn is not None:
                            ins = fn(eng)
                            ins.then_inc(hs[inc[0]], inc[1])
                return run
            block.tensor(runner("pe")); block.scalar(runner("act")); block.vector(runner("dve"))
            block.gpsimd(runner("pool")); block.sync(runner("sp"))
        for e in ENGS:
            pool["eng"][e][1] += self.cnt[e]
        for i in range(self.nsem_dma):
            pool["dma"][i][1] += self.dma_final.get("d%d" % i, 0)

    def emit(self):
        nc = self.nc
        import contextlib
        with contextlib.ExitStack() as st:
            for e in ENGS:
                self.sems[e] = st.enter_context(nc.semaphore("s_" + e))
            for i in range(self.nsem_dma):
                self.sems["d%d" % i] = st.enter_context(nc.semaphore("sd%d" % i))
            block = st.enter_context(nc.Block())
            sems = self.sems

            def runner(ename):
                def run(eng):
                    for waits, fn, inc in self.ops[ename]:
                        for k, v in waits:
                            eng.wait_ge(sems[k], v)
                        if fn is not None:
                            ins = fn(eng)
                            ins.then_inc(sems[inc[0]], inc[1])
                return run

            block.tensor(runner("pe"))
            block.scalar(runner("act"))
            block.vector(runner("dve"))
            block.gpsimd(runner("pool"))
            block.sync(runner("sp"))

    def stats(self):
        return {e: len(self.ops[e]) for e in ENGS}


D = 2048; NT = 2112; NLAT = 2048; NCTX = 64; SEG = 352; NSEG = 6
KC = 16
IN_COLS = 10752
EPS = 1e-6


class Ctx:
    _uid = [0]

    def __init__(self, nc, io=None, pool=None):
        self.nc = nc; self.P = Prog(nc); self.st = contextlib.ExitStack(); self.io = io; self.pool = pool
        Ctx._uid[0] += 1
        self.n = Ctx._uid[0] * 100000

    def sb(self, shape, dt, name=None):
        self.n += 1
        t = self.st.enter_context(self.nc.sbuf_tensor(name or ("t%d" % self.n), list(shape), dt))
        return t, Buf(name or ("t%d" % self.n))

    def ps(self, shape, dt=F32, name=None):
        self.n += 1
        t = self.st.enter_context(self.nc.psum_tensor(name or ("p%d" % self.n), list(shape), dt))
        return t, Buf(name or ("p%d" % self.n))

    def ring(self, n, shape, dt, psum=False):
        return Ring([self.ps(shape, dt) if psum else self.sb(shape, dt) for _ in range(n)])

    def din(self, name, shape, dt=F32):
        if self.io is not None:
            ap = self.io[name]
            assert list(ap.shape) == list(shape), (name, ap.shape, shape)
            return ap
        return self.nc.dram_tensor(name, list(shape), dt, kind="ExternalInput").ap()

    def dout(self, name, shape, dt=F32):
        if self.io is not None:
            ap = self.io[name]
            assert list(ap.shape) == list(shape), (name, ap.shape, shape)
            return ap, Buf(name)
        return self.nc.dram_tensor(name, list(shape), dt, kind="ExternalOutput").ap(), Buf(name)

    def finish(self, out_bufs):
        if self.pool is None:
            self.P.wait_final("sp", out_bufs)
            self.P.emit()
        else:
            self.P.barrier()
            self.P.emit_pooled(self.pool)


class Ring:
    def __init__(self, items):
        self.items = items; self.i = 0

    def next(self):
        it = self.items[self.i % len(self.items)]; self.i += 1
        return it


def make_ident(cx, dt=BF16):
    P = cx.P
    idf, Bidf = cx.sb([128, 128], F32)
    P.op("pool", lambda e: e.memset(idf[:, :], 0.0), writes=[Bidf])
    P.op("pool", lambda e: e.affine_select(out=idf[:, :], in_=idf[:, :], pattern=[[-1, 128]],
                                           compare_op=ALU.not_equal, fill=1.0, base=0, channel_multiplier=1),
         reads=[Bidf], writes=[Bidf])
    if dt == F32:
        return idf, Bidf
    idb, Bidb = cx.sb([128, 128], dt)
    P.op("dve", lambda e: e.tensor_copy(out=idb[:, :], in_=idf[:, :]), reads=[Bidf], writes=[Bidb])
    return idb, Bidb


def build_stage_m(nunits=48, nvar=3, nc=None, io=None, pool=None):
    nc = nc or bass.Bass("TRN2", target_bir_lowering=False)
    cx = Ctx(nc, io, pool); P = cx.P
    wm = cx.din("wm", [nunits, 128, KC, 128])
    cT = cx.din("cT", [128, KC, nvar])
    bm = cx.din("bm", [128, nunits])
    out, Bout = cx.dout("modT", [128, nunits, nvar])
    with cx.st:
        cs, Bcs = cx.sb([128, KC, nvar], F32)
        cb, Bcb = cx.sb([128, KC, nvar], BF16)
        bms, Bbms = cx.sb([128, nunits], F32)
        res, Bres = cx.sb([128, nunits, nvar], F32)
        wr = cx.ring(3, [128, KC, 128], BF16)
        pr = cx.ring(2, [128, 4], F32, psum=True)
        P.dma("sp", lambda e: e.dma_start(out=cs[:, :, :], in_=cT[:, :, :]), Bcs)
        P.dma("sp", lambda e: e.dma_start(out=bms[:, :], in_=bm[:, :]), Bbms)
        P.op("act", lambda e: e.activation(out=cb[:, :, :], in_=cs[:, :, :], func=AF.Silu), reads=[Bcs], writes=[Bcb])
        for u in range(nunits):
            wt, Bw = wr.next()
            P.dma("pool", lambda e, wt=wt, u=u: e.dma_start(out=wt[:, :, :], in_=wm[u, :, :, :]), Bw)
            pt, Bp = pr.next()
            for kc in range(KC):
                P.op("pe", lambda e, wt=wt, pt=pt, kc=kc: e.matmul(pt[:, 0:nvar], lhsT=wt[:, kc, :], rhs=cb[:, kc, :],
                                                                   start=(kc == 0), stop=(kc == KC - 1)),
                     reads=[Bw, Bcb], writes=[Bp])
            P.op("dve", lambda e, pt=pt, u=u: e.tensor_scalar(out=res[:, u, :], in0=pt[:, 0:nvar], scalar1=bms[:, u:u + 1],
                                                              scalar2=None, op0=ALU.add),
                 reads=[Bp, Bbms], writes=[Bres])
        P.dma("sp", lambda e: e.dma_start(out=out[:, :, :], in_=res[:, :, :]), Bout, reads=[Bres])
        cx.finish([Bout])
    return nc


def rmsnorm_mod_segment(cx, eps_t, xs, Bxs, ones_f, Bones, sc_all, Bsc, hT, BhT, seg, tmp_ring, ps_ring, rstd_ring, local=False):
    P = cx.P
    pt, Bp = ps_ring.next()
    for kc in range(KC):
        sq, Bsq = tmp_ring.next()
        P.op("act", lambda e, sq=sq, kc=kc: e.activation(out=sq[:, :], in_=xs[:, kc, :], func=AF.Square),
             reads=[Bxs], writes=[Bsq])
        P.op("pe", lambda e, sq=sq, pt=pt, kc=kc: e.matmul(pt[:, 0:SEG], lhsT=ones_f[:, :], rhs=sq[:, :],
                                                           start=(kc == 0), stop=(kc == KC - 1)),
             reads=[Bsq, Bones], writes=[Bp])
    rstd, Br = rstd_ring.next()
    P.op("act", lambda e: e.activation(out=rstd[:, :], in_=pt[:, 0:SEG], func=AF.Sqrt, scale=1.0 / D, bias=eps_t[:, 0:1]),
         reads=[Bp], writes=[Br])
    P.op("dve", lambda e: e.reciprocal(out=rstd[:, :], in_=rstd[:, :]), reads=[Br], writes=[Br])
    t0 = seg * SEG
    ho = 0 if local else t0
    if seg == NSEG - 1:
        parts = [(0, NLAT - t0, 0), (NLAT - t0, SEG, 1)]
    else:
        parts = [(0, SEG, 0)]
    for kc in range(KC):
        tm, Bt = tmp_ring.next()
        eng = "dve" if kc % 2 == 0 else "pool"
        P.op(eng, lambda e, tm=tm, kc=kc: e.tensor_tensor(out=tm[:, :], in0=xs[:, kc, :], in1=rstd[:, :], op=ALU.mult),
             reads=[Bxs, Br], writes=[Bt])
        for (a, b, var) in parts:
            P.op(eng, lambda e, tm=tm, kc=kc, a=a, b=b, var=var: e.tensor_scalar(
                out=hT[:, kc, ho + a:ho + b], in0=tm[:, a:b], scalar1=sc_all[:, kc, var, 0:1], scalar2=sc_all[:, kc, var, 1:2],
                op0=ALU.mult, op1=ALU.add), reads=[Bt, Bsc], writes=[BhT])


def build_stage_a(nc=None, io=None, pool=None):
    nc = nc or bass.Bass("TRN2", target_bir_lowering=False)
    cx = Ctx(nc, io, pool); P = cx.P
    xT = cx.din("xT", [KC, 128, NT])
    w_in = cx.din("w_in", [D, IN_COLS])
    gT = cx.din("g1T", [128, KC])
    modss = cx.din("modss", [128, KC, 2, 2])
    qkg = cx.din("qkg", [128, 2])
    cosT = cx.din("cosT", [128, NT]); sinT = cx.din("sinT", [128, NT])
    permM = cx.din("permM", [128, 128])
    bgT = cx.din("bgT", [128, 48])
    o_q, Boq = cx.dout("qT", [8, 128, NT], BF16)
    o_k, Bok = cx.dout("kT", [2, 128, NT], BF16)
    o_v, Bov = cx.dout("v", [NT, 256], BF16)
    o_u, Bou = cx.dout("uT", [8, 128, NT], F32)
    o_uc, Bouc = cx.dout("ucT", [8, 128, NT], F32)
    o_g, Bog = cx.dout("gT", [48, 128, NT], BF16)
    w_v = w_in.rearrange("(kc p) n -> p kc n", p=128)
    with cx.st:
        ones_f, Bones = cx.sb([128, 128], F32)
        P.op("pool", lambda e: e.memset(ones_f[:, :], 1.0), writes=[Bones])
        eps_t, Beps = cx.sb([128, 1], F32)
        P.op("pool", lambda e: e.memset(eps_t[:, :], EPS), writes=[Beps])
        g1, Bg1 = cx.sb([128, KC], F32)
        ms, Bms = cx.sb([128, KC, 2, 2], F32)
        sc_all, Bsc = cx.sb([128, KC, 2, 2], F32)
        qk, Bqk = cx.sb([128, 2], F32)
        cs, Bcs = cx.sb([128, NT], F32); sn, Bsn = cx.sb([128, NT], F32)
        pm, Bpm = cx.sb([128, 128], F32)
        bg, Bbg = cx.sb([128, 48], F32)
        hT, BhT = cx.sb([128, KC, NT], BF16)
        P.dma("sp", lambda e: e.dma_start(out=g1[:, :], in_=gT[:, :]), Bg1)
        P.dma("sp", lambda e: e.dma_start(out=ms[:, :, :, :], in_=modss[:, :, :, :]), Bms)
        P.dma("sp", lambda e: e.dma_start(out=qk[:, :], in_=qkg[:, :]), Bqk)
        P.dma("sp", lambda e: e.dma_start(out=cs[:, :], in_=cosT[:, :]), Bcs)
        P.dma("sp", lambda e: e.dma_start(out=sn[:, :], in_=sinT[:, :]), Bsn)
        P.dma("sp", lambda e: e.dma_start(out=pm[:, :], in_=permM[:, :]), Bpm)
        P.dma("sp", lambda e: e.dma_start(out=bg[:, :], in_=bgT[:, :]), Bbg)
        for var in range(2):
            P.op("dve", lambda e, var=var: e.scalar_tensor_tensor(out=sc_all[:, :, var, 0], in0=ms[:, :, var, 1], scalar=1.0,
                                                                  in1=g1[:, :], op0=ALU.add, op1=ALU.mult),
                 reads=[Bms, Bg1], writes=[Bsc])
            P.op("dve", lambda e, var=var: e.tensor_copy(out=sc_all[:, :, var, 1], in_=ms[:, :, var, 0]),
                 reads=[Bms], writes=[Bsc])
        xr = cx.ring(1, [128, KC, SEG], F32)
        tmp_ring = cx.ring(4, [128, SEG], F32)
        rstd_ring = cx.ring(2, [128, SEG], F32)
        ps_aux = cx.ring(2, [128, 512], F32, psum=True)
        for seg in range(NSEG):
            xs, Bxs = xr.next()
            P.dma("sp", lambda e, xs=xs, seg=seg: e.dma_start(
                out=xs[:, :, :], in_=xT[:, :, seg * SEG:(seg + 1) * SEG].rearrange("k p t -> p k t")), Bxs)
            rmsnorm_mod_segment(cx, eps_t, xs, Bxs, ones_f, Bones, sc_all, Bsc, hT, BhT, seg, tmp_ring, ps_aux, rstd_ring)
        wr = cx.ring(2, [128, KC, 256], BF16)
        ps_mm = cx.ring(4, [128, 512], F32, psum=True)
        stg_b = cx.ring(3, [128, SEG], BF16)
        stg_f = cx.ring(3, [128, SEG], F32)
        sig_ring = cx.ring(2, [128, NT], F32)
        groups = [0, 2, 4, 6, 8, 10, 12, 14, 16, 18]
        for i in (0, 2, 4, 6):
            groups += [28 + i, 20 + i]
        groups += list(range(36, 84, 2))
        sig_tiles = {}
        for g0 in groups:
            wt, Bw = wr.next()
            P.dma("pool", lambda e, wt=wt, g0=g0: e.dma_start(out=wt[:, :, :], in_=w_v[:, :, g0 * 128:g0 * 128 + 256]), Bw)
            for j in range(2):
                ct = g0 + j
                if ct in (10, 11):
                    continue
                if 28 <= ct < 36:
                    sg, Bsg = sig_ring.next(); sig_tiles[ct - 8] = (sg, Bsg)
                for seg in range(NSEG):
                    t0 = seg * SEG
                    pt, Bp = ps_mm.next()
                    for kc in range(KC):
                        P.op("pe", lambda e, pt=pt, wt=wt, j=j, kc=kc, t0=t0: e.matmul(
                            pt[:, 0:SEG], lhsT=wt[:, kc, j * 128:(j + 1) * 128], rhs=hT[:, kc, t0:t0 + SEG],
                            start=(kc == 0), stop=(kc == KC - 1)), reads=[Bw, BhT], writes=[Bp])
                    if ct < 10:
                        gi = 0 if ct < 8 else 1
                        sq, Bsq = tmp_ring.next()
                        P.op("act", lambda e, sq=sq, pt=pt: e.activation(out=sq[:, :], in_=pt[:, 0:SEG], func=AF.Square),
                             reads=[Bp], writes=[Bsq])
                        pa, Bpa = ps_aux.next()
                        P.op("pe", lambda e, pa=pa, sq=sq: e.matmul(pa[:, 0:SEG], lhsT=ones_f[:, :], rhs=sq[:, :],
                                                                    start=True, stop=True), reads=[Bsq, Bones], writes=[Bpa])
                        rstd, Br = rstd_ring.next()
                        P.op("act", lambda e, rstd=rstd, pa=pa: e.activation(out=rstd[:, :], in_=pa[:, 0:SEG], func=AF.Sqrt,
                                                                             scale=1.0 / 128, bias=eps_t[:, 0:1]), reads=[Bpa], writes=[Br])
                        P.op("dve", lambda e, rstd=rstd: e.reciprocal(out=rstd[:, :], in_=rstd[:, :]), reads=[Br], writes=[Br])
                        xn, Bxn = tmp_ring.next()
                        P.op("dve", lambda e, xn=xn, pt=pt, rstd=rstd, gi=gi: e.scalar_tensor_tensor(
                            out=xn[:, :], in0=pt[:, 0:SEG], scalar=qk[:, gi:gi + 1], in1=rstd[:, :], op0=ALU.mult, op1=ALU.mult),
                            reads=[Bp, Bqk, Br], writes=[Bxn])
                        pa2, Bpa2 = ps_aux.next()
                        P.op("pe", lambda e, pa2=pa2, xn=xn: e.matmul(pa2[:, 0:SEG], lhsT=pm[:, :], rhs=xn[:, :],
                                                                      start=True, stop=True), reads=[Bxn, Bpm], writes=[Bpa2])
                        t1, Bt1 = tmp_ring.next()
                        P.op("pool", lambda e, t1=t1, xn=xn, t0=t0: e.tensor_tensor(out=t1[:, :], in0=xn[:, :], in1=cs[:, t0:t0 + SEG],
                                                                                   op=ALU.mult), reads=[Bxn, Bcs], writes=[Bt1])
                        t2, Bt2 = tmp_ring.next()
                        P.op("dve", lambda e, t2=t2, pa2=pa2, t0=t0: e.tensor_tensor(out=t2[:, :], in0=pa2[:, 0:SEG], in1=sn[:, t0:t0 + SEG],
                                                                                    op=ALU.mult), reads=[Bpa2, Bsn], writes=[Bt2])
                        ob, Bob = stg_b.next()
                        P.op("pool", lambda e, ob=ob, t1=t1, t2=t2: e.tensor_tensor(out=ob[:, :], in0=t1[:, :], in1=t2[:, :], op=ALU.add),
                             reads=[Bt1, Bt2], writes=[Bob])
                        if ct < 8:
                            P.dma("sp", lambda e, ob=ob, ct=ct, t0=t0: e.dma_start(out=o_q[ct, :, t0:t0 + SEG], in_=ob[:, :]), Boq, reads=[Bob])
                        else:
                            P.dma("sp", lambda e, ob=ob, ct=ct, t0=t0: e.dma_start(out=o_k[ct - 8, :, t0:t0 + SEG], in_=ob[:, :]), Bok, reads=[Bob])
                    elif ct < 20:
                        of, Bof = stg_f.next()
                        P.op("act", lambda e, of=of, pt=pt: e.activation(out=of[:, :], in_=pt[:, 0:SEG], func=AF.Copy),
                             reads=[Bp], writes=[Bof])
                        P.dma("sp", lambda e, of=of, ct=ct, t0=t0: e.dma_start(out=o_u[ct - 12, :, t0:t0 + SEG], in_=of[:, :]), Bou, reads=[Bof])
                    elif ct < 28:
                        sg, Bsg = sig_tiles[ct]
                        of, Bof = stg_f.next()
                        P.op("dve", lambda e, of=of, pt=pt, sg=sg, t0=t0: e.tensor_tensor(out=of[:, :], in0=pt[:, 0:SEG], in1=sg[:, t0:t0 + SEG],
                                                                                         op=ALU.mult), reads=[Bp, Bsg], writes=[Bof])
                        P.dma("sp", lambda e, of=of, ct=ct, t0=t0: e.dma_start(out=o_uc[ct - 20, :, t0:t0 + SEG], in_=of[:, :]), Bouc, reads=[Bof])
                    elif ct < 36:
                        P.op("act", lambda e, sg=sg, pt=pt, t0=t0: e.activation(out=sg[:, t0:t0 + SEG], in_=pt[:, 0:SEG], func=AF.Sigmoid),
                             reads=[Bp], writes=[Bsg])
                    else:
                        ob, Bob = stg_b.next()
                        P.op("act", lambda e, ob=ob, pt=pt, ct=ct: e.activation(out=ob[:, :], in_=pt[:, 0:SEG], func=AF.Sigmoid,
                                                                               bias=bg[:, ct - 36:ct - 35]), reads=[Bp, Bbg], writes=[Bob])
                        P.dma("sp", lambda e, ob=ob, ct=ct, t0=t0: e.dma_start(out=o_g[ct - 36, :, t0:t0 + SEG], in_=ob[:, :]), Bog, reads=[Bob])
            if g0 == 10:
                vstg = cx.ring(2, [128, 256], BF16)
                for tb in range(17):
                    n = 128 if tb < 16 else 64
                    pt, Bp = ps_mm.next()
                    for kc in range(KC):
                        P.op("pe", lambda e, pt=pt, wt=wt, kc=kc, tb=tb, n=n: e.matmul(
                            pt[0:n, 0:256], lhsT=hT[:, kc, tb * 128:tb * 128 + n], rhs=wt[:, kc, 0:256],
                            start=(kc == 0), stop=(kc == KC - 1)), reads=[Bw, BhT], writes=[Bp])
                    vs, Bvs = vstg.next()
                    P.op("act", lambda e, vs=vs, pt=pt, n=n: e.activation(out=vs[0:n, :], in_=pt[0:n, 0:256], func=AF.Copy),
                         reads=[Bp], writes=[Bvs])
                    P.dma("sp", lambda e, vs=vs, tb=tb, n=n: e.dma_start(out=o_v[tb * 128:tb * 128 + n, :], in_=vs[0:n, :]), Bov, reads=[Bvs])
        cx.finish([Boq, Bok, Bov, Bou, Bouc, Bog])
    return nc, P


NKEY = 8448; NKT = 66

def build_stage_b1(nc=None, io=None, pool=None):
    nc = nc or bass.Bass("TRN2", target_bir_lowering=False)
    cx = Ctx(nc, io, pool); P = cx.P
    qT = cx.din("qT", [8, 128, NT], BF16)
    kT = cx.din("kT", [2, 128, NKEY], BF16)
    vv = cx.din("v", [NKEY, 256], BF16)
    o_a, Boa = cx.dout("attT", [8, 128, NT], BF16)
    scale = 1.0 / math.sqrt(128.0)
    with cx.st:
        qs, Bqs = cx.sb([128, 8, NT], BF16)
        ks, Bks = cx.sb([128, 2, NKEY], BF16)
        vs, Bvs = cx.sb([128, NKT, 256], BF16)
        ones_b, Bones = cx.sb([128, 128], BF16)
        P.op("pool", lambda e: e.memset(ones_b[:, :], 1.0), writes=[Bones])
        P.dma("sp", lambda e: e.dma_start(out=qs[:, :, :], in_=qT.rearrange("h p t -> p h t")), Bqs)
        for h in range(2):
            P.dma("sp", lambda e, h=h: e.dma_start(out=ks[:, h, :], in_=kT[h, :, :]), Bks)
        P.dma("sp", lambda e: e.dma_start(out=vs[:, :, :], in_=vv.rearrange("(kt p) c -> p kt c", p=128)), Bvs)
        ps_s = cx.ring(4, [128, 512], F32, psum=True)
        ps_o = cx.ring(2, [128, 512], F32, psum=True)
        ps_d = cx.ring(2, [128, 512], F32, psum=True)
        pT = cx.ring(4, [128, 512], BF16)
        rc = cx.ring(2, [128, 512], F32)
        ob = cx.ring(2, [128, 512], BF16)
        qtiles = [(i * 512, 512, list(range(NKT))) for i in range(4)] + [(NLAT, NCTX, [64, 65])]
        for h in range(8):
            kv = h // 4
            for (q0, ql, kts) in qtiles:
                po, Bpo = ps_o.next(); pd, Bpd = ps_d.next()
                SK = 2
                stiles = {}

                def issue_qk(i):
                    kt = kts[i]
                    pss, Bps = ps_s.next()
                    P.op("pe", lambda e, pss=pss, kt=kt, ql=ql, kv=kv, h=h, q0=q0: e.matmul(
                        pss[:, 0:ql], lhsT=ks[:, kv, kt * 128:(kt + 1) * 128], rhs=qs[:, h, q0:q0 + ql], start=True, stop=True),
                        reads=[Bks, Bqs], writes=[Bps])
                    stiles[i] = (pss, Bps)

                for i in range(min(SK, len(kts))):
                    issue_qk(i)
                for i, kt in enumerate(kts):
                    if i + SK < len(kts):
                        issue_qk(i + SK)
                    pss, Bps = stiles.pop(i)
                    pt, Bpt = pT.next()
                    P.op("act", lambda e, pt=pt, pss=pss, ql=ql: e.activation(out=pt[:, 0:ql], in_=pss[:, 0:ql], func=AF.Exp, scale=scale),
                         reads=[Bps], writes=[Bpt])
                    first = (i == 0); last = (i == len(kts) - 1)
                    P.op("pe", lambda e, pt=pt, kt=kt, first=first, last=last, po=po, ql=ql, kv=kv: e.matmul(
                        po[:, 0:ql], lhsT=vs[:, kt, kv * 128:(kv + 1) * 128], rhs=pt[:, 0:ql], start=first, stop=last),
                        reads=[Bvs, Bpt], writes=[Bpo])
                    P.op("pe", lambda e, pt=pt, first=first, last=last, pd=pd, ql=ql: e.matmul(
                        pd[:, 0:ql], lhsT=ones_b[:, :], rhs=pt[:, 0:ql], start=first, stop=last),
                        reads=[Bones, Bpt], writes=[Bpd])
                r, Br = rc.next()
                P.op("dve", lambda e, r=r, pd=pd, ql=ql: e.reciprocal(out=r[:, 0:ql], in_=pd[:, 0:ql]), reads=[Bpd], writes=[Br])
                o, Bo = ob.next()
                P.op("dve", lambda e, o=o, po=po, r=r, ql=ql: e.tensor_tensor(out=o[:, 0:ql], in0=po[:, 0:ql], in1=r[:, 0:ql], op=ALU.mult),
                     reads=[Bpo, Br], writes=[Bo])
                P.dma("sp", lambda e, o=o, h=h, q0=q0, ql=ql: e.dma_start(out=o_a[h, :, q0:q0 + ql], in_=o[:, 0:ql]), Boa, reads=[Bo])
        cx.finish([Boa])
    return nc, P


TT = 512

def build_stage_b2(nc=None, io=None, pool=None):
    nc = nc or bass.Bass("TRN2", target_bir_lowering=False)
    cx = Ctx(nc, io, pool); P = cx.P
    uT = cx.din("uT", [128, 2, NKEY])
    rowp = cx.din("rowp", [2, 3, 128, 128])
    colp = cx.din("colp", [2, 3, 128, 4])
    BT = cx.din("BT", [2, 2, 128, 128])
    CT = cx.din("CT", [2, 2, 128, 4, 128])
    dvec = cx.din("dvec", [128, 1])
    o_y, Boy = cx.dout("ysT", [128, 2, NKEY])
    with cx.st:
        halfpi, Bhp = cx.sb([128, 1], F32)
        P.op("pool", lambda e: e.memset(halfpi[:, :], math.pi / 2), writes=[Bhp])
        dv, Bdv = cx.sb([128, 1], F32)
        P.dma("sp", lambda e: e.dma_start(out=dv[:, :], in_=dvec[:, :]), Bdv)
        cnt = [0]
        def tmp(shape, dt=F32):
            return cx.sb(shape, dt)

        def ew(eng, fn, reads, writes):
            P.op(eng, fn, reads=reads, writes=writes)

        def cos_sin(theta, Bth, W):
            s, Bs = tmp([128, W]); c, Bc = tmp([128, W]); t1, Bt1 = tmp([128, W]); t2, Bt2 = tmp([128, W])
            ew("act", lambda e: e.activation(out=s[:, :], in_=theta[:, :], func=AF.Sin, scale=1.0 / 16), [Bth], [Bs])
            ew("act", lambda e: e.activation(out=c[:, :], in_=theta[:, :], func=AF.Sin, scale=1.0 / 16, bias=halfpi[:, 0:1]), [Bth, Bhp], [Bc])
            for _ in range(4):
                ew("dve", lambda e: e.tensor_tensor(out=t1[:, :], in0=c[:, :], in1=c[:, :], op=ALU.mult), [Bc], [Bt1])
                ew("dve", lambda e: e.tensor_tensor(out=t2[:, :], in0=s[:, :], in1=s[:, :], op=ALU.mult), [Bs], [Bt2])
                ew("dve", lambda e: e.scalar_tensor_tensor(out=s[:, :], in0=s[:, :], scalar=2.0, in1=c[:, :], op0=ALU.mult, op1=ALU.mult),
                   [Bs, Bc], [Bs])
                ew("dve", lambda e: e.tensor_tensor(out=c[:, :], in0=t1[:, :], in1=t2[:, :], op=ALU.subtract), [Bt1, Bt2], [Bc])
            return (c, Bc), (s, Bs)

        WBre = []; WBim = []; CTre = []; CTimN = []; Ec = []; Es = []; rcol = []
        for d in range(2):
            pr_, Bpr_ = tmp([128, 3, 128])
            P.dma("sp", lambda e, d=d, pr_=pr_: e.dma_start(out=pr_[:, :, :], in_=rowp[d].rearrange("k p n -> p k n")), Bpr_)
            bt, Bbt = tmp([128, 2, 128])
            P.dma("sp", lambda e, d=d, bt=bt: e.dma_start(out=bt[:, :, :], in_=BT[d].rearrange("k p n -> p k n")), Bbt)
            dt_, Bdt = tmp([128, 128]); ard, Bard = tmp([128, 128]); th, Bth = tmp([128, 128]); mag, Bmag = tmp([128, 128])
            ew("act", lambda e, dt_=dt_, pr_=pr_: e.activation(out=dt_[:, :], in_=pr_[:, 2, :], func=AF.Exp), [Bpr_], [Bdt])
            ew("dve", lambda e, ard=ard, pr_=pr_, dt_=dt_: e.tensor_tensor(out=ard[:, :], in0=pr_[:, 0, :], in1=dt_[:, :], op=ALU.mult), [Bpr_, Bdt], [Bard])
            ew("dve", lambda e, th=th, pr_=pr_, dt_=dt_: e.tensor_tensor(out=th[:, :], in0=pr_[:, 1, :], in1=dt_[:, :], op=ALU.mult), [Bpr_, Bdt], [Bth])
            ew("act", lambda e, mag=mag, ard=ard: e.activation(out=mag[:, :], in_=ard[:, :], func=AF.Exp), [Bard], [Bmag])
            (c, Bc), (s, Bs) = cos_sin(th, Bth, 128)
            lr, Blr = tmp([128, 128]); li, Bli = tmp([128, 128]); den, Bden = tmp([128, 128]); t1, Bt1 = tmp([128, 128]); t2, Bt2 = tmp([128, 128])
            fre, Bfre = tmp([128, 128]); fim, Bfim = tmp([128, 128])
            ew("dve", lambda e, lr=lr, mag=mag, c=c: e.tensor_tensor(out=lr[:, :], in0=mag[:, :], in1=c[:, :], op=ALU.mult), [Bmag, Bc], [Blr])
            ew("dve", lambda e, li=li, mag=mag, s=s: e.tensor_tensor(out=li[:, :], in0=mag[:, :], in1=s[:, :], op=ALU.mult), [Bmag, Bs], [Bli])
            ew("dve", lambda e, lr=lr: e.tensor_scalar(out=lr[:, :], in0=lr[:, :], scalar1=-1.0, scalar2=None, op0=ALU.add), [Blr], [Blr])
            ew("dve", lambda e, t1=t1, pr_=pr_: e.tensor_tensor(out=t1[:, :], in0=pr_[:, 0, :], in1=pr_[:, 0, :], op=ALU.mult), [Bpr_], [Bt1])
            ew("dve", lambda e, t2=t2, pr_=pr_: e.tensor_tensor(out=t2[:, :], in0=pr_[:, 1, :], in1=pr_[:, 1, :], op=ALU.mult), [Bpr_], [Bt2])
            ew("dve", lambda e, den=den, t1=t1, t2=t2: e.tensor_tensor(out=den[:, :], in0=t1[:, :], in1=t2[:, :], op=ALU.add), [Bt1, Bt2], [Bden])
            ew("dve", lambda e, den=den: e.reciprocal(out=den[:, :], in_=den[:, :]), [Bden], [Bden])
            ew("dve", lambda e, t1=t1, lr=lr, pr_=pr_: e.tensor_tensor(out=t1[:, :], in0=lr[:, :], in1=pr_[:, 0, :], op=ALU.mult), [Blr, Bpr_], [Bt1])
            ew("dve", lambda e, t2=t2, li=li, pr_=pr_: e.tensor_tensor(out=t2[:, :], in0=li[:, :], in1=pr_[:, 1, :], op=ALU.mult), [Bli, Bpr_], [Bt2])
            ew("dve", lambda e, t1=t1, t2=t2: e.tensor_tensor(out=t1[:, :], in0=t1[:, :], in1=t2[:, :], op=ALU.add), [Bt1, Bt2], [Bt1])
            ew("dve", lambda e, fre=fre, t1=t1, den=den: e.tensor_tensor(out=fre[:, :], in0=t1[:, :], in1=den[:, :], op=ALU.mult), [Bt1, Bden], [Bfre])
            ew("dve", lambda e, t1=t1, li=li, pr_=pr_: e.tensor_tensor(out=t1[:, :], in0=li[:, :], in1=pr_[:, 0, :], op=ALU.mult), [Bli, Bpr_], [Bt1])
            ew("dve", lambda e, t2=t2, lr=lr, pr_=pr_: e.tensor_tensor(out=t2[:, :], in0=lr[:, :], in1=pr_[:, 1, :], op=ALU.mult), [Blr, Bpr_], [Bt2])
            ew("dve", lambda e, t1=t1, t2=t2: e.tensor_tensor(out=t1[:, :], in0=t1[:, :], in1=t2[:, :], op=ALU.subtract), [Bt1, Bt2], [Bt1])
            ew("dve", lambda e, fim=fim, t1=t1, den=den: e.tensor_tensor(out=fim[:, :], in0=t1[:, :], in1=den[:, :], op=ALU.mult), [Bt1, Bden], [Bfim])
            wre, Bwre = tmp([128, 128], BF16); wim, Bwim = tmp([128, 128], BF16)
            ew("dve", lambda e, t1=t1, fre=fre, bt=bt: e.tensor_tensor(out=t1[:, :], in0=fre[:, :], in1=bt[:, 0, :], op=ALU.mult), [Bfre, Bbt], [Bt1])
            ew("dve", lambda e, t2=t2, fim=fim, bt=bt: e.tensor_tensor(out=t2[:, :], in0=fim[:, :], in1=bt[:, 1, :], op=ALU.mult), [Bfim, Bbt], [Bt2])
            ew("dve", lambda e, wre=wre, t1=t1, t2=t2: e.tensor_tensor(out=wre[:, :], in0=t1[:, :], in1=t2[:, :], op=ALU.subtract), [Bt1, Bt2], [Bwre])
            ew("dve", lambda e, t1=t1, fre=fre, bt=bt: e.tensor_tensor(out=t1[:, :], in0=fre[:, :], in1=bt[:, 1, :], op=ALU.mult), [Bfre, Bbt], [Bt1])
            ew("dve", lambda e, t2=t2, fim=fim, bt=bt: e.tensor_tensor(out=t2[:, :], in0=fim[:, :], in1=bt[:, 0, :], op=ALU.mult), [Bfim, Bbt], [Bt2])
            ew("dve", lambda e, wim=wim, t1=t1, t2=t2: e.tensor_tensor(out=wim[:, :], in0=t1[:, :], in1=t2[:, :], op=ALU.add), [Bt1, Bt2], [Bwim])
            WBre.append((wre, Bwre)); WBim.append((wim, Bwim))
            ctf, Bctf = tmp([128, 2, 4, 128])
            P.dma("sp", lambda e, d=d, ctf=ctf: e.dma_start(out=ctf[:, :, :, :], in_=CT[d].rearrange("k p q n -> p k q n")), Bctf)
            cre, Bcre = tmp([128, 4, 128], BF16); cim, Bcim = tmp([128, 4, 128], BF16)
            ew("dve", lambda e, cre=cre, ctf=ctf: e.tensor_copy(out=cre[:, :, :], in_=ctf[:, 0, :, :]), [Bctf], [Bcre])
            ew("dve", lambda e, cim=cim, ctf=ctf: e.tensor_scalar(out=cim[:, :, :], in0=ctf[:, 1, :, :], scalar1=-1.0, scalar2=None, op0=ALU.mult), [Bctf], [Bcim])
            CTre.append((cre, Bcre)); CTimN.append((cim, Bcim))
            pc, Bpc = tmp([128, 3, 4])
            P.dma("sp", lambda e, d=d, pc=pc: e.dma_start(out=pc[:, :, :], in_=colp[d].rearrange("k p q -> p k q")), Bpc)
            dtc, Bdtc = tmp([128, 4]); a2, Ba2 = tmp([128, 4]); thc, Bthc = tmp([128, 4]); r_, Br_ = tmp([128, 4])
            ew("act", lambda e, dtc=dtc, pc=pc: e.activation(out=dtc[:, :], in_=pc[:, 2, :], func=AF.Exp), [Bpc], [Bdtc])
            ew("dve", lambda e, a2=a2, pc=pc, dtc=dtc: e.tensor_tensor(out=a2[:, :], in0=pc[:, 0, :], in1=dtc[:, :], op=ALU.mult), [Bpc, Bdtc], [Ba2])
            ew("dve", lambda e, thc=thc, pc=pc, dtc=dtc: e.tensor_tensor(out=thc[:, :], in0=pc[:, 1, :], in1=dtc[:, :], op=ALU.mult), [Bpc, Bdtc], [Bthc])
            ew("act", lambda e, r_=r_, a2=a2: e.activation(out=r_[:, :], in_=a2[:, :], func=AF.Exp), [Ba2], [Br_])
            rcol.append((r_, Br_))
            (cc, Bcc), (sc, Bsc) = cos_sin(thc, Bthc, 4)
            ec, Bec = tmp([128, 4, TT]); es, Bes = tmp([128, 4, TT]); tt, Btt = tmp([128, TT])
            for q in range(4):
                ew("dve", lambda e, ec=ec, cc=cc, q=q: e.tensor_copy(out=ec[:, q, 0:1], in_=cc[:, q:q + 1]), [Bcc], [Bec])
                ew("dve", lambda e, es=es, sc=sc, q=q: e.tensor_scalar(out=es[:, q, 0:1], in0=sc[:, q:q + 1], scalar1=-1.0, scalar2=None, op0=ALU.mult), [Bsc], [Bes])
                m = 1
                while m < TT:
                    ew("dve", lambda e, q=q, m=m, tt=tt, es=es, ec=ec: e.tensor_scalar(out=tt[:, 0:m], in0=es[:, q, 0:m], scalar1=es[:, q, m - 1:m], scalar2=None, op0=ALU.mult), [Bes], [Btt])
                    ew("dve", lambda e, q=q, m=m, tt=tt, es=es, ec=ec: e.scalar_tensor_tensor(out=ec[:, q, m:2 * m], in0=ec[:, q, 0:m], scalar=ec[:, q, m - 1:m], in1=tt[:, 0:m],
                                                                         op0=ALU.mult, op1=ALU.subtract), [Bec, Btt], [Bec])
                    ew("dve", lambda e, q=q, m=m, tt=tt, es=es, ec=ec: e.tensor_scalar(out=tt[:, 0:m], in0=es[:, q, 0:m], scalar1=ec[:, q, m - 1:m], scalar2=None, op0=ALU.mult), [Bes, Bec], [Btt])
                    ew("dve", lambda e, q=q, m=m, tt=tt, es=es, ec=ec: e.scalar_tensor_tensor(out=es[:, q, m:2 * m], in0=ec[:, q, 0:m], scalar=es[:, q, m - 1:m], in1=tt[:, 0:m],
                                                                         op0=ALU.mult, op1=ALU.add), [Bec, Bes, Btt], [Bes])
                    m *= 2
            Ec.append((ec, Bec)); Es.append((es, Bes))
        yf, Byf = cx.sb([128, NKEY], F32)
        u_ring = cx.ring(2, [128, TT], F32); ub_ring = cx.ring(2, [128, TT], BF16)
        ps_r = cx.ring(2, [128, 512], F32, psum=True); ps_i = cx.ring(2, [128, 512], F32, psum=True)
        ps_y = cx.ring(2, [128, 512], F32, psum=True)
        tr = {k: cx.ring(2, [128, TT], F32) for k in ("m1", "m2", "m3", "m4", "gpr", "gpi", "gr", "gi")}
        hb_r = cx.ring(2, [128, TT], BF16); hb_i = cx.ring(2, [128, TT], BF16)
        ostg = cx.ring(2, [128, TT], F32)
        tiles = [(0, 256)] + [(256 + 512 * i, 512) for i in range(16)]
        for b in range(2):
            for d in range(2):
                car_r, Bcr = cx.sb([128, 4], F32); car_i, Bci = cx.sb([128, 4], F32)
                P.op("pool", lambda e, car_r=car_r: e.memset(car_r[:, :], 0.0), writes=[Bcr])
                P.op("pool", lambda e, car_i=car_i: e.memset(car_i[:, :], 0.0), writes=[Bci])
                order = tiles if d == 0 else [tiles[0]] + tiles[:0:-1]
                (wre, Bwre), (wim, Bwim) = WBre[d], WBim[d]
                (cre, Bcre), (cim, Bcim) = CTre[d], CTimN[d]
                (ec, Bec), (es, Bes) = Ec[d], Es[d]
                (r_, Br_) = rcol[d]
                for (t0, n) in order:
                    uf, Buf_ = u_ring.next()
                    P.dma("sp", lambda e, uf=uf, b=b, t0=t0, n=n: e.dma_start(out=uf[:, 0:n], in_=uT[:, b, t0:t0 + n]), Buf_)
                    ub, Bub = ub_ring.next()
                    if d == 0:
                        P.op("act", lambda e, ub=ub, uf=uf, n=n: e.activation(out=ub[:, 0:n], in_=uf[:, 0:n], func=AF.Copy), reads=[Buf_], writes=[Bub])
                    else:
                        P.op("dve", lambda e, ub=ub, uf=uf, n=n: e.tensor_copy(out=ub[:, 0:n], in_=uf[:, n - 1::-1] if False else uf[:, 0:n][:, ::-1]),
                             reads=[Buf_], writes=[Bub])
                    py, Bpy = ps_y.next()
                    for q in range(4):
                        pr, Bpr = ps_r.next(); pi, Bpi = ps_i.next()
                        P.op("pe", lambda e, pr=pr, q=q, ub=ub, n=n, wre=wre: e.matmul(pr[:, 0:n], lhsT=wre[32 * q:32 * q + 32, :], rhs=ub[32 * q:32 * q + 32, 0:n],
                                                                                   start=True, stop=True, tile_position=(32 * q, 0)), reads=[Bwre, Bub], writes=[Bpr])
                        P.op("pe", lambda e, pi=pi, q=q, ub=ub, n=n, wim=wim: e.matmul(pi[:, 0:n], lhsT=wim[32 * q:32 * q + 32, :], rhs=ub[32 * q:32 * q + 32, 0:n],
                                                                                   start=True, stop=True, tile_position=(32 * q, 0)), reads=[Bwim, Bub], writes=[Bpi])
                        T = {k: tr[k].next() for k in tr}
                        def tt_(eng, o, a, bb, op, q=q, n=n):
                            (ot, Bo), (at, Ba), (bt_, Bb) = o, a, bb
                            P.op(eng, lambda e: e.tensor_tensor(out=ot[:, 0:n], in0=at[:, 0:n], in1=bt_, op=op), reads=[Ba, Bb], writes=[Bo])
                        ecq = (ec[:, q, 0:n], Bec); esq = (es[:, q, 0:n], Bes)
                        def tt2(eng, o, a, tab, op, n=n):
                            (ot, Bo), (at, Ba), (tb, Bt) = o, a, tab
                            P.op(eng, lambda e: e.tensor_tensor(out=ot[:, 0:n], in0=at[:, 0:n], in1=tb, op=op), reads=[Ba, Bt], writes=[Bo])
                        def tt3(eng, o, a, bsrc, op, n=n):
                            (ot, Bo), (at, Ba), (bt2, Bb2) = o, a, bsrc
                            P.op(eng, lambda e: e.tensor_tensor(out=ot[:, 0:n], in0=at[:, 0:n], in1=bt2[:, 0:n], op=op), reads=[Ba, Bb2], writes=[Bo])
                        tt2("dve", T["m1"], (pr, Bpr), ecq, ALU.mult)
                        tt2("dve", T["m2"], (pi, Bpi), esq, ALU.mult)
                        tt3("pool", T["gpr"], T["m1"], T["m2"], ALU.subtract)
                        tt2("dve", T["m3"], (pi, Bpi), ecq, ALU.mult)
                        tt2("dve", T["m4"], (pr, Bpr), esq, ALU.mult)
                        tt3("pool", T["gpi"], T["m3"], T["m4"], ALU.add)
                        for (go, gp, car, Bcar) in ((T["gr"], T["gpr"], car_r, Bcr), (T["gi"], T["gpi"], car_i, Bci)):
                            (got, Bgo), (gpt, Bgp) = go, gp
                            P.op("dve", lambda e, got=got, gpt=gpt, car=car, q=q, n=n, r_=r_: e.tensor_tensor_scan(
                                out=got[:, 0:n], data0=r_[:, q:q + 1].to_broadcast([128, n]), data1=gpt[:, 0:n], initial=car[:, q:q + 1],
                                op0=ALU.mult, op1=ALU.add), reads=[Br_, Bgp, Bcar], writes=[Bgo])
                        tt2("pool", T["m1"], T["gr"], ecq, ALU.mult)
                        tt2("pool", T["m2"], T["gi"], esq, ALU.mult)
                        tt2("pool", T["m3"], T["gi"], ecq, ALU.mult)
                        tt2("pool", T["m4"], T["gr"], esq, ALU.mult)
                        hr, Bhr = hb_r.next(); hi, Bhi = hb_i.next()
                        tt3("dve", (hr, Bhr), T["m1"], T["m2"], ALU.add)
                        tt3("dve", (hi, Bhi), T["m3"], T["m4"], ALU.subtract)
                        (m1, Bm1), (m2, Bm2), (m3, Bm3), (m4, Bm4) = T["m1"], T["m2"], T["m3"], T["m4"]
                        P.op("pool", lambda e, car_r=car_r, m1=m1, m2=m2, q=q, n=n: e.tensor_tensor(out=car_r[:, q:q + 1], in0=m1[:, n - 1:n], in1=m2[:, n - 1:n], op=ALU.add),
                             reads=[Bm1, Bm2], writes=[Bcr])
                        P.op("pool", lambda e, car_i=car_i, m3=m3, m4=m4, q=q, n=n: e.tensor_tensor(out=car_i[:, q:q + 1], in0=m3[:, n - 1:n], in1=m4[:, n - 1:n], op=ALU.subtract),
                             reads=[Bm3, Bm4], writes=[Bci])
                        P.op("pe", lambda e, py=py, cre=cre, hr=hr, q=q, n=n: e.matmul(py[:, 0:n], lhsT=cre[:, q, :], rhs=hr[:, 0:n], start=(q == 0), stop=False),
                             reads=[Bcre, Bhr], writes=[Bpy])
                        P.op("pe", lambda e, py=py, cim=cim, hi=hi, q=q, n=n: e.matmul(py[:, 0:n], lhsT=cim[:, q, :], rhs=hi[:, 0:n], start=False, stop=(q == 3)),
                             reads=[Bcim, Bhi], writes=[Bpy])
                    if d == 0:
                        P.op("act", lambda e, py=py, t0=t0, n=n: e.activation(out=yf[:, t0:t0 + n], in_=py[:, 0:n], func=AF.Copy), reads=[Bpy], writes=[Byf])
                    else:
                        o1, Bo1 = ostg.next()
                        P.op("dve", lambda e, o1=o1, uf=uf, t0=t0, n=n: e.scalar_tensor_tensor(out=o1[:, 0:n], in0=uf[:, 0:n], scalar=dv[:, 0:1], in1=yf[:, t0:t0 + n],
                                                                                            op0=ALU.mult, op1=ALU.add), reads=[Buf_, Bdv, Byf], writes=[Bo1])
                        P.op("dve", lambda e, o1=o1, py=py, n=n: e.tensor_tensor(out=o1[:, 0:n], in0=o1[:, 0:n], in1=py[:, 0:n][:, ::-1], op=ALU.add),
                             reads=[Bo1, Bpy], writes=[Bo1])
                        P.dma("sp", lambda e, o1=o1, b=b, t0=t0, n=n: e.dma_start(out=o_y[:, b, t0:t0 + n], in_=o1[:, 0:n]), Boy, reads=[Bo1])
        cx.finish([Boy])
    return nc, P


CONV_PIECES = [(i * SEG, SEG) for i in range(NSEG - 1)] + [((NSEG - 1) * SEG, NLAT - (NSEG - 1) * SEG), (NLAT, NCTX)]
CONV_WOFF = []
_o = 0
for (_s, _n) in CONV_PIECES:
    CONV_WOFF.append(_o); _o += _n + 30
WTOT = _o


def build_stage_c(final=False, debug_seg=None, nc=None, io=None, pool=None):
    nc = nc or bass.Bass("TRN2", target_bir_lowering=False)
    cx = Ctx(nc, io, pool); P = cx.P
    xT = cx.din("xT", [KC, 128, NT])
    attT = cx.din("attT", [8, 128, NT], BF16)
    ysT = cx.din("ysT", [8, 128, NT])
    ucH = cx.din("ucH", [8, 128, WTOT])
    gT = cx.din("gT", [48, 128, NT], BF16)
    w_glu = cx.din("w_glu", [1024, 1024]); w_ssm_o = cx.din("w_ssm_o", [1024, D]); w_conv_o = cx.din("w_conv_o", [1024, D])
    w_attn_o = cx.din("w_attn_o", [1024, D]); w_out = cx.din("w_out", [D, D]); w1 = cx.din("w_mlp1", [D, 4 * D]); w2 = cx.din("w_mlp2", [4 * D, D])
    vec8 = cx.din("vec8", [128, 8, 4])
    convw = cx.din("convw", [128, 8, 31])
    g2T = cx.din("g2T", [128, KC])
    fgT = cx.din("fgT", [128, KC])
    mod2 = cx.din("mod2", [128, KC, 2, 4])
    o_x, Box = cx.dout("xT_out", [KC, 128, NT])
    if debug_seg is not None:
        dbg = {n: cx.dout("dbg_" + n, [128, k, SEG], dt) for n, k, dt in (("cacc", 8, F32), ("cv", 8, BF16), ("gs", 8, BF16), ("hT", 16, BF16), ("g32", 8, F32))}
    wv = lambda w: w.rearrange("(kc p) n -> p kc n", p=128)
    with cx.st:
        ones_f, Bones = cx.sb([128, 128], F32)
        P.op("pool", lambda e: e.memset(ones_f[:, :], 1.0), writes=[Bones])
        eps_t, Beps = cx.sb([128, 1], F32)
        P.op("pool", lambda e: e.memset(eps_t[:, :], EPS), writes=[Beps])
        v8, Bv8 = cx.sb([128, 8, 4], F32); cw, Bcw = cx.sb([128, 8, 31], F32)
        g2, Bg2 = cx.sb([128, KC], F32); fg, Bfg = cx.sb([128, KC], F32); m2, Bm2 = cx.sb([128, KC, 2, 4], F32)
        sc2, Bsc2 = cx.sb([128, KC, 2, 2], F32)
        for (t, B_, src) in ((v8, Bv8, vec8), (cw, Bcw, convw), (g2, Bg2, g2T), (fg, Bfg, fgT), (m2, Bm2, mod2)):
            P.dma("sp", lambda e, t=t, src=src: e.dma_start(out=t[:], in_=src), B_)
        for var in range(2):
            P.op("dve", lambda e, var=var: e.scalar_tensor_tensor(out=sc2[:, :, var, 0], in0=m2[:, :, var, 2], scalar=1.0, in1=g2[:, :],
                                                                  op0=ALU.add, op1=ALU.mult), reads=[Bm2, Bg2], writes=[Bsc2])
            P.op("dve", lambda e, var=var: e.tensor_copy(out=sc2[:, :, var, 1], in_=m2[:, :, var, 1]), reads=[Bm2], writes=[Bsc2])
        if final:
            scf, Bscf = cx.sb([128, KC, 2, 2], F32)
            P.op("pool", lambda e: e.memset(scf[:, :, :, :], 0.0), writes=[Bscf])
            for var in range(2):
                P.op("dve", lambda e, var=var: e.tensor_copy(out=scf[:, :, var, 0], in_=fg[:, :]), reads=[Bfg, Bscf], writes=[Bscf])
        xs, Bxs = cx.sb([128, KC, SEG], F32)
        hT, BhT = cx.sb([128, KC, SEG], BF16)
        hh, Bhh = cx.sb([128, 64, SEG], BF16)
        ys, Bys = cx.sb([128, 8, SEG], F32)
        gb, Bgb = cx.sb([128, 8, SEG], BF16); gs, Bgs = cx.sb([128, 8, SEG], BF16)
        cv, Bcv = cx.sb([128, 8, SEG], BF16); at, Bat = cx.sb([128, 8, SEG], BF16)
        cacc, Bcacc = cx.sb([128, 8, SEG], F32)
        ucw_ring = cx.ring(2, [128, SEG + 30], F32)
        gate_ring = cx.ring(6, [128, SEG], BF16)
        w_ring = cx.ring(4, [128, 16, 256], BF16)
        tmp_ring = cx.ring(4, [128, SEG], F32)
        rstd_ring = cx.ring(2, [128, SEG], F32)
        ps_mm = cx.ring(6, [128, 512], F32, psum=True)
        ps_aux = cx.ring(2, [128, 512], F32, psum=True)
        ostg = cx.ring(2, [128, SEG], F32) if final else None

        WSPEC = [("w_glu", w_glu, 8, 8), ("w_attn_o", w_attn_o, 8, 16), ("w_ssm_o", w_ssm_o, 8, 16), ("w_conv_o", w_conv_o, 8, 16),
                 ("w_out", w_out, 16, 16), ("w_mlp1", w1, 16, 64), ("w_mlp2", w2, 64, 16)]
        wsc = {}; BW = {}
        for (wn, W, nK, n_oc) in WSPEC:
            npc = (nK + 15) // 16
            cx.n += 1
            wsc[wn] = nc.dram_tensor("wsc_%s_%d" % (wn, cx.n), [n_oc // 2, npc, 128, 16 * 256], BF16).ap()
            BW[wn] = Buf("wsc_" + wn)
            for oc2 in range(0, n_oc, 2):
                for k0 in range(0, nK, 16):
                    nk = min(16, nK - k0)
                    wt, Bw = w_ring.next()
                    P.dma("pool", lambda e, wt=wt, W=W, k0=k0, nk=nk, oc2=oc2: e.dma_start(
                        out=wt[:, 0:nk, :], in_=wv(W)[:, k0:k0 + nk, oc2 * 128:oc2 * 128 + 256]), Bw)
                    P.dma("sp", lambda e, wt=wt, wn=wn, oc2=oc2, k0=k0, nk=nk: e.dma_start(
                        out=wsc[wn][oc2 // 2, k0 // 16, :, 0:nk * 256], in_=wt[:, 0:nk, :].rearrange("p k c -> p (k c)")), BW[wn], reads=[Bw])
        WNAME = {id(w_glu): "w_glu", id(w_attn_o): "w_attn_o", id(w_ssm_o): "w_ssm_o", id(w_conv_o): "w_conv_o", id(w_out): "w_out", id(w1): "w_mlp1", id(w2): "w_mlp2"}

        def load_w(wt, Bw, W, oc2, k0, nk):
            wn = WNAME[id(W)]
            P.dma("sp", lambda e: e.dma_start(out=wt[:, 0:nk, :].rearrange("p k c -> p (k c)"), in_=wsc[wn][oc2 // 2, k0 // 16, :, 0:nk * 256]), Bw, reads=[BW[wn]])

        def mm_group(W, nK, n_oc, rhs_of, Brhs, epilogue):
            for oc2 in range(0, n_oc, 2):
                pss = [ps_mm.next(), ps_mm.next()]
                for k0 in range(0, nK, 16):
                    nk = min(16, nK - k0)
                    wt, Bw = w_ring.next()
                    load_w(wt, Bw, W, oc2, k0, nk)
                    for j in range(2):
                        pt, Bp = pss[j]
                        for kc in range(nk):
                            P.op("pe", lambda e, pt=pt, wt=wt, j=j, kc=kc, k0=k0: e.matmul(
                                pt[:, 0:SEG], lhsT=wt[:, kc, j * 128:(j + 1) * 128], rhs=rhs_of(k0 + kc),
                                start=(k0 + kc == 0), stop=(k0 + kc == nK - 1)), reads=[Bw, Brhs], writes=[Bp])
                for j in range(2):
                    epilogue(oc2 + j, pss[j][0], pss[j][1])

        for seg in (range(NSEG) if debug_seg is None else [debug_seg]):
            t0 = seg * SEG
            parts = [(0, NLAT - t0, 0), (NLAT - t0, SEG, 1)] if seg == NSEG - 1 else [(0, SEG, 0)]
            P.dma("sp", lambda e, t0=t0: e.dma_start(out=xs[:, :, :], in_=xT[:, :, t0:t0 + SEG].rearrange("k p t -> p k t")), Bxs)
            P.dma("sp", lambda e, t0=t0: e.dma_start(out=ys[:, :, :], in_=ysT[:, :, t0:t0 + SEG].rearrange("k p t -> p k t")), Bys)
            P.dma("sp", lambda e, t0=t0: e.dma_start(out=at[:, :, :], in_=attT[:, :, t0:t0 + SEG].rearrange("k p t -> p k t")), Bat)
            for c in range(8):
                t1, Bt1 = tmp_ring.next()
                P.op("pool", lambda e, t1=t1, c=c: e.tensor_tensor(out=t1[:, :], in0=ys[:, c, :], in1=ys[:, c, :], op=ALU.mult), reads=[Bys], writes=[Bt1])
                P.op("pool", lambda e, t1=t1: e.tensor_scalar(out=t1[:, :], in0=t1[:, :], scalar1=0.044715, scalar2=1.0, op0=ALU.mult, op1=ALU.add),
                     reads=[Bt1], writes=[Bt1])
                P.op("pool", lambda e, t1=t1, c=c: e.tensor_tensor(out=t1[:, :], in0=t1[:, :], in1=ys[:, c, :], op=ALU.mult), reads=[Bt1, Bys], writes=[Bt1])
                P.op("act", lambda e, t1=t1: e.activation(out=t1[:, :], in_=t1[:, :], func=AF.Sigmoid, scale=1.5957691216), reads=[Bt1], writes=[Bt1])
                P.op("dve", lambda e, t1=t1, c=c: e.tensor_tensor(out=ys[:, c, :], in0=ys[:, c, :], in1=t1[:, :], op=ALU.mult), reads=[Bys, Bt1], writes=[Bys])
                P.op("act", lambda e, c=c: e.activation(out=gb[:, c, :], in_=ys[:, c, :], func=AF.Copy), reads=[Bys], writes=[Bgb])
            def ep_glu(oc, pt, Bp):
                t1, Bt1 = tmp_ring.next()
                P.op("act", lambda e: e.activation(out=t1[:, :], in_=pt[:, 0:SEG], func=AF.Sigmoid, bias=v8[:, oc, 0:1]), reads=[Bp, Bv8], writes=[Bt1])
                P.op("dve", lambda e: e.tensor_tensor(out=gs[:, oc, :], in0=ys[:, oc, :], in1=t1[:, :], op=ALU.mult), reads=[Bys, Bt1], writes=[Bgs])
            mm_group(w_glu, 8, 8, lambda k: gb[:, k, :], Bgb, ep_glu)
            for c in range(8):
                for pi_, (ps0, pn) in enumerate(CONV_PIECES):
                    if not (t0 <= ps0 < t0 + SEG):
                        continue
                    o0 = ps0 - t0
                    win, Bwin = ucw_ring.next()
                    P.dma("sp", lambda e, win=win, c=c, pi_=pi_, pn=pn: e.dma_start(out=win[:, 0:pn + 30], in_=ucH[c, :, CONV_WOFF[pi_]:CONV_WOFF[pi_] + pn + 30]), Bwin)
                    a2, Ba2 = tmp_ring.next(); tm, Btm = tmp_ring.next()
                    P.op("dve", lambda e, win=win, c=c, o0=o0, pn=pn: e.tensor_scalar(out=cacc[:, c, o0:o0 + pn], in0=win[:, 0:pn], scalar1=cw[:, c, 0:1],
                                                                                     scalar2=v8[:, c, 1:2], op0=ALU.mult, op1=ALU.add), reads=[Bwin, Bcw, Bv8], writes=[Bcacc])
                    for k in range(1, 16):
                        P.op("dve", lambda e, win=win, c=c, o0=o0, pn=pn, k=k: e.scalar_tensor_tensor(
                            out=cacc[:, c, o0:o0 + pn], in0=win[:, k:k + pn], scalar=cw[:, c, k:k + 1], in1=cacc[:, c, o0:o0 + pn],
                            op0=ALU.mult, op1=ALU.add), reads=[Bwin, Bcw, Bcacc], writes=[Bcacc])
                    P.op("pool", lambda e, win=win, a2=a2, c=c, pn=pn: e.tensor_scalar(out=a2[:, 0:pn], in0=win[:, 16:16 + pn], scalar1=cw[:, c, 16:17], scalar2=None,
                                                                                      op0=ALU.mult), reads=[Bwin, Bcw], writes=[Ba2])
                    for k in range(17, 31):
                        P.op("pool", lambda e, win=win, tm=tm, c=c, pn=pn, k=k: e.tensor_scalar(out=tm[:, 0:pn], in0=win[:, k:k + pn], scalar1=cw[:, c, k:k + 1], scalar2=None,
                                                                                              op0=ALU.mult), reads=[Bwin, Bcw], writes=[Btm])
                        P.op("pool", lambda e, a2=a2, tm=tm, pn=pn: e.tensor_tensor(out=a2[:, 0:pn], in0=a2[:, 0:pn], in1=tm[:, 0:pn], op=ALU.add), reads=[Ba2, Btm], writes=[Ba2])
                    P.op("pool", lambda e, a2=a2, c=c, o0=o0, pn=pn: e.tensor_tensor(out=cacc[:, c, o0:o0 + pn], in0=cacc[:, c, o0:o0 + pn], in1=a2[:, 0:pn], op=ALU.add),
                         reads=[Bcacc, Ba2], writes=[Bcacc])
            p1, Bp1 = ps_aux.next(); p2, Bp2 = ps_aux.next()
            for c in range(8):
                P.op("pe", lambda e, c=c, p1=p1: e.matmul(p1[:, 0:SEG], lhsT=ones_f[:, :], rhs=cacc[:, c, :], start=(c == 0), stop=(c == 7)), reads=[Bones, Bcacc], writes=[Bp1])
            for c in range(8):
                sq, Bsq = tmp_ring.next()
                P.op("act", lambda e, sq=sq, c=c: e.activation(out=sq[:, :], in_=cacc[:, c, :], func=AF.Square), reads=[Bcacc], writes=[Bsq])
                P.op("pe", lambda e, sq=sq, c=c, p2=p2: e.matmul(p2[:, 0:SEG], lhsT=ones_f[:, :], rhs=sq[:, :], start=(c == 0), stop=(c == 7)), reads=[Bones, Bsq], writes=[Bp2])
            mean, Bmean = rstd_ring.next(); rstd, Brstd = rstd_ring.next(); msq, Bmsq = tmp_ring.next()
            P.op("dve", lambda e, mean=mean, p1=p1: e.tensor_scalar(out=mean[:, :], in0=p1[:, 0:SEG], scalar1=1.0 / 1024, scalar2=None, op0=ALU.mult), reads=[Bp1], writes=[Bmean])
            P.op("pool", lambda e, msq=msq, mean=mean: e.tensor_tensor(out=msq[:, :], in0=mean[:, :], in1=mean[:, :], op=ALU.mult), reads=[Bmean], writes=[Bmsq])
            P.op("dve", lambda e, rstd=rstd, p2=p2, msq=msq: e.scalar_tensor_tensor(out=rstd[:, :], in0=p2[:, 0:SEG], scalar=1.0 / 1024, in1=msq[:, :], op0=ALU.mult, op1=ALU.subtract),
                 reads=[Bp2, Bmsq], writes=[Brstd])
            P.op("act", lambda e, rstd=rstd: e.activation(out=rstd[:, :], in_=rstd[:, :], func=AF.Sqrt, bias=eps_t[:, 0:1]), reads=[Brstd, Beps], writes=[Brstd])
            P.op("dve", lambda e, rstd=rstd: e.reciprocal(out=rstd[:, :], in_=rstd[:, :]), reads=[Brstd], writes=[Brstd])
            for c in range(8):
                t1, Bt1 = tmp_ring.next()
                P.op("pool", lambda e, t1=t1, c=c, mean=mean: e.tensor_tensor(out=t1[:, :], in0=cacc[:, c, :], in1=mean[:, :], op=ALU.subtract), reads=[Bcacc, Bmean], writes=[Bt1])
                P.op("pool", lambda e, t1=t1, rstd=rstd: e.tensor_tensor(out=t1[:, :], in0=t1[:, :], in1=rstd[:, :], op=ALU.mult), reads=[Bt1, Brstd], writes=[Bt1])
                P.op("pool", lambda e, t1=t1, c=c: e.tensor_scalar(out=t1[:, :], in0=t1[:, :], scalar1=v8[:, c, 2:3], scalar2=v8[:, c, 3:4], op0=ALU.mult, op1=ALU.add),
                     reads=[Bt1, Bv8], writes=[Bt1])
                P.op("act", lambda e, t1=t1, c=c: e.activation(out=cv[:, c, :], in_=t1[:, :], func=AF.Silu), reads=[Bt1], writes=[Bcv])
            branches = [(w_attn_o, at, Bat, 0), (w_ssm_o, gs, Bgs, 16), (w_conv_o, cv, Bcv, 32)]
            for oc2 in range(0, 16, 2):
                wts = []
                for (W, src, Bsrc, goff) in branches:
                    wt, Bw = w_ring.next()
                    load_w(wt, Bw, W, oc2, 0, 8)
                    wts.append((wt, Bw))
                for j in range(2):
                    oc = oc2 + j
                    acc = None
                    for bi, (W, src, Bsrc, goff) in enumerate(branches):
                        wt, Bw = wts[bi]
                        pt, Bp = ps_mm.next()
                        for kc in range(8):
                            P.op("pe", lambda e, pt=pt, wt=wt, j=j, kc=kc, src=src: e.matmul(pt[:, 0:SEG], lhsT=wt[:, kc, j * 128:(j + 1) * 128], rhs=src[:, kc, :],
                                                                                            start=(kc == 0), stop=(kc == 7)), reads=[Bw, Bsrc], writes=[Bp])
                        gt_, Bgt = gate_ring.next()
                        P.dma("sp", lambda e, gt_=gt_, goff=goff, oc=oc, t0=t0: e.dma_start(out=gt_[:, :], in_=gT[goff + oc, :, t0:t0 + SEG]), Bgt)
                        t1, Bt1 = tmp_ring.next()
                        P.op("dve", lambda e, t1=t1, pt=pt, gt_=gt_: e.tensor_tensor(out=t1[:, :], in0=pt[:, 0:SEG], in1=gt_[:, :], op=ALU.mult), reads=[Bp, Bgt], writes=[Bt1])
                        if acc is None:
                            acc = (t1, Bt1)
                        elif bi == 1:
                            a_, Ba_ = acc
                            P.op("pool", lambda e, a_=a_, t1=t1: e.tensor_tensor(out=a_[:, :], in0=a_[:, :], in1=t1[:, :], op=ALU.add), reads=[Ba_, Bt1], writes=[Ba_])
                        else:
                            a_, Ba_ = acc
                            P.op("pool", lambda e, a_=a_, t1=t1, oc=oc: e.tensor_tensor(out=hT[:, oc, :], in0=a_[:, :], in1=t1[:, :], op=ALU.add), reads=[Ba_, Bt1], writes=[BhT])
            if debug_seg is not None:
                for n, (t_, B_) in (("cacc", (cacc, Bcacc)), ("cv", (cv, Bcv)), ("gs", (gs, Bgs)), ("hT", (hT, BhT)), ("g32", (ys, Bys))):
                    P.dma("sp", lambda e, n=n, t_=t_: e.dma_start(out=dbg[n][0][:, :, :], in_=t_[:, :, :]), dbg[n][1], reads=[B_])
            def ep_res(gidx):
                def ep(oc, pt, Bp):
                    for (a, b, var) in parts:
                        P.op("dve", lambda e, a=a, b=b, var=var: e.scalar_tensor_tensor(out=xs[:, oc, a:b], in0=pt[:, a:b], scalar=m2[:, oc, var, gidx:gidx + 1],
                                                                                       in1=xs[:, oc, a:b], op0=ALU.mult, op1=ALU.add), reads=[Bp, Bm2, Bxs], writes=[Bxs])
                return ep
            mm_group(w_out, 16, 16, lambda k: hT[:, k, :], BhT, ep_res(0))
            rmsnorm_mod_segment(cx, eps_t, xs, Bxs, ones_f, Bones, sc2, Bsc2, hT, BhT, seg, tmp_ring, ps_aux, rstd_ring, local=True)
            def ep_mlp1(oc, pt, Bp):
                t1, Bt1 = tmp_ring.next()
                P.op("act", lambda e: e.activation(out=t1[:, :], in_=pt[:, 0:SEG], func=AF.Relu), reads=[Bp], writes=[Bt1])
                P.op("pool", lambda e: e.tensor_tensor(out=hh[:, oc, :], in0=t1[:, :], in1=t1[:, :], op=ALU.mult), reads=[Bt1], writes=[Bhh])
            mm_group(w1, 16, 64, lambda k: hT[:, k, :], BhT, ep_mlp1)
            mm_group(w2, 64, 16, lambda k: hh[:, k, :], Bhh, ep_res(3))
            if not final:
                P.dma("sp", lambda e, t0=t0: e.dma_start(out=o_x[:, :, t0:t0 + SEG].rearrange("k p t -> p k t"), in_=xs[:, :, :]), Box, reads=[Bxs])
            else:
                pt, Bp = ps_aux.next()
                for kc in range(KC):
                    sq, Bsq = tmp_ring.next()
                    P.op("act", lambda e, sq=sq, kc=kc: e.activation(out=sq[:, :], in_=xs[:, kc, :], func=AF.Square), reads=[Bxs], writes=[Bsq])
                    P.op("pe", lambda e, sq=sq, kc=kc, pt=pt: e.matmul(pt[:, 0:SEG], lhsT=ones_f[:, :], rhs=sq[:, :], start=(kc == 0), stop=(kc == KC - 1)),
                         reads=[Bsq, Bones], writes=[Bp])
                rs, Brs = rstd_ring.next()
                P.op("act", lambda e, rs=rs, pt=pt: e.activation(out=rs[:, :], in_=pt[:, 0:SEG], func=AF.Sqrt, scale=1.0 / D, bias=eps_t[:, 0:1]), reads=[Bp, Beps], writes=[Brs])
                P.op("dve", lambda e, rs=rs: e.reciprocal(out=rs[:, :], in_=rs[:, :]), reads=[Brs], writes=[Brs])
                for kc in range(KC):
                    og, Bog_ = ostg.next()
                    P.op("dve", lambda e, og=og, kc=kc, rs=rs: e.scalar_tensor_tensor(out=og[:, :], in0=xs[:, kc, :], scalar=fg[:, kc:kc + 1], in1=rs[:, :],
                                                                                     op0=ALU.mult, op1=ALU.mult), reads=[Bxs, Bfg, Brs], writes=[Bog_])
                    P.dma("sp", lambda e, og=og, kc=kc, t0=t0: e.dma_start(out=o_x[kc, :, t0:t0 + SEG], in_=og[:, :]), Box, reads=[Bog_])
        cx.finish([Box] + ([dbg[n][1] for n in dbg] if debug_seg is not None else []))
    return nc, P


L = 8192; C = 256; B = 2


def core_tokens_T(x_lat, x_ctx, core):
    b, s = core // 4, core % 4
    t = np.concatenate([x_lat[b, s * NLAT:(s + 1) * NLAT], x_ctx[b, s * NCTX:(s + 1) * NCTX]], axis=0)
    return np.ascontiguousarray(t.T)


def rope_tables(core):
    s = core % 4
    t = np.arange(s * NLAT, (s + 1) * NLAT, dtype=np.float32)
    inv_freq = (10000.0 ** (-np.arange(0, 64, 2, dtype=np.float32) / 64)).astype(np.float32)
    d = np.arange(128)
    a = d // 64; i = d % 32; half = (d % 64) // 32
    pos = np.where(a[:, None] == 0, np.floor(t / 64)[None, :], np.mod(t, 64)[None, :]).astype(np.float32)
    ang = pos * inv_freq[i][:, None]
    cosT = np.ones((128, NT), np.float32); sinT = np.zeros((128, NT), np.float32)
    cosT[:, :NLAT] = np.cos(ang)
    sinT[:, :NLAT] = np.sin(ang) * np.where(half == 0, -1.0, 1.0)[:, None]
    return cosT, sinT


def perm_matrix():
    d = np.arange(128)
    partner = np.where((d % 64) < 32, d + 32, d - 32)
    Pm = np.zeros((128, 128), np.float32)
    Pm[partner, d] = 1.0
    return Pm


def colT(v, nchunks):
    return np.ascontiguousarray(v.reshape(nchunks, 128).T)


def ssm_params(d, l, chunk):
    rowp = np.zeros((2, 3, 128, 128), np.float32); colp = np.zeros((2, 3, 128, 4), np.float32)
    BT = np.zeros((2, 2, 128, 128), np.float32); CT = np.zeros((2, 2, 128, 4, 128), np.float32)
    for dr in range(2):
        srcs = [d["ssm_a_re"][l, dr], d["ssm_a_im"][l, dr], np.repeat(d["ssm_log_dt"][l, dr][:, None], 64, axis=1)]
        bs = [d["ssm_b_re"][l, dr], d["ssm_b_im"][l, dr]]
        cs = [d["ssm_c_re"][l, dr], d["ssm_c_im"][l, dr]]
        for q in range(4):
            gA = 8 * chunk + 2 * q; gB = gA + 1
            for k in range(3):
                row = np.concatenate([srcs[k][gA], srcs[k][gB]])
                rowp[dr, k, 32 * q:32 * q + 32, :] = row[None, :]
                colp[dr, k, :, q] = row
            for k in range(2):
                BT[dr, k, 32 * q:32 * q + 16, 0:64] = bs[k][gA].T
                BT[dr, k, 32 * q + 16:32 * q + 32, 64:128] = bs[k][gB].T
                CT[dr, k, 0:64, q, 32 * q:32 * q + 16] = cs[k][gA].T
                CT[dr, k, 64:128, q, 32 * q + 16:32 * q + 32] = cs[k][gB].T
    dvec = np.ascontiguousarray(d["ssm_d"][l][128 * chunk:128 * chunk + 128, None])
    return {"rowp": rowp, "colp": colp, "BT": BT, "CT": CT, "dvec": dvec}


def conv_windows(uc_lat, uc_ctx, core, pieces, wtot):
    b, s = core // 4, core % 4
    out = np.zeros((wtot, 1024), np.float32)
    o = 0
    for (ps0, pn) in pieces:
        if ps0 < NLAT:
            src = uc_lat[b]; g0 = s * NLAT + ps0
        else:
            src = uc_ctx[b]; g0 = s * NCTX + (ps0 - NLAT)
        lo, hi = g0 - 15, g0 + pn + 15
        a, e = max(lo, 0), min(hi, src.shape[0])
        out[o + (a - lo):o + (e - lo)] = src[a:e]
        o += pn + 30
    return np.ascontiguousarray(out.T).reshape(8, 128, wtot)


_PROGS = {}


def _prog(name):
    if name not in _PROGS:
        if name == "m":
            _PROGS[name] = build_stage_m()
        elif name == "a":
            _PROGS[name] = build_stage_a()[0]
        elif name == "b1":
            _PROGS[name] = build_stage_b1()[0]
        elif name == "b2":
            _PROGS[name] = build_stage_b2()[0]
        elif name == "c":
            _PROGS[name] = build_stage_c(False)[0]
        elif name == "cf":
            _PROGS[name] = build_stage_c(True)[0]
    return _PROGS[name]


def _run(name, in_maps):
    res = run_bass_kernel_spmd(_prog(name), in_maps, core_ids=list(range(8)))
    return [{k: np.asarray(v) for k, v in r.items()} for r in res.results]


def kernel(x, c, ctx, c_ctx, norm1_g, norm2_g, w_mod, b_mod, w_in, b_gate, q_norm_g, k_norm_g, w_attn_o,
           ssm_a_re, ssm_a_im, ssm_log_dt, ssm_b_re, ssm_b_im, ssm_c_re, ssm_c_im, ssm_d, w_glu, b_glu, w_ssm_o,
           conv_w, conv_b, conv_ln_g, conv_ln_b, w_conv_o, w_out, w_mlp1, w_mlp2, final_g):
    f32 = np.float32
    d = dict(ssm_a_re=np.asarray(ssm_a_re, f32), ssm_a_im=np.asarray(ssm_a_im, f32), ssm_log_dt=np.asarray(ssm_log_dt, f32),
             ssm_b_re=np.asarray(ssm_b_re, f32), ssm_b_im=np.asarray(ssm_b_im, f32), ssm_c_re=np.asarray(ssm_c_re, f32),
             ssm_c_im=np.asarray(ssm_c_im, f32), ssm_d=np.asarray(ssm_d, f32))
    x = np.asarray(x, f32); ctx = np.asarray(ctx, f32); w_mod = np.asarray(w_mod, f32); b_mod = np.asarray(b_mod, f32)
    DEPTH = 4
    c_all = np.concatenate([np.asarray(c, f32), np.asarray(c_ctx, f32)[None, :]], axis=0)
    cT = np.ascontiguousarray(c_all.reshape(3, 16, 128).transpose(2, 1, 0))
    in_maps = []
    for core in range(8):
        wm = np.empty((48, 128, 16, 128), f32); bm = np.empty((128, 48), f32)
        for u in range(48):
            l, j = divmod(core * 48 + u, 96)
            wm[u] = w_mod[l][:, j * 128:(j + 1) * 128].reshape(16, 128, 128).transpose(1, 0, 2)
            bm[:, u] = b_mod[l][j * 128:(j + 1) * 128]
        in_maps.append({"wm": wm, "cT": cT, "bm": bm})
    outs = _run("m", in_maps)
    mod = np.empty((DEPTH, 3, 12288), f32)
    for core in range(8):
        for u in range(48):
            l, j = divmod(core * 48 + u, 96)
            mod[l, :, j * 128:(j + 1) * 128] = outs[core]["modT"][:, u, :].T
    del in_maps
    xT = [core_tokens_T(x, ctx, core).reshape(16, 128, NT) for core in range(8)]
    Pm = perm_matrix()
    ropes = [rope_tables(core) for core in range(8)]
    for l in range(DEPTH):
        w_in_l = np.ascontiguousarray(np.asarray(w_in[l], f32))
        g1T = colT(np.asarray(norm1_g[l], f32), 16)
        qkg = np.stack([np.asarray(q_norm_g[l], f32), np.asarray(k_norm_g[l], f32)], axis=1)
        bgT = colT(np.asarray(b_gate[l], f32), 48)
        in_maps = []
        for core in range(8):
            b = core // 4
            modss = np.empty((128, 16, 2, 2), f32)
            for vi, v in enumerate((b, 2)):
                modss[:, :, vi, 0] = colT(mod[l, v, 0:2048], 16)
                modss[:, :, vi, 1] = colT(mod[l, v, 2048:4096], 16)
            in_maps.append({"xT": xT[core], "w_in": w_in_l, "g1T": g1T, "modss": modss, "qkg": qkg,
                            "cosT": ropes[core][0], "sinT": ropes[core][1], "permM": Pm, "bgT": bgT})
        oa = _run("a", in_maps)
        del in_maps, w_in_l
        kT_all = []; v_all = []
        for b in range(2):
            kk = np.empty((2, 128, NKEY), oa[0]["kT"].dtype); vv = np.empty((NKEY, 256), oa[0]["v"].dtype)
            for s in range(4):
                o = oa[4 * b + s]
                kk[:, :, s * NLAT:(s + 1) * NLAT] = o["kT"][:, :, :NLAT]; kk[:, :, L + s * NCTX:L + (s + 1) * NCTX] = o["kT"][:, :, NLAT:]
                vv[s * NLAT:(s + 1) * NLAT] = o["v"][:NLAT]; vv[L + s * NCTX:L + (s + 1) * NCTX] = o["v"][NLAT:]
            kT_all.append(kk); v_all.append(vv)
        ob1 = _run("b1", [{"qT": oa[core]["qT"], "kT": kT_all[core // 4], "v": v_all[core // 4]} for core in range(8)])
        del kT_all, v_all
        in_maps = []
        for k in range(8):
            m = ssm_params(d, l, k)
            uT = np.empty((128, 2, NKEY), f32)
            for b in range(2):
                for s in range(4):
                    o = oa[4 * b + s]["uT"][k]
                    uT[:, b, C + s * NLAT:C + (s + 1) * NLAT] = o[:, :NLAT]; uT[:, b, s * NCTX:(s + 1) * NCTX] = o[:, NLAT:]
            m["uT"] = uT
            in_maps.append(m)
        ob2 = _run("b2", in_maps)
        del in_maps
        uc_lat = np.empty((2, L, 1024), f32); uc_ctx = np.empty((2, C, 1024), f32)
        for core in range(8):
            b, s = core // 4, core % 4
            u2 = oa[core]["ucT"].reshape(1024, NT)
            uc_lat[b, s * NLAT:(s + 1) * NLAT] = u2[:, :NLAT].T; uc_ctx[b, s * NCTX:(s + 1) * NCTX] = u2[:, NLAT:].T
        Wl = {k: np.ascontiguousarray(np.asarray(v[l], f32)) for k, v in (("w_glu", w_glu), ("w_ssm_o", w_ssm_o), ("w_conv_o", w_conv_o),
                                                                          ("w_attn_o", w_attn_o), ("w_out", w_out), ("w_mlp1", w_mlp1), ("w_mlp2", w_mlp2))}
        vec8 = np.ascontiguousarray(np.stack([colT(np.asarray(t[l], f32), 8) for t in (b_glu, conv_b, conv_ln_g, conv_ln_b)], axis=2))
        convw = np.ascontiguousarray(np.asarray(conv_w[l], f32).T.reshape(8, 128, 31).transpose(1, 0, 2))
        g2T = colT(np.asarray(norm2_g[l], f32), 16); fgT = colT(np.asarray(final_g, f32), 16)
        in_maps = []
        for core in range(8):
            b, s = core // 4, core % 4
            m = dict(Wl)
            m["xT"] = xT[core]; m["attT"] = ob1[core]["attT"]; m["gT"] = oa[core]["gT"]
            ys = np.empty((8, 128, NT), f32)
            for k in range(8):
                ys[k, :, :NLAT] = ob2[k]["ysT"][:, b, C + s * NLAT:C + (s + 1) * NLAT]; ys[k, :, NLAT:] = ob2[k]["ysT"][:, b, s * NCTX:(s + 1) * NCTX]
            m["ysT"] = ys
            m["ucH"] = conv_windows(uc_lat, uc_ctx, core, CONV_PIECES, WTOT)
            m["vec8"] = vec8; m["convw"] = convw; m["g2T"] = g2T; m["fgT"] = fgT
            mod2 = np.empty((128, 16, 2, 4), f32)
            for vi, v in enumerate((b, 2)):
                for i, mi in enumerate((2, 3, 4, 5)):
                    mod2[:, :, vi, i] = colT(mod[l, v, mi * 2048:(mi + 1) * 2048], 16)
            m["mod2"] = mod2
            in_maps.append(m)
        oc = _run("cf" if l == DEPTH - 1 else "c", in_maps)
        del in_maps, oa, ob1, ob2
        xT = [oc[core]["xT_out"] for core in range(8)]
    out = np.empty((2, L, D), f32)
    for core in range(8):
        b, s = core // 4, core % 4
        out[b, s * NLAT:(s + 1) * NLAT] = xT[core].reshape(D, NT)[:, :NLAT].T
    return out
```

```python
import contextlib, math
import numpy as np
import concourse.bass as bass
import concourse.mybir as mybir
from concourse.bass_utils import run_bass_kernel_spmd

F32 = mybir.dt.float32
BF16 = mybir.dt.bfloat16
AF = mybir.ActivationFunctionType
ALU = mybir.AluOpType
AX = mybir.AxisListType

ENGS = ("pe", "act", "dve", "pool", "sp")
CONSERVATIVE = False


class Buf:
    __slots__ = ("name", "lw", "rd", "dsem", "dcnt", "lw_dma")

    def __init__(self, name):
        self.name = name
        self.lw = None
        self.rd = []
        self.dsem = None
        self.dcnt = 0
        self.lw_dma = False


class Prog:
    def __init__(self, nc):
        self.nc = nc
        self.ops = {e: [] for e in ENGS}
        self.cnt = {e: 0 for e in ENGS}
        self.waited = {e: {} for e in ENGS}
        self.nsem_dma = 0
        self.sems = {}
        self.final_waits = []
        self.dma_final = {}

    def _need(self, eng, ev, waits):
        if ev is None:
            return
        k, v = ev
        if self.waited[eng].get(k, 0) >= v:
            return
        self.waited[eng][k] = v
        waits[k] = max(waits.get(k, 0), v)

    def _deps(self, eng, reads, writes, dma_dst=None, is_dma=False):
        waits = {}
        for b in reads:
            self._need(eng, b.lw, waits)
        for b in writes:
            if dma_dst is b and b.lw_dma and not b.rd:
                pass
            elif b.lw is not None and (is_dma or b.lw[0] != eng or (CONSERVATIVE and eng != "pe")):
                self._need(eng, b.lw, waits)
            for ev in b.rd:
                if is_dma or ev[0] != eng or (CONSERVATIVE and eng != "pe"):
                    self._need(eng, ev, waits)
        return list(waits.items())

    def op(self, eng, fn, reads=(), writes=()):
        waits = self._deps(eng, reads, writes)
        self.cnt[eng] += 1
        ev = (eng, self.cnt[eng])
        self.ops[eng].append((waits, fn, (eng, 1)))
        for b in reads:
            b.rd.append(ev)
        for b in writes:
            b.lw = ev
            b.rd = []
            b.lw_dma = False
        return ev

    def dma(self, eng, fn, dst, reads=(), extra_writes=(), inc=16):
        waits = self._deps(eng, reads, (dst,) + tuple(extra_writes), dma_dst=dst, is_dma=True)
        if dst.dsem is None:
            dst.dsem = "d%d" % self.nsem_dma
            self.nsem_dma += 1
        dst.dcnt += inc
        ev = (dst.dsem, dst.dcnt)
        self.dma_final[dst.dsem] = dst.dcnt
        self.ops[eng].append((waits, fn, (dst.dsem, inc)))
        for b in reads:
            b.rd.append(ev)
        for b in (dst,) + tuple(extra_writes):
            b.lw = ev
            b.rd = []
            b.lw_dma = True
        return ev

    def wait_final(self, eng, bufs):
        waits = {}
        for b in bufs:
            self._need(eng, b.lw, waits)
        self.ops[eng].append((list(waits.items()), None, None))

    def barrier(self):
        evs = [(e, self.cnt[e]) for e in ENGS if self.cnt[e] > 0] + list(self.dma_final.items())
        for eng in ENGS:
            waits = {}
            for ev in evs:
                if ev[0] != eng:
                    self._need(eng, ev, waits)
            self.ops[eng].append((list(waits.items()), None, None))

    def emit_pooled(self, pool):
        nc = self.nc
        assert self.nsem_dma <= len(pool["dma"]), (self.nsem_dma, len(pool["dma"]))
        hs = {}; base = {}
        for e in ENGS:
            hs[e], base[e] = pool["eng"][e]
        for i in range(self.nsem_dma):
            hs["d%d" % i], base["d%d" % i] = pool["dma"][i]
        with nc.Block() as block:
            def runner(ename):
                def run(eng):
                    for waits, fn, inc in self.ops[ename]:
                        for k, v in waits:
                            eng.wait_ge(hs[k], base[k] + v)
                        if fn is not None:
                            ins = fn(eng)
                            ins.then_inc(hs[inc[0]], inc[1])
                return run
            block.tensor(runner("pe")); block.scalar(runner("act")); block.vector(runner("dve"))
            block.gpsimd(runner("pool")); block.sync(runner("sp"))
        for e in ENGS:
            pool["eng"][e][1] += self.cnt[e]
        for i in range(self.nsem_dma):
            pool["dma"][i][1] += self.dma_final.get("d%d" % i, 0)

    def emit(self):
        nc = self.nc
        import contextlib
        with contextlib.ExitStack() as st:
            for e in ENGS:
                self.sems[e] = st.enter_context(nc.semaphore("s_" + e))
            for i in range(self.nsem_dma):
                self.sems["d%d" % i] = st.enter_context(nc.semaphore("sd%d" % i))
            block = st.enter_context(nc.Block())
            sems = self.sems

            def runner(ename):
                def run(eng):
                    for waits, fn, inc in self.ops[ename]:
                        for k, v in waits:
                            eng.wait_ge(sems[k], v)
                        if fn is not None:
                            ins = fn(eng)
                            ins.then_inc(sems[inc[0]], inc[1])
                return run

            block.tensor(runner("pe"))
            block.scalar(runner("act"))
            block.vector(runner("dve"))
            block.gpsimd(runner("pool"))
            block.sync(runner("sp"))

    def stats(self):
        return {e: len(self.ops[e]) for e in ENGS}


D = 2048; NT = 2112; NLAT = 2048; NCTX = 64; SEG = 352; NSEG = 6
KC = 16
IN_COLS = 10752
EPS = 1e-6


class Ctx:
    _uid = [0]

    def __init__(self, nc, io=None, pool=None):
        self.nc = nc; self.P = Prog(nc); self.st = contextlib.ExitStack(); self.io = io; self.pool = pool
        Ctx._uid[0] += 1
        self.n = Ctx._uid[0] * 100000

    def sb(self, shape, dt, name=None):
        self.n += 1
        t = self.st.enter_context(self.nc.sbuf_tensor(name or ("t%d" % self.n), list(shape), dt))
        return t, Buf(name or ("t%d" % self.n))

    def ps(self, shape, dt=F32, name=None):
        self.n += 1
        t = self.st.enter_context(self.nc.psum_tensor(name or ("p%d" % self.n), list(shape), dt))
        return t, Buf(name or ("p%d" % self.n))

    def ring(self, n, shape, dt, psum=False):
        return Ring([self.ps(shape, dt) if psum else self.sb(shape, dt) for _ in range(n)])

    def din(self, name, shape, dt=F32):
        if self.io is not None:
            ap = self.io[name]
            assert list(ap.shape) == list(shape), (name, ap.shape, shape)
            return ap
        return self.nc.dram_tensor(name, list(shape), dt, kind="ExternalInput").ap()

    def dout(self, name, shape, dt=F32):
        if self.io is not None:
            ap = self.io[name]
            assert list(ap.shape) == list(shape), (name, ap.shape, shape)
            return ap, Buf(name)
        return self.nc.dram_tensor(name, list(shape), dt, kind="ExternalOutput").ap(), Buf(name)

    def finish(self, out_bufs):
        if self.pool is None:
            self.P.wait_final("sp", out_bufs)
            self.P.emit()
        else:
            self.P.barrier()
            self.P.emit_pooled(self.pool)


class Ring:
    def __init__(self, items):
        self.items = items; self.i = 0

    def next(self):
        it = self.items[self.i % len(self.items)]; self.i += 1
        return it


def make_ident(cx, dt=BF16):
    P = cx.P
    idf, Bidf = cx.sb([128, 128], F32)
    P.op("pool", lambda e: e.memset(idf[:, :], 0.0), writes=[Bidf])
    P.op("pool", lambda e: e.affine_select(out=idf[:, :], in_=idf[:, :], pattern=[[-1, 128]],
                                           compare_op=ALU.not_equal, fill=1.0, base=0, channel_multiplier=1),
         reads=[Bidf], writes=[Bidf])
    if dt == F32:
        return idf, Bidf
    idb, Bidb = cx.sb([128, 128], dt)
    P.op("dve", lambda e: e.tensor_copy(out=idb[:, :], in_=idf[:, :]), reads=[Bidf], writes=[Bidb])
    return idb, Bidb


def build_stage_m(nunits=48, nvar=3, nc=None, io=None, pool=None):
    nc = nc or bass.Bass("TRN2", target_bir_lowering=False)
    cx = Ctx(nc, io, pool); P = cx.P
    wm = cx.din("wm", [nunits, 128, KC, 128])
    cT = cx.din("cT", [128, KC, nvar])
    bm = cx.din("bm", [128, nunits])
    out, Bout = cx.dout("modT", [128, nunits, nvar])
    with cx.st:
        cs, Bcs = cx.sb([128, KC, nvar], F32)
        cb, Bcb = cx.sb([128, KC, nvar], BF16)
        bms, Bbms = cx.sb([128, nunits], F32)
        res, Bres = cx.sb([128, nunits, nvar], F32)
        wr = cx.ring(3, [128, KC, 128], BF16)
        pr = cx.ring(2, [128, 4], F32, psum=True)
        P.dma("sp", lambda e: e.dma_start(out=cs[:, :, :], in_=cT[:, :, :]), Bcs)
        P.dma("sp", lambda e: e.dma_start(out=bms[:, :], in_=bm[:, :]), Bbms)
        P.op("act", lambda e: e.activation(out=cb[:, :, :], in_=cs[:, :, :], func=AF.Silu), reads=[Bcs], writes=[Bcb])
        for u in range(nunits):
            wt, Bw = wr.next()
            P.dma("pool", lambda e, wt=wt, u=u: e.dma_start(out=wt[:, :, :], in_=wm[u, :, :, :]), Bw)
            pt, Bp = pr.next()
            for kc in range(KC):
                P.op("pe", lambda e, wt=wt, pt=pt, kc=kc: e.matmul(pt[:, 0:nvar], lhsT=wt[:, kc, :], rhs=cb[:, kc, :],
                                                                   start=(kc == 0), stop=(kc == KC - 1)),
                     reads=[Bw, Bcb], writes=[Bp])
            P.op("dve", lambda e, pt=pt, u=u: e.tensor_scalar(out=res[:, u, :], in0=pt[:, 0:nvar], scalar1=bms[:, u:u + 1],
                                                              scalar2=None, op0=ALU.add),
                 reads=[Bp, Bbms], writes=[Bres])
        P.dma("sp", lambda e: e.dma_start(out=out[:, :, :], in_=res[:, :, :]), Bout, reads=[Bres])
        cx.finish([Bout])
    return nc


def rmsnorm_mod_segment(cx, eps_t, xs, Bxs, ones_f, Bones, sc_all, Bsc, hT, BhT, seg, tmp_ring, ps_ring, rstd_ring, local=False, engs=("dve", "pool")):
    P = cx.P
    pt, Bp = ps_ring.next()
    for kc in range(KC):
        sq, Bsq = tmp_ring.next()
        P.op("act", lambda e, sq=sq, kc=kc: e.activation(out=sq[:, :], in_=xs[:, kc, :], func=AF.Square),
             reads=[Bxs], writes=[Bsq])
        P.op("pe", lambda e, sq=sq, pt=pt, kc=kc: e.matmul(pt[:, 0:SEG], lhsT=ones_f[:, :], rhs=sq[:, :],
                                                           start=(kc == 0), stop=(kc == KC - 1)),
             reads=[Bsq, Bones], writes=[Bp])
    rstd, Br = rstd_ring.next()
    P.op("act", lambda e: e.activation(out=rstd[:, :], in_=pt[:, 0:SEG], func=AF.Sqrt, scale=1.0 / D, bias=eps_t[:, 0:1]),
         reads=[Bp], writes=[Br])
    P.op("dve", lambda e: e.reciprocal(out=rstd[:, :], in_=rstd[:, :]), reads=[Br], writes=[Br])
    t0 = seg * SEG
    ho = 0 if local else t0
    if seg == NSEG - 1:
        parts = [(0, NLAT - t0, 0), (NLAT - t0, SEG, 1)]
    else:
        parts = [(0, SEG, 0)]
    for kc in range(KC):
        tm, Bt = tmp_ring.next()
        eng = engs[kc % len(engs)]
        P.op(eng, lambda e, tm=tm, kc=kc: e.tensor_tensor(out=tm[:, :], in0=xs[:, kc, :], in1=rstd[:, :], op=ALU.mult),
             reads=[Bxs, Br], writes=[Bt])
        for (a, b, var) in parts:
            P.op(eng, lambda e, tm=tm, kc=kc, a=a, b=b, var=var: e.tensor_scalar(
                out=hT[:, kc, ho + a:ho + b], in0=tm[:, a:b], scalar1=sc_all[:, kc, var, 0:1], scalar2=sc_all[:, kc, var, 1:2],
                op0=ALU.mult, op1=ALU.add), reads=[Bt, Bsc], writes=[BhT])


def build_stage_a(nc=None, io=None, pool=None):
    nc = nc or bass.Bass("TRN2", target_bir_lowering=False)
    cx = Ctx(nc, io, pool); P = cx.P
    xT = cx.din("xT", [KC, 128, NT])
    w_in = cx.din("w_in", [D, IN_COLS])
    gT = cx.din("g1T", [128, KC])
    modss = cx.din("modss", [128, KC, 2, 2])
    qkg = cx.din("qkg", [128, 2])
    cosT = cx.din("cosT", [128, NT]); sinT = cx.din("sinT", [128, NT])
    permM = cx.din("permM", [128, 128])
    bgT = cx.din("bgT", [128, 48])
    o_q, Boq = cx.dout("qT", [8, 128, NT], BF16)
    o_k, Bok = cx.dout("kT", [2, 128, NT], BF16)
    o_v, Bov = cx.dout("v", [NT, 256], BF16)
    o_u, Bou = cx.dout("uT", [8, 128, NT], F32)
    o_uc, Bouc = cx.dout("ucT", [8, 128, NT], F32)
    o_g, Bog = cx.dout("gT", [48, 128, NT], BF16)
    w_v = w_in.rearrange("(kc p) n -> p kc n", p=128)
    with cx.st:
        ones_f, Bones = cx.sb([128, 128], F32)
        P.op("pool", lambda e: e.memset(ones_f[:, :], 1.0), writes=[Bones])
        eps_t, Beps = cx.sb([128, 1], F32)
        P.op("pool", lambda e: e.memset(eps_t[:, :], EPS), writes=[Beps])
        g1, Bg1 = cx.sb([128, KC], F32)
        ms, Bms = cx.sb([128, KC, 2, 2], F32)
        sc_all, Bsc = cx.sb([128, KC, 2, 2], F32)
        qk, Bqk = cx.sb([128, 2], F32)
        cs, Bcs = cx.sb([128, NT], F32); sn, Bsn = cx.sb([128, NT], F32)
        pm, Bpm = cx.sb([128, 128], F32)
        bg, Bbg = cx.sb([128, 48], F32)
        hT, BhT = cx.sb([128, KC, NT], BF16)
        P.dma("sp", lambda e: e.dma_start(out=g1[:, :], in_=gT[:, :]), Bg1)
        P.dma("sp", lambda e: e.dma_start(out=ms[:, :, :, :], in_=modss[:, :, :, :]), Bms)
        P.dma("sp", lambda e: e.dma_start(out=qk[:, :], in_=qkg[:, :]), Bqk)
        P.dma("sp", lambda e: e.dma_start(out=cs[:, :], in_=cosT[:, :]), Bcs)
        P.dma("sp", lambda e: e.dma_start(out=sn[:, :], in_=sinT[:, :]), Bsn)
        P.dma("sp", lambda e: e.dma_start(out=pm[:, :], in_=permM[:, :]), Bpm)
        P.dma("sp", lambda e: e.dma_start(out=bg[:, :], in_=bgT[:, :]), Bbg)
        for var in range(2):
            P.op("dve", lambda e, var=var: e.scalar_tensor_tensor(out=sc_all[:, :, var, 0], in0=ms[:, :, var, 1], scalar=1.0,
                                                                  in1=g1[:, :], op0=ALU.add, op1=ALU.mult),
                 reads=[Bms, Bg1], writes=[Bsc])
            P.op("dve", lambda e, var=var: e.tensor_copy(out=sc_all[:, :, var, 1], in_=ms[:, :, var, 0]),
                 reads=[Bms], writes=[Bsc])
        xr = cx.ring(1, [128, KC, SEG], F32)
        tmp_ring = cx.ring(4, [128, SEG], F32)
        rstd_ring = cx.ring(2, [128, SEG], F32)
        ps_aux = cx.ring(2, [128, 512], F32, psum=True)
        for seg in range(NSEG):
            xs, Bxs = xr.next()
            P.dma("sp", lambda e, xs=xs, seg=seg: e.dma_start(
                out=xs[:, :, :], in_=xT[:, :, seg * SEG:(seg + 1) * SEG].rearrange("k p t -> p k t")), Bxs)
            rmsnorm_mod_segment(cx, eps_t, xs, Bxs, ones_f, Bones, sc_all, Bsc, hT, BhT, seg, tmp_ring, ps_aux, rstd_ring)
        wr = cx.ring(2, [128, KC, 256], BF16)
        ps_mm = cx.ring(4, [128, 512], F32, psum=True)
        stg_b = cx.ring(3, [128, SEG], BF16)
        stg_f = cx.ring(3, [128, SEG], F32)
        sig_ring = cx.ring(2, [128, NT], F32)
        groups = [0, 2, 4, 6, 8, 10, 12, 14, 16, 18]
        for i in (0, 2, 4, 6):
            groups += [28 + i, 20 + i]
        groups += list(range(36, 84, 2))
        sig_tiles = {}
        for g0 in groups:
            wt, Bw = wr.next()
            P.dma("pool", lambda e, wt=wt, g0=g0: e.dma_start(out=wt[:, :, :], in_=w_v[:, :, g0 * 128:g0 * 128 + 256]), Bw)
            for j in range(2):
                ct = g0 + j
                if ct in (10, 11):
                    continue
                if 28 <= ct < 36:
                    sg, Bsg = sig_ring.next(); sig_tiles[ct - 8] = (sg, Bsg)
                for seg in range(NSEG):
                    t0 = seg * SEG
                    pt, Bp = ps_mm.next()
                    for kc in range(KC):
                        P.op("pe", lambda e, pt=pt, wt=wt, j=j, kc=kc, t0=t0: e.matmul(
                            pt[:, 0:SEG], lhsT=wt[:, kc, j * 128:(j + 1) * 128], rhs=hT[:, kc, t0:t0 + SEG],
                            start=(kc == 0), stop=(kc == KC - 1)), reads=[Bw, BhT], writes=[Bp])
                    if ct < 10:
                        gi = 0 if ct < 8 else 1
                        sq, Bsq = tmp_ring.next()
                        P.op("act", lambda e, sq=sq, pt=pt: e.activation(out=sq[:, :], in_=pt[:, 0:SEG], func=AF.Square),
                             reads=[Bp], writes=[Bsq])
                        pa, Bpa = ps_aux.next()
                        P.op("pe", lambda e, pa=pa, sq=sq: e.matmul(pa[:, 0:SEG], lhsT=ones_f[:, :], rhs=sq[:, :],
                                                                    start=True, stop=True), reads=[Bsq, Bones], writes=[Bpa])
                        rstd, Br = rstd_ring.next()
                        P.op("act", lambda e, rstd=rstd, pa=pa: e.activation(out=rstd[:, :], in_=pa[:, 0:SEG], func=AF.Sqrt,
                                                                             scale=1.0 / 128, bias=eps_t[:, 0:1]), reads=[Bpa], writes=[Br])
                        P.op("dve", lambda e, rstd=rstd: e.reciprocal(out=rstd[:, :], in_=rstd[:, :]), reads=[Br], writes=[Br])
                        xn, Bxn = tmp_ring.next()
                        P.op("dve", lambda e, xn=xn, pt=pt, rstd=rstd, gi=gi: e.scalar_tensor_tensor(
                            out=xn[:, :], in0=pt[:, 0:SEG], scalar=qk[:, gi:gi + 1], in1=rstd[:, :], op0=ALU.mult, op1=ALU.mult),
                            reads=[Bp, Bqk, Br], writes=[Bxn])
                        pa2, Bpa2 = ps_aux.next()
                        P.op("pe", lambda e, pa2=pa2, xn=xn: e.matmul(pa2[:, 0:SEG], lhsT=pm[:, :], rhs=xn[:, :],
                                                                      start=True, stop=True), reads=[Bxn, Bpm], writes=[Bpa2])
                        t1, Bt1 = tmp_ring.next()
                        P.op("pool", lambda e, t1=t1, xn=xn, t0=t0: e.tensor_tensor(out=t1[:, :], in0=xn[:, :], in1=cs[:, t0:t0 + SEG],
                                                                                   op=ALU.mult), reads=[Bxn, Bcs], writes=[Bt1])
                        t2, Bt2 = tmp_ring.next()
                        P.op("dve", lambda e, t2=t2, pa2=pa2, t0=t0: e.tensor_tensor(out=t2[:, :], in0=pa2[:, 0:SEG], in1=sn[:, t0:t0 + SEG],
                                                                                    op=ALU.mult), reads=[Bpa2, Bsn], writes=[Bt2])
                        ob, Bob = stg_b.next()
                        P.op("pool", lambda e, ob=ob, t1=t1, t2=t2: e.tensor_tensor(out=ob[:, :], in0=t1[:, :], in1=t2[:, :], op=ALU.add),
                             reads=[Bt1, Bt2], writes=[Bob])
                        if ct < 8:
                            P.dma("sp", lambda e, ob=ob, ct=ct, t0=t0: e.dma_start(out=o_q[ct, :, t0:t0 + SEG], in_=ob[:, :]), Boq, reads=[Bob])
                        else:
                            P.dma("sp", lambda e, ob=ob, ct=ct, t0=t0: e.dma_start(out=o_k[ct - 8, :, t0:t0 + SEG], in_=ob[:, :]), Bok, reads=[Bob])
                    elif ct < 20:
                        of, Bof = stg_f.next()
                        P.op("act", lambda e, of=of, pt=pt: e.activation(out=of[:, :], in_=pt[:, 0:SEG], func=AF.Copy),
                             reads=[Bp], writes=[Bof])
                        P.dma("sp", lambda e, of=of, ct=ct, t0=t0: e.dma_start(out=o_u[ct - 12, :, t0:t0 + SEG], in_=of[:, :]), Bou, reads=[Bof])
                    elif ct < 28:
                        sg, Bsg = sig_tiles[ct]
                        of, Bof = stg_f.next()
                        P.op("dve", lambda e, of=of, pt=pt, sg=sg, t0=t0: e.tensor_tensor(out=of[:, :], in0=pt[:, 0:SEG], in1=sg[:, t0:t0 + SEG],
                                                                                         op=ALU.mult), reads=[Bp, Bsg], writes=[Bof])
                        P.dma("sp", lambda e, of=of, ct=ct, t0=t0: e.dma_start(out=o_uc[ct - 20, :, t0:t0 + SEG], in_=of[:, :]), Bouc, reads=[Bof])
                    elif ct < 36:
                        P.op("act", lambda e, sg=sg, pt=pt, t0=t0: e.activation(out=sg[:, t0:t0 + SEG], in_=pt[:, 0:SEG], func=AF.Sigmoid),
                             reads=[Bp], writes=[Bsg])
                    else:
                        ob, Bob = stg_b.next()
                        P.op("act", lambda e, ob=ob, pt=pt, ct=ct: e.activation(out=ob[:, :], in_=pt[:, 0:SEG], func=AF.Sigmoid,
                                                                               bias=bg[:, ct - 36:ct - 35]), reads=[Bp, Bbg], writes=[Bob])
                        P.dma("sp", lambda e, ob=ob, ct=ct, t0=t0: e.dma_start(out=o_g[ct - 36, :, t0:t0 + SEG], in_=ob[:, :]), Bog, reads=[Bob])
            if g0 == 10:
                vstg = cx.ring(2, [128, 256], BF16)
                for tb in range(17):
                    n = 128 if tb < 16 else 64
                    pt, Bp = ps_mm.next()
                    for kc in range(KC):
                        P.op("pe", lambda e, pt=pt, wt=wt, kc=kc, tb=tb, n=n: e.matmul(
                            pt[0:n, 0:256], lhsT=hT[:, kc, tb * 128:tb * 128 + n], rhs=wt[:, kc, 0:256],
                            start=(kc == 0), stop=(kc == KC - 1)), reads=[Bw, BhT], writes=[Bp])
                    vs, Bvs = vstg.next()
                    P.op("act", lambda e, vs=vs, pt=pt, n=n: e.activation(out=vs[0:n, :], in_=pt[0:n, 0:256], func=AF.Copy),
                         reads=[Bp], writes=[Bvs])
                    P.dma("sp", lambda e, vs=vs, tb=tb, n=n: e.dma_start(out=o_v[tb * 128:tb * 128 + n, :], in_=vs[0:n, :]), Bov, reads=[Bvs])
        cx.finish([Boq, Bok, Bov, Bou, Bouc, Bog])
    return nc, P


NKEY = 8448; NKT = 66

def build_stage_b1(nc=None, io=None, pool=None):
    nc = nc or bass.Bass("TRN2", target_bir_lowering=False)
    cx = Ctx(nc, io, pool); P = cx.P
    qT = cx.din("qT", [8, 128, NT], BF16)
    kT = cx.din("kT", [2, 128, NKEY], BF16)
    vv = cx.din("v", [NKEY, 256], BF16)
    o_a, Boa = cx.dout("attT", [8, 128, NT], BF16)
    scale = 1.0 / math.sqrt(128.0)
    with cx.st:
        qs, Bqs = cx.sb([128, 8, NT], BF16)
        ks, Bks = cx.sb([128, 2, NKEY], BF16)
        vs, Bvs = cx.sb([128, NKT, 256], BF16)
        ones_b, Bones = cx.sb([128, 128], BF16)
        P.op("pool", lambda e: e.memset(ones_b[:, :], 1.0), writes=[Bones])
        P.dma("sp", lambda e: e.dma_start(out=qs[:, :, :], in_=qT.rearrange("h p t -> p h t")), Bqs)
        for h in range(2):
            P.dma("sp", lambda e, h=h: e.dma_start(out=ks[:, h, :], in_=kT[h, :, :]), Bks)
        P.dma("sp", lambda e: e.dma_start(out=vs[:, :, :], in_=vv.rearrange("(kt p) c -> p kt c", p=128)), Bvs)
        ps_s = cx.ring(4, [128, 512], F32, psum=True)
        ps_o = cx.ring(2, [128, 512], F32, psum=True)
        ps_d = cx.ring(2, [128, 512], F32, psum=True)
        pT = cx.ring(4, [128, 512], BF16)
        rc = cx.ring(2, [128, 512], F32)
        ob = cx.ring(2, [128, 512], BF16)
        qtiles = [(i * 512, 512, list(range(NKT))) for i in range(4)] + [(NLAT, NCTX, [64, 65])]
        for h in range(8):
            kv = h // 4
            for (q0, ql, kts) in qtiles:
                po, Bpo = ps_o.next(); pd, Bpd = ps_d.next()
                SK = 2
                stiles = {}

                def issue_qk(i):
                    kt = kts[i]
                    pss, Bps = ps_s.next()
                    P.op("pe", lambda e, pss=pss, kt=kt, ql=ql, kv=kv, h=h, q0=q0: e.matmul(
                        pss[:, 0:ql], lhsT=ks[:, kv, kt * 128:(kt + 1) * 128], rhs=qs[:, h, q0:q0 + ql], start=True, stop=True),
                        reads=[Bks, Bqs], writes=[Bps])
                    stiles[i] = (pss, Bps)

                for i in range(min(SK, len(kts))):
                    issue_qk(i)
                for i, kt in enumerate(kts):
                    if i + SK < len(kts):
                        issue_qk(i + SK)
                    pss, Bps = stiles.pop(i)
                    pt, Bpt = pT.next()
                    P.op("act", lambda e, pt=pt, pss=pss, ql=ql: e.activation(out=pt[:, 0:ql], in_=pss[:, 0:ql], func=AF.Exp, scale=scale),
                         reads=[Bps], writes=[Bpt])
                    first = (i == 0); last = (i == len(kts) - 1)
                    P.op("pe", lambda e, pt=pt, kt=kt, first=first, last=last, po=po, ql=ql, kv=kv: e.matmul(
                        po[:, 0:ql], lhsT=vs[:, kt, kv * 128:(kv + 1) * 128], rhs=pt[:, 0:ql], start=first, stop=last),
                        reads=[Bvs, Bpt], writes=[Bpo])
                    P.op("pe", lambda e, pt=pt, first=first, last=last, pd=pd, ql=ql: e.matmul(
                        pd[:, 0:ql], lhsT=ones_b[:, :], rhs=pt[:, 0:ql], start=first, stop=last),
                        reads=[Bones, Bpt], writes=[Bpd])
                r, Br = rc.next()
                P.op("dve", lambda e, r=r, pd=pd, ql=ql: e.reciprocal(out=r[:, 0:ql], in_=pd[:, 0:ql]), reads=[Bpd], writes=[Br])
                o, Bo = ob.next()
                P.op("dve", lambda e, o=o, po=po, r=r, ql=ql: e.tensor_tensor(out=o[:, 0:ql], in0=po[:, 0:ql], in1=r[:, 0:ql], op=ALU.mult),
                     reads=[Bpo, Br], writes=[Bo])
                P.dma("sp", lambda e, o=o, h=h, q0=q0, ql=ql: e.dma_start(out=o_a[h, :, q0:q0 + ql], in_=o[:, 0:ql]), Boa, reads=[Bo])
        cx.finish([Boa])
    return nc, P


TT = 512

def build_stage_b2(nc=None, io=None, pool=None):
    nc = nc or bass.Bass("TRN2", target_bir_lowering=False)
    cx = Ctx(nc, io, pool); P = cx.P
    uT = cx.din("uT", [128, 2, NKEY])
    rowp = cx.din("rowp", [2, 3, 128, 128])
    colp = cx.din("colp", [2, 3, 128, 4])
    BT = cx.din("BT", [2, 2, 128, 128])
    CT = cx.din("CT", [2, 2, 128, 4, 128])
    dvec = cx.din("dvec", [128, 1])
    o_y, Boy = cx.dout("ysT", [128, 2, NKEY])
    with cx.st:
        halfpi, Bhp = cx.sb([128, 1], F32)
        P.op("pool", lambda e: e.memset(halfpi[:, :], math.pi / 2), writes=[Bhp])
        dv, Bdv = cx.sb([128, 1], F32)
        P.dma("sp", lambda e: e.dma_start(out=dv[:, :], in_=dvec[:, :]), Bdv)
        cnt = [0]
        def tmp(shape, dt=F32):
            return cx.sb(shape, dt)

        def ew(eng, fn, reads, writes):
            P.op(eng, fn, reads=reads, writes=writes)

        def cos_sin(theta, Bth, W):
            s, Bs = tmp([128, W]); c, Bc = tmp([128, W]); t1, Bt1 = tmp([128, W]); t2, Bt2 = tmp([128, W])
            ew("act", lambda e: e.activation(out=s[:, :], in_=theta[:, :], func=AF.Sin, scale=1.0 / 16), [Bth], [Bs])
            ew("act", lambda e: e.activation(out=c[:, :], in_=theta[:, :], func=AF.Sin, scale=1.0 / 16, bias=halfpi[:, 0:1]), [Bth, Bhp], [Bc])
            for _ in range(4):
                ew("dve", lambda e: e.tensor_tensor(out=t1[:, :], in0=c[:, :], in1=c[:, :], op=ALU.mult), [Bc], [Bt1])
                ew("dve", lambda e: e.tensor_tensor(out=t2[:, :], in0=s[:, :], in1=s[:, :], op=ALU.mult), [Bs], [Bt2])
                ew("dve", lambda e: e.scalar_tensor_tensor(out=s[:, :], in0=s[:, :], scalar=2.0, in1=c[:, :], op0=ALU.mult, op1=ALU.mult),
                   [Bs, Bc], [Bs])
                ew("dve", lambda e: e.tensor_tensor(out=c[:, :], in0=t1[:, :], in1=t2[:, :], op=ALU.subtract), [Bt1, Bt2], [Bc])
            return (c, Bc), (s, Bs)

        WBre = []; WBim = []; CTre = []; CTimN = []; Ec = []; Es = []; rcol = []
        for d in range(2):
            pr_, Bpr_ = tmp([128, 3, 128])
            P.dma("sp", lambda e, d=d, pr_=pr_: e.dma_start(out=pr_[:, :, :], in_=rowp[d].rearrange("k p n -> p k n")), Bpr_)
            bt, Bbt = tmp([128, 2, 128])
            P.dma("sp", lambda e, d=d, bt=bt: e.dma_start(out=bt[:, :, :], in_=BT[d].rearrange("k p n -> p k n")), Bbt)
            dt_, Bdt = tmp([128, 128]); ard, Bard = tmp([128, 128]); th, Bth = tmp([128, 128]); mag, Bmag = tmp([128, 128])
            ew("act", lambda e, dt_=dt_, pr_=pr_: e.activation(out=dt_[:, :], in_=pr_[:, 2, :], func=AF.Exp), [Bpr_], [Bdt])
            ew("dve", lambda e, ard=ard, pr_=pr_, dt_=dt_: e.tensor_tensor(out=ard[:, :], in0=pr_[:, 0, :], in1=dt_[:, :], op=ALU.mult), [Bpr_, Bdt], [Bard])
            ew("dve", lambda e, th=th, pr_=pr_, dt_=dt_: e.tensor_tensor(out=th[:, :], in0=pr_[:, 1, :], in1=dt_[:, :], op=ALU.mult), [Bpr_, Bdt], [Bth])
            ew("act", lambda e, mag=mag, ard=ard: e.activation(out=mag[:, :], in_=ard[:, :], func=AF.Exp), [Bard], [Bmag])
            (c, Bc), (s, Bs) = cos_sin(th, Bth, 128)
            lr, Blr = tmp([128, 128]); li, Bli = tmp([128, 128]); den, Bden = tmp([128, 128]); t1, Bt1 = tmp([128, 128]); t2, Bt2 = tmp([128, 128])
            fre, Bfre = tmp([128, 128]); fim, Bfim = tmp([128, 128])
            ew("dve", lambda e, lr=lr, mag=mag, c=c: e.tensor_tensor(out=lr[:, :], in0=mag[:, :], in1=c[:, :], op=ALU.mult), [Bmag, Bc], [Blr])
            ew("dve", lambda e, li=li, mag=mag, s=s: e.tensor_tensor(out=li[:, :], in0=mag[:, :], in1=s[:, :], op=ALU.mult), [Bmag, Bs], [Bli])
            ew("dve", lambda e, lr=lr: e.tensor_scalar(out=lr[:, :], in0=lr[:, :], scalar1=-1.0, scalar2=None, op0=ALU.add), [Blr], [Blr])
            ew("dve", lambda e, t1=t1, pr_=pr_: e.tensor_tensor(out=t1[:, :], in0=pr_[:, 0, :], in1=pr_[:, 0, :], op=ALU.mult), [Bpr_], [Bt1])
            ew("dve", lambda e, t2=t2, pr_=pr_: e.tensor_tensor(out=t2[:, :], in0=pr_[:, 1, :], in1=pr_[:, 1, :], op=ALU.mult), [Bpr_], [Bt2])
            ew("dve", lambda e, den=den, t1=t1, t2=t2: e.tensor_tensor(out=den[:, :], in0=t1[:, :], in1=t2[:, :], op=ALU.add), [Bt1, Bt2], [Bden])
            ew("dve", lambda e, den=den: e.reciprocal(out=den[:, :], in_=den[:, :]), [Bden], [Bden])
            ew("dve", lambda e, t1=t1, lr=lr, pr_=pr_: e.tensor_tensor(out=t1[:, :], in0=lr[:, :], in1=pr_[:, 0, :], op=ALU.mult), [Blr, Bpr_], [Bt1])
            ew("dve", lambda e, t2=t2, li=li, pr_=pr_: e.tensor_tensor(out=t2[:, :], in0=li[:, :], in1=pr_[:, 1, :], op=ALU.mult), [Bli, Bpr_], [Bt2])
            ew("dve", lambda e, t1=t1, t2=t2: e.tensor_tensor(out=t1[:, :], in0=t1[:, :], in1=t2[:, :], op=ALU.add), [Bt1, Bt2], [Bt1])
            ew("dve", lambda e, fre=fre, t1=t1, den=den: e.tensor_tensor(out=fre[:, :], in0=t1[:, :], in1=den[:, :], op=ALU.mult), [Bt1, Bden], [Bfre])
            ew("dve", lambda e, t1=t1, li=li, pr_=pr_: e.tensor_tensor(out=t1[:, :], in0=li[:, :], in1=pr_[:, 0, :], op=ALU.mult), [Bli, Bpr_], [Bt1])
            ew("dve", lambda e, t2=t2, lr=lr, pr_=pr_: e.tensor_tensor(out=t2[:, :], in0=lr[:, :], in1=pr_[:, 1, :], op=ALU.mult), [Blr, Bpr_], [Bt2])
            ew("dve", lambda e, t1=t1, t2=t2: e.tensor_tensor(out=t1[:, :], in0=t1[:, :], in1=t2[:, :], op=ALU.subtract), [Bt1, Bt2], [Bt1])
            ew("dve", lambda e, fim=fim, t1=t1, den=den: e.tensor_tensor(out=fim[:, :], in0=t1[:, :], in1=den[:, :], op=ALU.mult), [Bt1, Bden], [Bfim])
            wre, Bwre = tmp([128, 128], BF16); wim, Bwim = tmp([128, 128], BF16)
            ew("dve", lambda e, t1=t1, fre=fre, bt=bt: e.tensor_tensor(out=t1[:, :], in0=fre[:, :], in1=bt[:, 0, :], op=ALU.mult), [Bfre, Bbt], [Bt1])
            ew("dve", lambda e, t2=t2, fim=fim, bt=bt: e.tensor_tensor(out=t2[:, :], in0=fim[:, :], in1=bt[:, 1, :], op=ALU.mult), [Bfim, Bbt], [Bt2])
            ew("dve", lambda e, wre=wre, t1=t1, t2=t2: e.tensor_tensor(out=wre[:, :], in0=t1[:, :], in1=t2[:, :], op=ALU.subtract), [Bt1, Bt2], [Bwre])
            ew("dve", lambda e, t1=t1, fre=fre, bt=bt: e.tensor_tensor(out=t1[:, :], in0=fre[:, :], in1=bt[:, 1, :], op=ALU.mult), [Bfre, Bbt], [Bt1])
            ew("dve", lambda e, t2=t2, fim=fim, bt=bt: e.tensor_tensor(out=t2[:, :], in0=fim[:, :], in1=bt[:, 0, :], op=ALU.mult), [Bfim, Bbt], [Bt2])
            ew("dve", lambda e, wim=wim, t1=t1, t2=t2: e.tensor_tensor(out=wim[:, :], in0=t1[:, :], in1=t2[:, :], op=ALU.add), [Bt1, Bt2], [Bwim])
            WBre.append((wre, Bwre)); WBim.append((wim, Bwim))
            ctf, Bctf = tmp([128, 2, 4, 128])
            P.dma("sp", lambda e, d=d, ctf=ctf: e.dma_start(out=ctf[:, :, :, :], in_=CT[d].rearrange("k p q n -> p k q n")), Bctf)
            cre, Bcre = tmp([128, 4, 128], BF16); cim, Bcim = tmp([128, 4, 128], BF16)
            ew("dve", lambda e, cre=cre, ctf=ctf: e.tensor_copy(out=cre[:, :, :], in_=ctf[:, 0, :, :]), [Bctf], [Bcre])
            ew("dve", lambda e, cim=cim, ctf=ctf: e.tensor_scalar(out=cim[:, :, :], in0=ctf[:, 1, :, :], scalar1=-1.0, scalar2=None, op0=ALU.mult), [Bctf], [Bcim])
            CTre.append((cre, Bcre)); CTimN.append((cim, Bcim))
            pc, Bpc = tmp([128, 3, 4])
            P.dma("sp", lambda e, d=d, pc=pc: e.dma_start(out=pc[:, :, :], in_=colp[d].rearrange("k p q -> p k q")), Bpc)
            dtc, Bdtc = tmp([128, 4]); a2, Ba2 = tmp([128, 4]); thc, Bthc = tmp([128, 4]); r_, Br_ = tmp([128, 4])
            ew("act", lambda e, dtc=dtc, pc=pc: e.activation(out=dtc[:, :], in_=pc[:, 2, :], func=AF.Exp), [Bpc], [Bdtc])
            ew("dve", lambda e, a2=a2, pc=pc, dtc=dtc: e.tensor_tensor(out=a2[:, :], in0=pc[:, 0, :], in1=dtc[:, :], op=ALU.mult), [Bpc, Bdtc], [Ba2])
            ew("dve", lambda e, thc=thc, pc=pc, dtc=dtc: e.tensor_tensor(out=thc[:, :], in0=pc[:, 1, :], in1=dtc[:, :], op=ALU.mult), [Bpc, Bdtc], [Bthc])
            ew("act", lambda e, r_=r_, a2=a2: e.activation(out=r_[:, :], in_=a2[:, :], func=AF.Exp), [Ba2], [Br_])
            rcol.append((r_, Br_))
            (cc, Bcc), (sc, Bsc) = cos_sin(thc, Bthc, 4)
            ec, Bec = tmp([128, 4, TT]); es, Bes = tmp([128, 4, TT]); tt, Btt = tmp([128, TT])
            for q in range(4):
                ew("dve", lambda e, ec=ec, cc=cc, q=q: e.tensor_copy(out=ec[:, q, 0:1], in_=cc[:, q:q + 1]), [Bcc], [Bec])
                ew("dve", lambda e, es=es, sc=sc, q=q: e.tensor_scalar(out=es[:, q, 0:1], in0=sc[:, q:q + 1], scalar1=-1.0, scalar2=None, op0=ALU.mult), [Bsc], [Bes])
                m = 1
                while m < TT:
                    ew("dve", lambda e, q=q, m=m, tt=tt, es=es, ec=ec: e.tensor_scalar(out=tt[:, 0:m], in0=es[:, q, 0:m], scalar1=es[:, q, m - 1:m], scalar2=None, op0=ALU.mult), [Bes], [Btt])
                    ew("dve", lambda e, q=q, m=m, tt=tt, es=es, ec=ec: e.scalar_tensor_tensor(out=ec[:, q, m:2 * m], in0=ec[:, q, 0:m], scalar=ec[:, q, m - 1:m], in1=tt[:, 0:m],
                                                                         op0=ALU.mult, op1=ALU.subtract), [Bec, Btt], [Bec])
                    ew("dve", lambda e, q=q, m=m, tt=tt, es=es, ec=ec: e.tensor_scalar(out=tt[:, 0:m], in0=es[:, q, 0:m], scalar1=ec[:, q, m - 1:m], scalar2=None, op0=ALU.mult), [Bes, Bec], [Btt])
                    ew("dve", lambda e, q=q, m=m, tt=tt, es=es, ec=ec: e.scalar_tensor_tensor(out=es[:, q, m:2 * m], in0=ec[:, q, 0:m], scalar=es[:, q, m - 1:m], in1=tt[:, 0:m],
                                                                         op0=ALU.mult, op1=ALU.add), [Bec, Bes, Btt], [Bes])
                    m *= 2
            Ec.append((ec, Bec)); Es.append((es, Bes))
        yf, Byf = cx.sb([128, NKEY], F32)
        u_ring = cx.ring(2, [128, TT], F32); ub_ring = cx.ring(2, [128, TT], BF16)
        ps_r = cx.ring(2, [128, 512], F32, psum=True); ps_i = cx.ring(2, [128, 512], F32, psum=True)
        ps_y = cx.ring(2, [128, 512], F32, psum=True)
        tr = {k: cx.ring(2, [128, TT], F32) for k in ("m1", "m2", "m3", "m4", "gpr", "gpi", "gr", "gi")}
        hb_r = cx.ring(2, [128, TT], BF16); hb_i = cx.ring(2, [128, TT], BF16)
        ostg = cx.ring(2, [128, TT], F32)
        tiles = [(0, 256)] + [(256 + 512 * i, 512) for i in range(16)]
        for b in range(2):
            for d in range(2):
                car_r, Bcr = cx.sb([128, 4], F32); car_i, Bci = cx.sb([128, 4], F32)
                P.op("pool", lambda e, car_r=car_r: e.memset(car_r[:, :], 0.0), writes=[Bcr])
                P.op("pool", lambda e, car_i=car_i: e.memset(car_i[:, :], 0.0), writes=[Bci])
                order = tiles if d == 0 else [tiles[0]] + tiles[:0:-1]
                (wre, Bwre), (wim, Bwim) = WBre[d], WBim[d]
                (cre, Bcre), (cim, Bcim) = CTre[d], CTimN[d]
                (ec, Bec), (es, Bes) = Ec[d], Es[d]
                (r_, Br_) = rcol[d]
                for (t0, n) in order:
                    uf, Buf_ = u_ring.next()
                    P.dma("sp", lambda e, uf=uf, b=b, t0=t0, n=n: e.dma_start(out=uf[:, 0:n], in_=uT[:, b, t0:t0 + n]), Buf_)
                    ub, Bub = ub_ring.next()
                    if d == 0:
                        P.op("act", lambda e, ub=ub, uf=uf, n=n: e.activation(out=ub[:, 0:n], in_=uf[:, 0:n], func=AF.Copy), reads=[Buf_], writes=[Bub])
                    else:
                        P.op("dve", lambda e, ub=ub, uf=uf, n=n: e.tensor_copy(out=ub[:, 0:n], in_=uf[:, n - 1::-1] if False else uf[:, 0:n][:, ::-1]),
                             reads=[Buf_], writes=[Bub])
                    py, Bpy = ps_y.next()
                    for q in range(4):
                        pr, Bpr = ps_r.next(); pi, Bpi = ps_i.next()
                        P.op("pe", lambda e, pr=pr, q=q, ub=ub, n=n, wre=wre: e.matmul(pr[:, 0:n], lhsT=wre[32 * q:32 * q + 32, :], rhs=ub[32 * q:32 * q + 32, 0:n],
                                                                                   start=True, stop=True, tile_position=(32 * q, 0)), reads=[Bwre, Bub], writes=[Bpr])
                        P.op("pe", lambda e, pi=pi, q=q, ub=ub, n=n, wim=wim: e.matmul(pi[:, 0:n], lhsT=wim[32 * q:32 * q + 32, :], rhs=ub[32 * q:32 * q + 32, 0:n],
                                                                                   start=True, stop=True, tile_position=(32 * q, 0)), reads=[Bwim, Bub], writes=[Bpi])
                        T = {k: tr[k].next() for k in tr}
                        def tt_(eng, o, a, bb, op, q=q, n=n):
                            (ot, Bo), (at, Ba), (bt_, Bb) = o, a, bb
                            P.op(eng, lambda e: e.tensor_tensor(out=ot[:, 0:n], in0=at[:, 0:n], in1=bt_, op=op), reads=[Ba, Bb], writes=[Bo])
                        ecq = (ec[:, q, 0:n], Bec); esq = (es[:, q, 0:n], Bes)
                        def tt2(eng, o, a, tab, op, n=n):
                            (ot, Bo), (at, Ba), (tb, Bt) = o, a, tab
                            P.op(eng, lambda e: e.tensor_tensor(out=ot[:, 0:n], in0=at[:, 0:n], in1=tb, op=op), reads=[Ba, Bt], writes=[Bo])
                        def tt3(eng, o, a, bsrc, op, n=n):
                            (ot, Bo), (at, Ba), (bt2, Bb2) = o, a, bsrc
                            P.op(eng, lambda e: e.tensor_tensor(out=ot[:, 0:n], in0=at[:, 0:n], in1=bt2[:, 0:n], op=op), reads=[Ba, Bb2], writes=[Bo])
                        tt2("dve", T["m1"], (pr, Bpr), ecq, ALU.mult)
                        tt2("dve", T["m2"], (pi, Bpi), esq, ALU.mult)
                        tt3("pool", T["gpr"], T["m1"], T["m2"], ALU.subtract)
                        tt2("dve", T["m3"], (pi, Bpi), ecq, ALU.mult)
                        tt2("dve", T["m4"], (pr, Bpr), esq, ALU.mult)
                        tt3("dve", T["gpi"], T["m3"], T["m4"], ALU.add)
                        for (go, gp, car, Bcar) in ((T["gr"], T["gpr"], car_r, Bcr), (T["gi"], T["gpi"], car_i, Bci)):
                            (got, Bgo), (gpt, Bgp) = go, gp
                            P.op("dve", lambda e, got=got, gpt=gpt, car=car, q=q, n=n, r_=r_: e.tensor_tensor_scan(
                                out=got[:, 0:n], data0=r_[:, q:q + 1].to_broadcast([128, n]), data1=gpt[:, 0:n], initial=car[:, q:q + 1],
                                op0=ALU.mult, op1=ALU.add), reads=[Br_, Bgp, Bcar], writes=[Bgo])
                        tt2("pool", T["m1"], T["gr"], ecq, ALU.mult)
                        tt2("dve", T["m2"], T["gi"], esq, ALU.mult)
                        tt2("pool", T["m3"], T["gi"], ecq, ALU.mult)
                        tt2("dve", T["m4"], T["gr"], esq, ALU.mult)
                        hr, Bhr = hb_r.next(); hi, Bhi = hb_i.next()
                        tt3("dve", (hr, Bhr), T["m1"], T["m2"], ALU.add)
                        tt3("dve", (hi, Bhi), T["m3"], T["m4"], ALU.subtract)
                        (m1, Bm1), (m2, Bm2), (m3, Bm3), (m4, Bm4) = T["m1"], T["m2"], T["m3"], T["m4"]
                        P.op("pool", lambda e, car_r=car_r, m1=m1, m2=m2, q=q, n=n: e.tensor_tensor(out=car_r[:, q:q + 1], in0=m1[:, n - 1:n], in1=m2[:, n - 1:n], op=ALU.add),
                             reads=[Bm1, Bm2], writes=[Bcr])
                        P.op("pool", lambda e, car_i=car_i, m3=m3, m4=m4, q=q, n=n: e.tensor_tensor(out=car_i[:, q:q + 1], in0=m3[:, n - 1:n], in1=m4[:, n - 1:n], op=ALU.subtract),
                             reads=[Bm3, Bm4], writes=[Bci])
                        P.op("pe", lambda e, py=py, cre=cre, hr=hr, q=q, n=n: e.matmul(py[:, 0:n], lhsT=cre[:, q, :], rhs=hr[:, 0:n], start=(q == 0), stop=False),
                             reads=[Bcre, Bhr], writes=[Bpy])
                        P.op("pe", lambda e, py=py, cim=cim, hi=hi, q=q, n=n: e.matmul(py[:, 0:n], lhsT=cim[:, q, :], rhs=hi[:, 0:n], start=False, stop=(q == 3)),
                             reads=[Bcim, Bhi], writes=[Bpy])
                    if d == 0:
                        P.op("act", lambda e, py=py, t0=t0, n=n: e.activation(out=yf[:, t0:t0 + n], in_=py[:, 0:n], func=AF.Copy), reads=[Bpy], writes=[Byf])
                    else:
                        o1, Bo1 = ostg.next()
                        P.op("dve", lambda e, o1=o1, uf=uf, t0=t0, n=n: e.scalar_tensor_tensor(out=o1[:, 0:n], in0=uf[:, 0:n], scalar=dv[:, 0:1], in1=yf[:, t0:t0 + n],
                                                                                            op0=ALU.mult, op1=ALU.add), reads=[Buf_, Bdv, Byf], writes=[Bo1])
                        P.op("dve", lambda e, o1=o1, py=py, n=n: e.tensor_tensor(out=o1[:, 0:n], in0=o1[:, 0:n], in1=py[:, 0:n][:, ::-1], op=ALU.add),
                             reads=[Bo1, Bpy], writes=[Bo1])
                        P.dma("sp", lambda e, o1=o1, b=b, t0=t0, n=n: e.dma_start(out=o_y[:, b, t0:t0 + n], in_=o1[:, 0:n]), Boy, reads=[Bo1])
        cx.finish([Boy])
    return nc, P


CONV_PIECES = [(i * SEG, SEG) for i in range(NSEG - 1)] + [((NSEG - 1) * SEG, NLAT - (NSEG - 1) * SEG), (NLAT, NCTX)]
CONV_WOFF = []
_o = 0
for (_s, _n) in CONV_PIECES:
    CONV_WOFF.append(_o); _o += _n + 30
WTOT = _o


def build_stage_c(final=False, debug_seg=None, nc=None, io=None, pool=None):
    nc = nc or bass.Bass("TRN2", target_bir_lowering=False)
    cx = Ctx(nc, io, pool); P = cx.P
    xT = cx.din("xT", [KC, 128, NT])
    attT = cx.din("attT", [8, 128, NT], BF16)
    ysT = cx.din("ysT", [8, 128, NT])
    ucH = cx.din("ucH", [8, 128, WTOT])
    gT = cx.din("gT", [48, 128, NT], BF16)
    w_glu = cx.din("w_glu", [1024, 1024]); w_ssm_o = cx.din("w_ssm_o", [1024, D]); w_conv_o = cx.din("w_conv_o", [1024, D])
    w_attn_o = cx.din("w_attn_o", [1024, D]); w_out = cx.din("w_out", [D, D]); w1 = cx.din("w_mlp1", [D, 4 * D]); w2 = cx.din("w_mlp2", [4 * D, D])
    vec8 = cx.din("vec8", [128, 8, 4])
    convw = cx.din("convw", [128, 8, 31])
    g2T = cx.din("g2T", [128, KC])
    fgT = cx.din("fgT", [128, KC])
    mod2 = cx.din("mod2", [128, KC, 2, 4])
    o_x, Box = cx.dout("xT_out", [KC, 128, NT])
    if debug_seg is not None:
        dbg = {n: cx.dout("dbg_" + n, [128, k, SEG], dt) for n, k, dt in (("cacc", 8, F32), ("cv", 8, BF16), ("gs", 8, BF16), ("hT", 16, BF16), ("g32", 8, F32))}
    wv = lambda w: w.rearrange("(kc p) n -> p kc n", p=128)
    with cx.st:
        ones_f, Bones = cx.sb([128, 128], F32)
        P.op("pool", lambda e: e.memset(ones_f[:, :], 1.0), writes=[Bones])
        eps_t, Beps = cx.sb([128, 1], F32)
        P.op("pool", lambda e: e.memset(eps_t[:, :], EPS), writes=[Beps])
        v8, Bv8 = cx.sb([128, 8, 4], F32); cw, Bcw = cx.sb([128, 8, 31], F32)
        g2, Bg2 = cx.sb([128, KC], F32); fg, Bfg = cx.sb([128, KC], F32); m2, Bm2 = cx.sb([128, KC, 2, 4], F32)
        sc2, Bsc2 = cx.sb([128, KC, 2, 2], F32)
        for (t, B_, src) in ((v8, Bv8, vec8), (cw, Bcw, convw), (g2, Bg2, g2T), (fg, Bfg, fgT), (m2, Bm2, mod2)):
            P.dma("sp", lambda e, t=t, src=src: e.dma_start(out=t[:], in_=src), B_)
        for var in range(2):
            P.op("dve", lambda e, var=var: e.scalar_tensor_tensor(out=sc2[:, :, var, 0], in0=m2[:, :, var, 2], scalar=1.0, in1=g2[:, :],
                                                                  op0=ALU.add, op1=ALU.mult), reads=[Bm2, Bg2], writes=[Bsc2])
            P.op("dve", lambda e, var=var: e.tensor_copy(out=sc2[:, :, var, 1], in_=m2[:, :, var, 1]), reads=[Bm2], writes=[Bsc2])
        if final:
            scf, Bscf = cx.sb([128, KC, 2, 2], F32)
            P.op("pool", lambda e: e.memset(scf[:, :, :, :], 0.0), writes=[Bscf])
            for var in range(2):
                P.op("dve", lambda e, var=var: e.tensor_copy(out=scf[:, :, var, 0], in_=fg[:, :]), reads=[Bfg, Bscf], writes=[Bscf])
        xs, Bxs = cx.sb([128, KC, SEG], F32)
        hT, BhT = cx.sb([128, KC, SEG], BF16)
        hh, Bhh = cx.sb([128, 64, SEG], BF16)
        ys, Bys = cx.sb([128, 8, SEG], F32)
        gb, Bgb = cx.sb([128, 8, SEG], BF16); gs, Bgs = cx.sb([128, 8, SEG], BF16)
        cv, Bcv = cx.sb([128, 8, SEG], BF16); at, Bat = cx.sb([128, 8, SEG], BF16)
        cacc, Bcacc = cx.sb([128, 8, SEG], F32)
        ucw_ring = cx.ring(2, [128, SEG + 30], F32)
        gate_ring = cx.ring(6, [128, SEG], BF16)
        w_ring = cx.ring(4, [128, 16, 256], BF16)
        tmp_ring = cx.ring(4, [128, SEG], F32)
        rstd_ring = cx.ring(2, [128, SEG], F32)
        ps_mm = cx.ring(6, [128, 512], F32, psum=True)
        ps_aux = cx.ring(2, [128, 512], F32, psum=True)
        ostg = cx.ring(2, [128, SEG], F32) if final else None

        WSPEC = [("w_glu", w_glu, 8, 8), ("w_attn_o", w_attn_o, 8, 16), ("w_ssm_o", w_ssm_o, 8, 16), ("w_conv_o", w_conv_o, 8, 16),
                 ("w_out", w_out, 16, 16), ("w_mlp1", w1, 16, 64), ("w_mlp2", w2, 64, 16)]
        wsc = {}; BW = {}
        for (wn, W, nK, n_oc) in WSPEC:
            npc = (nK + 15) // 16
            cx.n += 1
            wsc[wn] = nc.dram_tensor("wsc_%s_%d" % (wn, cx.n), [n_oc // 2, npc, 128, 16 * 256], BF16).ap()
            BW[wn] = Buf("wsc_" + wn)
            for oc2 in range(0, n_oc, 2):
                for k0 in range(0, nK, 16):
                    nk = min(16, nK - k0)
                    wt, Bw = w_ring.next()
                    P.dma("pool", lambda e, wt=wt, W=W, k0=k0, nk=nk, oc2=oc2: e.dma_start(
                        out=wt[:, 0:nk, :], in_=wv(W)[:, k0:k0 + nk, oc2 * 128:oc2 * 128 + 256]), Bw)
                    P.dma("sp", lambda e, wt=wt, wn=wn, oc2=oc2, k0=k0, nk=nk: e.dma_start(
                        out=wsc[wn][oc2 // 2, k0 // 16, :, 0:nk * 256], in_=wt[:, 0:nk, :].rearrange("p k c -> p (k c)")), BW[wn], reads=[Bw])
        WNAME = {id(w_glu): "w_glu", id(w_attn_o): "w_attn_o", id(w_ssm_o): "w_ssm_o", id(w_conv_o): "w_conv_o", id(w_out): "w_out", id(w1): "w_mlp1", id(w2): "w_mlp2"}

        def load_w(wt, Bw, W, oc2, k0, nk):
            wn = WNAME[id(W)]
            P.dma("sp", lambda e: e.dma_start(out=wt[:, 0:nk, :].rearrange("p k c -> p (k c)"), in_=wsc[wn][oc2 // 2, k0 // 16, :, 0:nk * 256]), Bw, reads=[BW[wn]])

        def mm_group(W, nK, n_oc, rhs_of, Brhs, epilogue):
            for oc2 in range(0, n_oc, 2):
                pss = [ps_mm.next(), ps_mm.next()]
                for k0 in range(0, nK, 16):
                    nk = min(16, nK - k0)
                    wt, Bw = w_ring.next()
                    load_w(wt, Bw, W, oc2, k0, nk)
                    for j in range(2):
                        pt, Bp = pss[j]
                        for kc in range(nk):
                            P.op("pe", lambda e, pt=pt, wt=wt, j=j, kc=kc, k0=k0: e.matmul(
                                pt[:, 0:SEG], lhsT=wt[:, kc, j * 128:(j + 1) * 128], rhs=rhs_of(k0 + kc),
                                start=(k0 + kc == 0), stop=(k0 + kc == nK - 1)), reads=[Bw, Brhs], writes=[Bp])
                for j in range(2):
                    epilogue(oc2 + j, pss[j][0], pss[j][1])

        for seg in (range(NSEG) if debug_seg is None else [debug_seg]):
            t0 = seg * SEG
            parts = [(0, NLAT - t0, 0), (NLAT - t0, SEG, 1)] if seg == NSEG - 1 else [(0, SEG, 0)]
            P.dma("sp", lambda e, t0=t0: e.dma_start(out=xs[:, :, :], in_=xT[:, :, t0:t0 + SEG].rearrange("k p t -> p k t")), Bxs)
            P.dma("sp", lambda e, t0=t0: e.dma_start(out=ys[:, :, :], in_=ysT[:, :, t0:t0 + SEG].rearrange("k p t -> p k t")), Bys)
            P.dma("sp", lambda e, t0=t0: e.dma_start(out=at[:, :, :], in_=attT[:, :, t0:t0 + SEG].rearrange("k p t -> p k t")), Bat)
            for c in range(8):
                t1, Bt1 = tmp_ring.next()
                P.op("dve", lambda e, t1=t1, c=c: e.tensor_tensor(out=t1[:, :], in0=ys[:, c, :], in1=ys[:, c, :], op=ALU.mult), reads=[Bys], writes=[Bt1])
                P.op("dve", lambda e, t1=t1: e.tensor_scalar(out=t1[:, :], in0=t1[:, :], scalar1=0.044715, scalar2=1.0, op0=ALU.mult, op1=ALU.add),
                     reads=[Bt1], writes=[Bt1])
                P.op("pool", lambda e, t1=t1, c=c: e.tensor_tensor(out=t1[:, :], in0=t1[:, :], in1=ys[:, c, :], op=ALU.mult), reads=[Bt1, Bys], writes=[Bt1])
                P.op("act", lambda e, t1=t1: e.activation(out=t1[:, :], in_=t1[:, :], func=AF.Sigmoid, scale=1.5957691216), reads=[Bt1], writes=[Bt1])
                P.op("dve", lambda e, t1=t1, c=c: e.tensor_tensor(out=ys[:, c, :], in0=ys[:, c, :], in1=t1[:, :], op=ALU.mult), reads=[Bys, Bt1], writes=[Bys])
                P.op("act", lambda e, c=c: e.activation(out=gb[:, c, :], in_=ys[:, c, :], func=AF.Copy), reads=[Bys], writes=[Bgb])
            def ep_glu(oc, pt, Bp):
                t1, Bt1 = tmp_ring.next()
                P.op("act", lambda e: e.activation(out=t1[:, :], in_=pt[:, 0:SEG], func=AF.Sigmoid, bias=v8[:, oc, 0:1]), reads=[Bp, Bv8], writes=[Bt1])
                P.op("dve", lambda e: e.tensor_tensor(out=gs[:, oc, :], in0=ys[:, oc, :], in1=t1[:, :], op=ALU.mult), reads=[Bys, Bt1], writes=[Bgs])
            mm_group(w_glu, 8, 8, lambda k: gb[:, k, :], Bgb, ep_glu)
            for c in range(8):
                for pi_, (ps0, pn) in enumerate(CONV_PIECES):
                    if not (t0 <= ps0 < t0 + SEG):
                        continue
                    o0 = ps0 - t0
                    win, Bwin = ucw_ring.next()
                    P.dma("sp", lambda e, win=win, c=c, pi_=pi_, pn=pn: e.dma_start(out=win[:, 0:pn + 30], in_=ucH[c, :, CONV_WOFF[pi_]:CONV_WOFF[pi_] + pn + 30]), Bwin)
                    a2, Ba2 = tmp_ring.next()
                    NDVE = 31
                    P.op("dve", lambda e, win=win, c=c, o0=o0, pn=pn: e.tensor_scalar(out=cacc[:, c, o0:o0 + pn], in0=win[:, 0:pn], scalar1=cw[:, c, 0:1],
                                                                                     scalar2=v8[:, c, 1:2], op0=ALU.mult, op1=ALU.add), reads=[Bwin, Bcw, Bv8], writes=[Bcacc])
                    for k in range(1, NDVE):
                        P.op("dve", lambda e, win=win, c=c, o0=o0, pn=pn, k=k: e.scalar_tensor_tensor(
                            out=cacc[:, c, o0:o0 + pn], in0=win[:, k:k + pn], scalar=cw[:, c, k:k + 1], in1=cacc[:, c, o0:o0 + pn],
                            op0=ALU.mult, op1=ALU.add), reads=[Bwin, Bcw, Bcacc], writes=[Bcacc])
            p1, Bp1 = ps_aux.next(); p2, Bp2 = ps_aux.next()
            for c in range(8):
                P.op("pe", lambda e, c=c, p1=p1: e.matmul(p1[:, 0:SEG], lhsT=ones_f[:, :], rhs=cacc[:, c, :], start=(c == 0), stop=(c == 7)), reads=[Bones, Bcacc], writes=[Bp1])
            for c in range(8):
                sq, Bsq = tmp_ring.next()
                P.op("act", lambda e, sq=sq, c=c: e.activation(out=sq[:, :], in_=cacc[:, c, :], func=AF.Square), reads=[Bcacc], writes=[Bsq])
                P.op("pe", lambda e, sq=sq, c=c, p2=p2: e.matmul(p2[:, 0:SEG], lhsT=ones_f[:, :], rhs=sq[:, :], start=(c == 0), stop=(c == 7)), reads=[Bones, Bsq], writes=[Bp2])
            mean, Bmean = rstd_ring.next(); rstd, Brstd = rstd_ring.next(); msq, Bmsq = tmp_ring.next()
            P.op("dve", lambda e, mean=mean, p1=p1: e.tensor_scalar(out=mean[:, :], in0=p1[:, 0:SEG], scalar1=1.0 / 1024, scalar2=None, op0=ALU.mult), reads=[Bp1], writes=[Bmean])
            P.op("pool", lambda e, msq=msq, mean=mean: e.tensor_tensor(out=msq[:, :], in0=mean[:, :], in1=mean[:, :], op=ALU.mult), reads=[Bmean], writes=[Bmsq])
            P.op("dve", lambda e, rstd=rstd, p2=p2, msq=msq: e.scalar_tensor_tensor(out=rstd[:, :], in0=p2[:, 0:SEG], scalar=1.0 / 1024, in1=msq[:, :], op0=ALU.mult, op1=ALU.subtract),
                 reads=[Bp2, Bmsq], writes=[Brstd])
            P.op("act", lambda e, rstd=rstd: e.activation(out=rstd[:, :], in_=rstd[:, :], func=AF.Sqrt, bias=eps_t[:, 0:1]), reads=[Brstd, Beps], writes=[Brstd])
            P.op("dve", lambda e, rstd=rstd: e.reciprocal(out=rstd[:, :], in_=rstd[:, :]), reads=[Brstd], writes=[Brstd])
            for c in range(8):
                t1, Bt1 = tmp_ring.next()
                P.op("dve", lambda e, t1=t1, c=c, mean=mean: e.tensor_tensor(out=t1[:, :], in0=cacc[:, c, :], in1=mean[:, :], op=ALU.subtract), reads=[Bcacc, Bmean], writes=[Bt1])
                P.op("pool", lambda e, t1=t1, rstd=rstd: e.tensor_tensor(out=t1[:, :], in0=t1[:, :], in1=rstd[:, :], op=ALU.mult), reads=[Bt1, Brstd], writes=[Bt1])
                P.op("pool", lambda e, t1=t1, c=c: e.tensor_scalar(out=t1[:, :], in0=t1[:, :], scalar1=v8[:, c, 2:3], scalar2=v8[:, c, 3:4], op0=ALU.mult, op1=ALU.add),
                     reads=[Bt1, Bv8], writes=[Bt1])
                P.op("act", lambda e, t1=t1, c=c: e.activation(out=cv[:, c, :], in_=t1[:, :], func=AF.Silu), reads=[Bt1], writes=[Bcv])
            branches = [(w_attn_o, at, Bat, 0), (w_ssm_o, gs, Bgs, 16), (w_conv_o, cv, Bcv, 32)]
            for oc2 in range(0, 16, 2):
                wts = []
                for (W, src, Bsrc, goff) in branches:
                    wt, Bw = w_ring.next()
                    load_w(wt, Bw, W, oc2, 0, 8)
                    wts.append((wt, Bw))
                for j in range(2):
                    oc = oc2 + j
                    acc = None
                    for bi, (W, src, Bsrc, goff) in enumerate(branches):
                        wt, Bw = wts[bi]
                        pt, Bp = ps_mm.next()
                        for kc in range(8):
                            P.op("pe", lambda e, pt=pt, wt=wt, j=j, kc=kc, src=src: e.matmul(pt[:, 0:SEG], lhsT=wt[:, kc, j * 128:(j + 1) * 128], rhs=src[:, kc, :],
                                                                                            start=(kc == 0), stop=(kc == 7)), reads=[Bw, Bsrc], writes=[Bp])
                        gt_, Bgt = gate_ring.next()
                        P.dma("sp", lambda e, gt_=gt_, goff=goff, oc=oc, t0=t0: e.dma_start(out=gt_[:, :], in_=gT[goff + oc, :, t0:t0 + SEG]), Bgt)
                        t1, Bt1 = tmp_ring.next()
                        P.op("dve", lambda e, t1=t1, pt=pt, gt_=gt_: e.tensor_tensor(out=t1[:, :], in0=pt[:, 0:SEG], in1=gt_[:, :], op=ALU.mult), reads=[Bp, Bgt], writes=[Bt1])
                        if acc is None:
                            acc = (t1, Bt1)
                        elif bi == 1:
                            a_, Ba_ = acc
                            P.op("dve", lambda e, a_=a_, t1=t1: e.tensor_tensor(out=a_[:, :], in0=a_[:, :], in1=t1[:, :], op=ALU.add), reads=[Ba_, Bt1], writes=[Ba_])
                        else:
                            a_, Ba_ = acc
                            P.op("dve", lambda e, a_=a_, t1=t1, oc=oc: e.tensor_tensor(out=hT[:, oc, :], in0=a_[:, :], in1=t1[:, :], op=ALU.add), reads=[Ba_, Bt1], writes=[BhT])
            if debug_seg is not None:
                for n, (t_, B_) in (("cacc", (cacc, Bcacc)), ("cv", (cv, Bcv)), ("gs", (gs, Bgs)), ("hT", (hT, BhT)), ("g32", (ys, Bys))):
                    P.dma("sp", lambda e, n=n, t_=t_: e.dma_start(out=dbg[n][0][:, :, :], in_=t_[:, :, :]), dbg[n][1], reads=[B_])
            def ep_res(gidx):
                def ep(oc, pt, Bp):
                    for (a, b, var) in parts:
                        P.op("dve", lambda e, a=a, b=b, var=var: e.scalar_tensor_tensor(out=xs[:, oc, a:b], in0=pt[:, a:b], scalar=m2[:, oc, var, gidx:gidx + 1],
                                                                                       in1=xs[:, oc, a:b], op0=ALU.mult, op1=ALU.add), reads=[Bp, Bm2, Bxs], writes=[Bxs])
                return ep
            mm_group(w_out, 16, 16, lambda k: hT[:, k, :], BhT, ep_res(0))
            rmsnorm_mod_segment(cx, eps_t, xs, Bxs, ones_f, Bones, sc2, Bsc2, hT, BhT, seg, tmp_ring, ps_aux, rstd_ring, local=True, engs=("dve", "dve", "pool"))
            def ep_mlp1(oc, pt, Bp):
                t1, Bt1 = tmp_ring.next()
                P.op("act", lambda e: e.activation(out=t1[:, :], in_=pt[:, 0:SEG], func=AF.Relu), reads=[Bp], writes=[Bt1])
                P.op("act", lambda e: e.activation(out=hh[:, oc, :], in_=t1[:, :], func=AF.Square), reads=[Bt1], writes=[Bhh])
            mm_group(w1, 16, 64, lambda k: hT[:, k, :], BhT, ep_mlp1)
            mm_group(w2, 64, 16, lambda k: hh[:, k, :], Bhh, ep_res(3))
            if not final:
                P.dma("sp", lambda e, t0=t0: e.dma_start(out=o_x[:, :, t0:t0 + SEG].rearrange("k p t -> p k t"), in_=xs[:, :, :]), Box, reads=[Bxs])
            else:
                pt, Bp = ps_aux.next()
                for kc in range(KC):
                    sq, Bsq = tmp_ring.next()
                    P.op("act", lambda e, sq=sq, kc=kc: e.activation(out=sq[:, :], in_=xs[:, kc, :], func=AF.Square), reads=[Bxs], writes=[Bsq])
                    P.op("pe", lambda e, sq=sq, kc=kc, pt=pt: e.matmul(pt[:, 0:SEG], lhsT=ones_f[:, :], rhs=sq[:, :], start=(kc == 0), stop=(kc == KC - 1)),
                         reads=[Bsq, Bones], writes=[Bp])
                rs, Brs = rstd_ring.next()
                P.op("act", lambda e, rs=rs, pt=pt: e.activation(out=rs[:, :], in_=pt[:, 0:SEG], func=AF.Sqrt, scale=1.0 / D, bias=eps_t[:, 0:1]), reads=[Bp, Beps], writes=[Brs])
                P.op("dve", lambda e, rs=rs: e.reciprocal(out=rs[:, :], in_=rs[:, :]), reads=[Brs], writes=[Brs])
                for kc in range(KC):
                    og, Bog_ = ostg.next()
                    P.op("dve", lambda e, og=og, kc=kc, rs=rs: e.scalar_tensor_tensor(out=og[:, :], in0=xs[:, kc, :], scalar=fg[:, kc:kc + 1], in1=rs[:, :],
                                                                                     op0=ALU.mult, op1=ALU.mult), reads=[Bxs, Bfg, Brs], writes=[Bog_])
                    P.dma("sp", lambda e, og=og, kc=kc, t0=t0: e.dma_start(out=o_x[kc, :, t0:t0 + SEG], in_=og[:, :]), Box, reads=[Bog_])
        cx.finish([Box] + ([dbg[n][1] for n in dbg] if debug_seg is not None else []))
    return nc, P


L = 8192; C = 256; B = 2


def core_tokens_T(x_lat, x_ctx, core):
    b, s = core // 4, core % 4
    t = np.concatenate([x_lat[b, s * NLAT:(s + 1) * NLAT], x_ctx[b, s * NCTX:(s + 1) * NCTX]], axis=0)
    return np.ascontiguousarray(t.T)


def rope_tables(core):
    s = core % 4
    t = np.arange(s * NLAT, (s + 1) * NLAT, dtype=np.float32)
    inv_freq = (10000.0 ** (-np.arange(0, 64, 2, dtype=np.float32) / 64)).astype(np.float32)
    d = np.arange(128)
    a = d // 64; i = d % 32; half = (d % 64) // 32
    pos = np.where(a[:, None] == 0, np.floor(t / 64)[None, :], np.mod(t, 64)[None, :]).astype(np.float32)
    ang = pos * inv_freq[i][:, None]
    cosT = np.ones((128, NT), np.float32); sinT = np.zeros((128, NT), np.float32)
    cosT[:, :NLAT] = np.cos(ang)
    sinT[:, :NLAT] = np.sin(ang) * np.where(half == 0, -1.0, 1.0)[:, None]
    return cosT, sinT


def perm_matrix():
    d = np.arange(128)
    partner = np.where((d % 64) < 32, d + 32, d - 32)
    Pm = np.zeros((128, 128), np.float32)
    Pm[partner, d] = 1.0
    return Pm


def colT(v, nchunks):
    return np.ascontiguousarray(v.reshape(nchunks, 128).T)


def ssm_params(d, l, chunk):
    rowp = np.zeros((2, 3, 128, 128), np.float32); colp = np.zeros((2, 3, 128, 4), np.float32)
    BT = np.zeros((2, 2, 128, 128), np.float32); CT = np.zeros((2, 2, 128, 4, 128), np.float32)
    for dr in range(2):
        srcs = [d["ssm_a_re"][l, dr], d["ssm_a_im"][l, dr], np.repeat(d["ssm_log_dt"][l, dr][:, None], 64, axis=1)]
        bs = [d["ssm_b_re"][l, dr], d["ssm_b_im"][l, dr]]
        cs = [d["ssm_c_re"][l, dr], d["ssm_c_im"][l, dr]]
        for q in range(4):
            gA = 8 * chunk + 2 * q; gB = gA + 1
            for k in range(3):
                row = np.concatenate([srcs[k][gA], srcs[k][gB]])
                rowp[dr, k, 32 * q:32 * q + 32, :] = row[None, :]
                colp[dr, k, :, q] = row
            for k in range(2):
                BT[dr, k, 32 * q:32 * q + 16, 0:64] = bs[k][gA].T
                BT[dr, k, 32 * q + 16:32 * q + 32, 64:128] = bs[k][gB].T
                CT[dr, k, 0:64, q, 32 * q:32 * q + 16] = cs[k][gA].T
                CT[dr, k, 64:128, q, 32 * q + 16:32 * q + 32] = cs[k][gB].T
    dvec = np.ascontiguousarray(d["ssm_d"][l][128 * chunk:128 * chunk + 128, None])
    return {"rowp": rowp, "colp": colp, "BT": BT, "CT": CT, "dvec": dvec}


def conv_windows(uc_lat, uc_ctx, core, pieces, wtot):
    b, s = core // 4, core % 4
    out = np.zeros((wtot, 1024), np.float32)
    o = 0
    for (ps0, pn) in pieces:
        if ps0 < NLAT:
            src = uc_lat[b]; g0 = s * NLAT + ps0
        else:
            src = uc_ctx[b]; g0 = s * NCTX + (ps0 - NLAT)
        lo, hi = g0 - 15, g0 + pn + 15
        a, e = max(lo, 0), min(hi, src.shape[0])
        out[o + (a - lo):o + (e - lo)] = src[a:e]
        o += pn + 30
    return np.ascontiguousarray(out.T).reshape(8, 128, wtot)


_PROGS = {}


def _prog(name):
    if name not in _PROGS:
        if name == "m":
            _PROGS[name] = build_stage_m()
        elif name == "a":
            _PROGS[name] = build_stage_a()[0]
        elif name == "b1":
            _PROGS[name] = build_stage_b1()[0]
        elif name == "b2":
            _PROGS[name] = build_stage_b2()[0]
        elif name == "c":
            _PROGS[name] = build_stage_c(False)[0]
        elif name == "cf":
            _PROGS[name] = build_stage_c(True)[0]
    return _PROGS[name]


def _run(name, in_maps):
    res = run_bass_kernel_spmd(_prog(name), in_maps, core_ids=list(range(8)))
    return [{k: np.asarray(v) for k, v in r.items()} for r in res.results]


def kernel(x, c, ctx, c_ctx, norm1_g, norm2_g, w_mod, b_mod, w_in, b_gate, q_norm_g, k_norm_g, w_attn_o,
           ssm_a_re, ssm_a_im, ssm_log_dt, ssm_b_re, ssm_b_im, ssm_c_re, ssm_c_im, ssm_d, w_glu, b_glu, w_ssm_o,
           conv_w, conv_b, conv_ln_g, conv_ln_b, w_conv_o, w_out, w_mlp1, w_mlp2, final_g):
    f32 = np.float32
    d = dict(ssm_a_re=np.asarray(ssm_a_re, f32), ssm_a_im=np.asarray(ssm_a_im, f32), ssm_log_dt=np.asarray(ssm_log_dt, f32),
             ssm_b_re=np.asarray(ssm_b_re, f32), ssm_b_im=np.asarray(ssm_b_im, f32), ssm_c_re=np.asarray(ssm_c_re, f32),
             ssm_c_im=np.asarray(ssm_c_im, f32), ssm_d=np.asarray(ssm_d, f32))
    x = np.asarray(x, f32); ctx = np.asarray(ctx, f32); w_mod = np.asarray(w_mod, f32); b_mod = np.asarray(b_mod, f32)
    DEPTH = 4
    c_all = np.concatenate([np.asarray(c, f32), np.asarray(c_ctx, f32)[None, :]], axis=0)
    cT = np.ascontiguousarray(c_all.reshape(3, 16, 128).transpose(2, 1, 0))
    in_maps = []
    for core in range(8):
        wm = np.empty((48, 128, 16, 128), f32); bm = np.empty((128, 48), f32)
        for u in range(48):
            l, j = divmod(core * 48 + u, 96)
            wm[u] = w_mod[l][:, j * 128:(j + 1) * 128].reshape(16, 128, 128).transpose(1, 0, 2)
            bm[:, u] = b_mod[l][j * 128:(j + 1) * 128]
        in_maps.append({"wm": wm, "cT": cT, "bm": bm})
    outs = _run("m", in_maps)
    mod = np.empty((DEPTH, 3, 12288), f32)
    for core in range(8):
        for u in range(48):
            l, j = divmod(core * 48 + u, 96)
            mod[l, :, j * 128:(j + 1) * 128] = outs[core]["modT"][:, u, :].T
    del in_maps
    xT = [core_tokens_T(x, ctx, core).reshape(16, 128, NT) for core in range(8)]
    Pm = perm_matrix()
    ropes = [rope_tables(core) for core in range(8)]
    for l in range(DEPTH):
        w_in_l = np.ascontiguousarray(np.asarray(w_in[l], f32))
        g1T = colT(np.asarray(norm1_g[l], f32), 16)
        qkg = np.stack([np.asarray(q_norm_g[l], f32), np.asarray(k_norm_g[l], f32)], axis=1)
        bgT = colT(np.asarray(b_gate[l], f32), 48)
        in_maps = []
        for core in range(8):
            b = core // 4
            modss = np.empty((128, 16, 2, 2), f32)
            for vi, v in enumerate((b, 2)):
                modss[:, :, vi, 0] = colT(mod[l, v, 0:2048], 16)
                modss[:, :, vi, 1] = colT(mod[l, v, 2048:4096], 16)
            in_maps.append({"xT": xT[core], "w_in": w_in_l, "g1T": g1T, "modss": modss, "qkg": qkg,
                            "cosT": ropes[core][0], "sinT": ropes[core][1], "permM": Pm, "bgT": bgT})
        oa = _run("a", in_maps)
        del in_maps, w_in_l
        kT_all = []; v_all = []
        for b in range(2):
            kk = np.empty((2, 128, NKEY), oa[0]["kT"].dtype); vv = np.empty((NKEY, 256), oa[0]["v"].dtype)
            for s in range(4):
                o = oa[4 * b + s]
                kk[:, :, s * NLAT:(s + 1) * NLAT] = o["kT"][:, :, :NLAT]; kk[:, :, L + s * NCTX:L + (s + 1) * NCTX] = o["kT"][:, :, NLAT:]
                vv[s * NLAT:(s + 1) * NLAT] = o["v"][:NLAT]; vv[L + s * NCTX:L + (s + 1) * NCTX] = o["v"][NLAT:]
            kT_all.append(kk); v_all.append(vv)
        ob1 = _run("b1", [{"qT": oa[core]["qT"], "kT": kT_all[core // 4], "v": v_all[core // 4]} for core in range(8)])
        del kT_all, v_all
        in_maps = []
        for k in range(8):
            m = ssm_params(d, l, k)
            uT = np.empty((128, 2, NKEY), f32)
            for b in range(2):
                for s in range(4):
                    o = oa[4 * b + s]["uT"][k]
                    uT[:, b, C + s * NLAT:C + (s + 1) * NLAT] = o[:, :NLAT]; uT[:, b, s * NCTX:(s + 1) * NCTX] = o[:, NLAT:]
            m["uT"] = uT
            in_maps.append(m)
        ob2 = _run("b2", in_maps)
        del in_maps
        uc_lat = np.empty((2, L, 1024), f32); uc_ctx = np.empty((2, C, 1024), f32)
        for core in range(8):
            b, s = core // 4, core % 4
            u2 = oa[core]["ucT"].reshape(1024, NT)
            uc_lat[b, s * NLAT:(s + 1) * NLAT] = u2[:, :NLAT].T; uc_ctx[b, s * NCTX:(s + 1) * NCTX] = u2[:, NLAT:].T
        Wl = {k: np.ascontiguousarray(np.asarray(v[l], f32)) for k, v in (("w_glu", w_glu), ("w_ssm_o", w_ssm_o), ("w_conv_o", w_conv_o),
                                                                          ("w_attn_o", w_attn_o), ("w_out", w_out), ("w_mlp1", w_mlp1), ("w_mlp2", w_mlp2))}
        vec8 = np.ascontiguousarray(np.stack([colT(np.asarray(t[l], f32), 8) for t in (b_glu, conv_b, conv_ln_g, conv_ln_b)], axis=2))
        convw = np.ascontiguousarray(np.asarray(conv_w[l], f32).T.reshape(8, 128, 31).transpose(1, 0, 2))
        g2T = colT(np.asarray(norm2_g[l], f32), 16); fgT = colT(np.asarray(final_g, f32), 16)
        in_maps = []
        for core in range(8):
            b, s = core // 4, core % 4
            m = dict(Wl)
            m["xT"] = xT[core]; m["attT"] = ob1[core]["attT"]; m["gT"] = oa[core]["gT"]
            ys = np.empty((8, 128, NT), f32)
            for k in range(8):
                ys[k, :, :NLAT] = ob2[k]["ysT"][:, b, C + s * NLAT:C + (s + 1) * NLAT]; ys[k, :, NLAT:] = ob2[k]["ysT"][:, b, s * NCTX:(s + 1) * NCTX]
            m["ysT"] = ys
            m["ucH"] = conv_windows(uc_lat, uc_ctx, core, CONV_PIECES, WTOT)
            m["vec8"] = vec8; m["convw"] = convw; m["g2T"] = g2T; m["fgT"] = fgT
            mod2 = np.empty((128, 16, 2, 4), f32)
            for vi, v in enumerate((b, 2)):
                for i, mi in enumerate((2, 3, 4, 5)):
                    mod2[:, :, vi, i] = colT(mod[l, v, mi * 2048:(mi + 1) * 2048], 16)
            m["mod2"] = mod2
            in_maps.append(m)
        oc = _run("cf" if l == DEPTH - 1 else "c", in_maps)
        del in_maps, oa, ob1, ob2
        xT = [oc[core]["xT_out"] for core in range(8)]
    out = np.empty((2, L, D), f32)
    for core in range(8):
        b, s = core // 4, core % 4
        out[b, s * NLAT:(s + 1) * NLAT] = xT[core].reshape(D, NT)[:, :NLAT].T
    return out
```

```python
import contextlib, math
import numpy as np
import concourse.bass as bass
import concourse.mybir as mybir
from concourse.bass_utils import run_bass_kernel_spmd

F32 = mybir.dt.float32
BF16 = mybir.dt.bfloat16
AF = mybir.ActivationFunctionType
ALU = mybir.AluOpType
AX = mybir.AxisListType

ENGS = ("pe", "act", "dve", "pool", "sp")
CONSERVATIVE = False


class Buf:
    __slots__ = ("name", "lw", "rd", "dsem", "dcnt", "lw_dma")

    def __init__(self, name):
        self.name = name
        self.lw = None
        self.rd = []
        self.dsem = None
        self.dcnt = 0
        self.lw_dma = False


class Prog:
    def __init__(self, nc):
        self.nc = nc
        self.ops = {e: [] for e in ENGS}
        self.cnt = {e: 0 for e in ENGS}
        self.waited = {e: {} for e in ENGS}
        self.nsem_dma = 0
        self.sems = {}
        self.final_waits = []
        self.dma_final = {}

    def _need(self, eng, ev, waits):
        if ev is None:
            return
        k, v = ev
        if self.waited[eng].get(k, 0) >= v:
            return
        self.waited[eng][k] = v
        waits[k] = max(waits.get(k, 0), v)

    def _deps(self, eng, reads, writes, dma_dst=None, is_dma=False):
        waits = {}
        for b in reads:
            self._need(eng, b.lw, waits)
        for b in writes:
            if dma_dst is b and b.lw_dma and not b.rd:
                pass
            elif b.lw is not None and (is_dma or b.lw[0] != eng or (CONSERVATIVE and eng != "pe")):
                self._need(eng, b.lw, waits)
            for ev in b.rd:
                if is_dma or ev[0] != eng or (CONSERVATIVE and eng != "pe"):
                    self._need(eng, ev, waits)
        return list(waits.items())

    def op(self, eng, fn, reads=(), writes=()):
        waits = self._deps(eng, reads, writes)
        self.cnt[eng] += 1
        ev = (eng, self.cnt[eng])
        self.ops[eng].append((waits, fn, (eng, 1)))
        for b in reads:
            b.rd.append(ev)
        for b in writes:
            b.lw = ev
            b.rd = []
            b.lw_dma = False
        return ev

    def dma(self, eng, fn, dst, reads=(), extra_writes=(), inc=16):
        waits = self._deps(eng, reads, (dst,) + tuple(extra_writes), dma_dst=dst, is_dma=True)
        if dst.dsem is None:
            dst.dsem = "d%d" % self.nsem_dma
            self.nsem_dma += 1
        dst.dcnt += inc
        ev = (dst.dsem, dst.dcnt)
        self.dma_final[dst.dsem] = dst.dcnt
        self.ops[eng].append((waits, fn, (dst.dsem, inc)))
        for b in reads:
            b.rd.append(ev)
        for b in (dst,) + tuple(extra_writes):
            b.lw = ev
            b.rd = []
            b.lw_dma = True
        return ev

    def wait_final(self, eng, bufs):
        waits = {}
        for b in bufs:
            self._need(eng, b.lw, waits)
        self.ops[eng].append((list(waits.items()), None, None))

    def barrier(self):
        evs = [(e, self.cnt[e]) for e in ENGS if self.cnt[e] > 0] + list(self.dma_final.items())
        for eng in ENGS:
            waits = {}
            for ev in evs:
                if ev[0] != eng:
                    self._need(eng, ev, waits)
            self.ops[eng].append((list(waits.items()), None, None))

    def emit_pooled(self, pool):
        nc = self.nc
        assert self.nsem_dma <= len(pool["dma"]), (self.nsem_dma, len(pool["dma"]))
        hs = {}; base = {}
        for e in ENGS:
            hs[e], base[e] = pool["eng"][e]
        for i in range(self.nsem_dma):
            hs["d%d" % i], base["d%d" % i] = pool["dma"][i]
        with nc.Block() as block:
            def runner(ename):
                def run(eng):
                    for waits, fn, inc in self.ops[ename]:
                        for k, v in waits:
                            eng.wait_ge(hs[k], base[k] + v)
                        if fn is not None:
                            ins = fn(eng)
                            ins.then_inc(hs[inc[0]], inc[1])
                return run
            block.tensor(runner("pe")); block.scalar(runner("act")); block.vector(runner("dve"))
            block.gpsimd(runner("pool")); block.sync(runner("sp"))
        for e in ENGS:
            pool["eng"][e][1] += self.cnt[e]
        for i in range(self.nsem_dma):
            pool["dma"][i][1] += self.dma_final.get("d%d" % i, 0)

    def emit(self):
        nc = self.nc
        import contextlib
        with contextlib.ExitStack() as st:
            for e in ENGS:
                self.sems[e] = st.enter_context(nc.semaphore("s_" + e))
            for i in range(self.nsem_dma):
                self.sems["d%d" % i] = st.enter_context(nc.semaphore("sd%d" % i))
            block = st.enter_context(nc.Block())
            sems = self.sems

            def runner(ename):
                def run(eng):
                    for waits, fn, inc in self.ops[ename]:
                        for k, v in waits:
                            eng.wait_ge(sems[k], v)
                        if fn is not None:
                            ins = fn(eng)
                            ins.then_inc(sems[inc[0]], inc[1])
                return run

            block.tensor(runner("pe"))
            block.scalar(runner("act"))
            block.vector(runner("dve"))
            block.gpsimd(runner("pool"))
            block.sync(runner("sp"))

    def stats(self):
        return {e: len(self.ops[e]) for e in ENGS}


D = 2048; NT = 2112; NLAT = 2048; NCTX = 64; SEG = 352; NSEG = 6
KC = 16
IN_COLS = 10752
EPS = 1e-6


class Ctx:
    _uid = [0]

    def __init__(self, nc, io=None, pool=None):
        self.nc = nc; self.P = Prog(nc); self.st = contextlib.ExitStack(); self.io = io; self.pool = pool
        Ctx._uid[0] += 1
        self.n = Ctx._uid[0] * 100000

    def sb(self, shape, dt, name=None):
        self.n += 1
        t = self.st.enter_context(self.nc.sbuf_tensor(name or ("t%d" % self.n), list(shape), dt))
        return t, Buf(name or ("t%d" % self.n))

    def ps(self, shape, dt=F32, name=None):
        self.n += 1
        t = self.st.enter_context(self.nc.psum_tensor(name or ("p%d" % self.n), list(shape), dt))
        return t, Buf(name or ("p%d" % self.n))

    def ring(self, n, shape, dt, psum=False):
        return Ring([self.ps(shape, dt) if psum else self.sb(shape, dt) for _ in range(n)])

    def din(self, name, shape, dt=F32):
        if self.io is not None:
            ap = self.io[name]
            assert list(ap.shape) == list(shape), (name, ap.shape, shape)
            return ap
        return self.nc.dram_tensor(name, list(shape), dt, kind="ExternalInput").ap()

    def dout(self, name, shape, dt=F32):
        if self.io is not None:
            ap = self.io[name]
            assert list(ap.shape) == list(shape), (name, ap.shape, shape)
            return ap, Buf(name)
        return self.nc.dram_tensor(name, list(shape), dt, kind="ExternalOutput").ap(), Buf(name)

    def finish(self, out_bufs):
        if self.pool is None:
            self.P.wait_final("sp", out_bufs)
            self.P.emit()
        else:
            self.P.barrier()
            self.P.emit_pooled(self.pool)


class Ring:
    def __init__(self, items):
        self.items = items; self.i = 0

    def next(self):
        it = self.items[self.i % len(self.items)]; self.i += 1
        return it


def make_ident(cx, dt=BF16):
    P = cx.P
    idf, Bidf = cx.sb([128, 128], F32)
    P.op("pool", lambda e: e.memset(idf[:, :], 0.0), writes=[Bidf])
    P.op("pool", lambda e: e.affine_select(out=idf[:, :], in_=idf[:, :], pattern=[[-1, 128]],
                                           compare_op=ALU.not_equal, fill=1.0, base=0, channel_multiplier=1),
         reads=[Bidf], writes=[Bidf])
    if dt == F32:
        return idf, Bidf
    idb, Bidb = cx.sb([128, 128], dt)
    P.op("dve", lambda e: e.tensor_copy(out=idb[:, :], in_=idf[:, :]), reads=[Bidf], writes=[Bidb])
    return idb, Bidb


def build_stage_m(nunits=48, nvar=3, nc=None, io=None, pool=None):
    nc = nc or bass.Bass("TRN2", target_bir_lowering=False)
    cx = Ctx(nc, io, pool); P = cx.P
    wm = cx.din("wm", [nunits, 128, KC, 128])
    cT = cx.din("cT", [128, KC, nvar])
    bm = cx.din("bm", [128, nunits])
    out, Bout = cx.dout("modT", [128, nunits, nvar])
    with cx.st:
        cs, Bcs = cx.sb([128, KC, nvar], F32)
        cb, Bcb = cx.sb([128, KC, nvar], BF16)
        bms, Bbms = cx.sb([128, nunits], F32)
        res, Bres = cx.sb([128, nunits, nvar], F32)
        wr = cx.ring(3, [128, KC, 128], BF16)
        pr = cx.ring(2, [128, 4], F32, psum=True)
        P.dma("sp", lambda e: e.dma_start(out=cs[:, :, :], in_=cT[:, :, :]), Bcs)
        P.dma("sp", lambda e: e.dma_start(out=bms[:, :], in_=bm[:, :]), Bbms)
        P.op("act", lambda e: e.activation(out=cb[:, :, :], in_=cs[:, :, :], func=AF.Silu), reads=[Bcs], writes=[Bcb])
        for u in range(nunits):
            wt, Bw = wr.next()
            P.dma("pool", lambda e, wt=wt, u=u: e.dma_start(out=wt[:, :, :], in_=wm[u, :, :, :]), Bw)
            pt, Bp = pr.next()
            for kc in range(KC):
                P.op("pe", lambda e, wt=wt, pt=pt, kc=kc: e.matmul(pt[:, 0:nvar], lhsT=wt[:, kc, :], rhs=cb[:, kc, :],
                                                                   start=(kc == 0), stop=(kc == KC - 1)),
                     reads=[Bw, Bcb], writes=[Bp])
            P.op("dve", lambda e, pt=pt, u=u: e.tensor_scalar(out=res[:, u, :], in0=pt[:, 0:nvar], scalar1=bms[:, u:u + 1],
                                                              scalar2=None, op0=ALU.add),
                 reads=[Bp, Bbms], writes=[Bres])
        P.dma("sp", lambda e: e.dma_start(out=out[:, :, :], in_=res[:, :, :]), Bout, reads=[Bres])
        cx.finish([Bout])
    return nc


def rmsnorm_mod_segment(cx, eps_t, xs, Bxs, ones_f, Bones, sc_all, Bsc, hT, BhT, seg, tmp_ring, ps_ring, rstd_ring, local=False, engs=("dve", "pool")):
    P = cx.P
    pt, Bp = ps_ring.next()
    for kc in range(KC):
        sq, Bsq = tmp_ring.next()
        P.op("act", lambda e, sq=sq, kc=kc: e.activation(out=sq[:, :], in_=xs[:, kc, :], func=AF.Square),
             reads=[Bxs], writes=[Bsq])
        P.op("pe", lambda e, sq=sq, pt=pt, kc=kc: e.matmul(pt[:, 0:SEG], lhsT=ones_f[:, :], rhs=sq[:, :],
                                                           start=(kc == 0), stop=(kc == KC - 1)),
             reads=[Bsq, Bones], writes=[Bp])
    rstd, Br = rstd_ring.next()
    P.op("act", lambda e: e.activation(out=rstd[:, :], in_=pt[:, 0:SEG], func=AF.Sqrt, scale=1.0 / D, bias=eps_t[:, 0:1]),
         reads=[Bp], writes=[Br])
    P.op("dve", lambda e: e.reciprocal(out=rstd[:, :], in_=rstd[:, :]), reads=[Br], writes=[Br])
    t0 = seg * SEG
    ho = 0 if local else t0
    if seg == NSEG - 1:
        parts = [(0, NLAT - t0, 0), (NLAT - t0, SEG, 1)]
    else:
        parts = [(0, SEG, 0)]
    for kc in range(KC):
        tm, Bt = tmp_ring.next()
        eng = engs[kc % len(engs)]
        P.op(eng, lambda e, tm=tm, kc=kc: e.tensor_tensor(out=tm[:, :], in0=xs[:, kc, :], in1=rstd[:, :], op=ALU.mult),
             reads=[Bxs, Br], writes=[Bt])
        for (a, b, var) in parts:
            P.op(eng, lambda e, tm=tm, kc=kc, a=a, b=b, var=var: e.tensor_scalar(
                out=hT[:, kc, ho + a:ho + b], in0=tm[:, a:b], scalar1=sc_all[:, kc, var, 0:1], scalar2=sc_all[:, kc, var, 1:2],
                op0=ALU.mult, op1=ALU.add), reads=[Bt, Bsc], writes=[BhT])


def build_stage_a(nc=None, io=None, pool=None):
    nc = nc or bass.Bass("TRN2", target_bir_lowering=False)
    cx = Ctx(nc, io, pool); P = cx.P
    xT = cx.din("xT", [KC, 128, NT])
    w_in = cx.din("w_in", [D, IN_COLS])
    gT = cx.din("g1T", [128, KC])
    modss = cx.din("modss", [128, KC, 2, 2])
    qkg = cx.din("qkg", [128, 2])
    cosT = cx.din("cosT", [128, NT]); sinT = cx.din("sinT", [128, NT])
    permM = cx.din("permM", [128, 128])
    bgT = cx.din("bgT", [128, 48])
    o_q, Boq = cx.dout("qT", [8, 128, NT], BF16)
    o_k, Bok = cx.dout("kT", [2, 128, NT], BF16)
    o_v, Bov = cx.dout("v", [NT, 256], BF16)
    o_u, Bou = cx.dout("uT", [8, 128, NT], F32)
    o_uc, Bouc = cx.dout("ucT", [8, 128, NT], F32)
    o_g, Bog = cx.dout("gT", [48, 128, NT], BF16)
    w_v = w_in.rearrange("(kc p) n -> p kc n", p=128)
    with cx.st:
        ones_f, Bones = cx.sb([128, 128], F32)
        P.op("pool", lambda e: e.memset(ones_f[:, :], 1.0), writes=[Bones])
        eps_t, Beps = cx.sb([128, 1], F32)
        P.op("pool", lambda e: e.memset(eps_t[:, :], EPS), writes=[Beps])
        g1, Bg1 = cx.sb([128, KC], F32)
        ms, Bms = cx.sb([128, KC, 2, 2], F32)
        sc_all, Bsc = cx.sb([128, KC, 2, 2], F32)
        qk, Bqk = cx.sb([128, 2], F32)
        cs, Bcs = cx.sb([128, NT], F32); sn, Bsn = cx.sb([128, NT], F32)
        pm, Bpm = cx.sb([128, 128], F32)
        bg, Bbg = cx.sb([128, 48], F32)
        hT, BhT = cx.sb([128, KC, NT], BF16)
        P.dma("sp", lambda e: e.dma_start(out=g1[:, :], in_=gT[:, :]), Bg1)
        P.dma("sp", lambda e: e.dma_start(out=ms[:, :, :, :], in_=modss[:, :, :, :]), Bms)
        P.dma("sp", lambda e: e.dma_start(out=qk[:, :], in_=qkg[:, :]), Bqk)
        P.dma("sp", lambda e: e.dma_start(out=cs[:, :], in_=cosT[:, :]), Bcs)
        P.dma("sp", lambda e: e.dma_start(out=sn[:, :], in_=sinT[:, :]), Bsn)
        P.dma("sp", lambda e: e.dma_start(out=pm[:, :], in_=permM[:, :]), Bpm)
        P.dma("sp", lambda e: e.dma_start(out=bg[:, :], in_=bgT[:, :]), Bbg)
        for var in range(2):
            P.op("dve", lambda e, var=var: e.scalar_tensor_tensor(out=sc_all[:, :, var, 0], in0=ms[:, :, var, 1], scalar=1.0,
                                                                  in1=g1[:, :], op0=ALU.add, op1=ALU.mult),
                 reads=[Bms, Bg1], writes=[Bsc])
            P.op("dve", lambda e, var=var: e.tensor_copy(out=sc_all[:, :, var, 1], in_=ms[:, :, var, 0]),
                 reads=[Bms], writes=[Bsc])
        xr = cx.ring(1, [128, KC, SEG], F32)
        tmp_ring = cx.ring(4, [128, SEG], F32)
        rstd_ring = cx.ring(2, [128, SEG], F32)
        ps_aux = cx.ring(2, [128, 512], F32, psum=True)
        for seg in range(NSEG):
            xs, Bxs = xr.next()
            P.dma("sp", lambda e, xs=xs, seg=seg: e.dma_start(
                out=xs[:, :, :], in_=xT[:, :, seg * SEG:(seg + 1) * SEG].rearrange("k p t -> p k t")), Bxs)
            rmsnorm_mod_segment(cx, eps_t, xs, Bxs, ones_f, Bones, sc_all, Bsc, hT, BhT, seg, tmp_ring, ps_aux, rstd_ring)
        wr = cx.ring(2, [128, KC, 256], BF16)
        ps_mm = cx.ring(4, [128, 512], F32, psum=True)
        stg_b = cx.ring(3, [128, SEG], BF16)
        stg_f = cx.ring(3, [128, SEG], F32)
        sig_ring = cx.ring(2, [128, NT], F32)
        groups = [0, 2, 4, 6, 8, 10, 12, 14, 16, 18]
        for i in (0, 2, 4, 6):
            groups += [28 + i, 20 + i]
        groups += list(range(36, 84, 2))
        sig_tiles = {}
        for g0 in groups:
            wt, Bw = wr.next()
            P.dma("pool", lambda e, wt=wt, g0=g0: e.dma_start(out=wt[:, :, :], in_=w_v[:, :, g0 * 128:g0 * 128 + 256]), Bw)
            for j in range(2):
                ct = g0 + j
                if ct in (10, 11):
                    continue
                if 28 <= ct < 36:
                    sg, Bsg = sig_ring.next(); sig_tiles[ct - 8] = (sg, Bsg)
                for seg in range(NSEG):
                    t0 = seg * SEG
                    pt, Bp = ps_mm.next()
                    for kc in range(KC):
                        P.op("pe", lambda e, pt=pt, wt=wt, j=j, kc=kc, t0=t0: e.matmul(
                            pt[:, 0:SEG], lhsT=wt[:, kc, j * 128:(j + 1) * 128], rhs=hT[:, kc, t0:t0 + SEG],
                            start=(kc == 0), stop=(kc == KC - 1)), reads=[Bw, BhT], writes=[Bp])
                    if ct < 10:
                        gi = 0 if ct < 8 else 1
                        sq, Bsq = tmp_ring.next()
                        P.op("act", lambda e, sq=sq, pt=pt: e.activation(out=sq[:, :], in_=pt[:, 0:SEG], func=AF.Square),
                             reads=[Bp], writes=[Bsq])
                        pa, Bpa = ps_aux.next()
                        P.op("pe", lambda e, pa=pa, sq=sq: e.matmul(pa[:, 0:SEG], lhsT=ones_f[:, :], rhs=sq[:, :],
                                                                    start=True, stop=True), reads=[Bsq, Bones], writes=[Bpa])
                        rstd, Br = rstd_ring.next()
                        P.op("act", lambda e, rstd=rstd, pa=pa: e.activation(out=rstd[:, :], in_=pa[:, 0:SEG], func=AF.Sqrt,
                                                                             scale=1.0 / 128, bias=eps_t[:, 0:1]), reads=[Bpa], writes=[Br])
                        P.op("dve", lambda e, rstd=rstd: e.reciprocal(out=rstd[:, :], in_=rstd[:, :]), reads=[Br], writes=[Br])
                        xn, Bxn = tmp_ring.next()
                        P.op("dve", lambda e, xn=xn, pt=pt, rstd=rstd, gi=gi: e.scalar_tensor_tensor(
                            out=xn[:, :], in0=pt[:, 0:SEG], scalar=qk[:, gi:gi + 1], in1=rstd[:, :], op0=ALU.mult, op1=ALU.mult),
                            reads=[Bp, Bqk, Br], writes=[Bxn])
                        pa2, Bpa2 = ps_aux.next()
                        P.op("pe", lambda e, pa2=pa2, xn=xn: e.matmul(pa2[:, 0:SEG], lhsT=pm[:, :], rhs=xn[:, :],
                                                                      start=True, stop=True), reads=[Bxn, Bpm], writes=[Bpa2])
                        t1, Bt1 = tmp_ring.next()
                        P.op("pool", lambda e, t1=t1, xn=xn, t0=t0: e.tensor_tensor(out=t1[:, :], in0=xn[:, :], in1=cs[:, t0:t0 + SEG],
                                                                                   op=ALU.mult), reads=[Bxn, Bcs], writes=[Bt1])
                        t2, Bt2 = tmp_ring.next()
                        P.op("dve", lambda e, t2=t2, pa2=pa2, t0=t0: e.tensor_tensor(out=t2[:, :], in0=pa2[:, 0:SEG], in1=sn[:, t0:t0 + SEG],
                                                                                    op=ALU.mult), reads=[Bpa2, Bsn], writes=[Bt2])
                        ob, Bob = stg_b.next()
                        P.op("pool", lambda e, ob=ob, t1=t1, t2=t2: e.tensor_tensor(out=ob[:, :], in0=t1[:, :], in1=t2[:, :], op=ALU.add),
                             reads=[Bt1, Bt2], writes=[Bob])
                        if ct < 8:
                            P.dma("sp", lambda e, ob=ob, ct=ct, t0=t0: e.dma_start(out=o_q[ct, :, t0:t0 + SEG], in_=ob[:, :]), Boq, reads=[Bob])
                        else:
                            P.dma("sp", lambda e, ob=ob, ct=ct, t0=t0: e.dma_start(out=o_k[ct - 8, :, t0:t0 + SEG], in_=ob[:, :]), Bok, reads=[Bob])
                    elif ct < 20:
                        of, Bof = stg_f.next()
                        P.op("act", lambda e, of=of, pt=pt: e.activation(out=of[:, :], in_=pt[:, 0:SEG], func=AF.Copy),
                             reads=[Bp], writes=[Bof])
                        P.dma("sp", lambda e, of=of, ct=ct, t0=t0: e.dma_start(out=o_u[ct - 12, :, t0:t0 + SEG], in_=of[:, :]), Bou, reads=[Bof])
                    elif ct < 28:
                        sg, Bsg = sig_tiles[ct]
                        of, Bof = stg_f.next()
                        P.op("dve", lambda e, of=of, pt=pt, sg=sg, t0=t0: e.tensor_tensor(out=of[:, :], in0=pt[:, 0:SEG], in1=sg[:, t0:t0 + SEG],
                                                                                         op=ALU.mult), reads=[Bp, Bsg], writes=[Bof])
                        P.dma("sp", lambda e, of=of, ct=ct, t0=t0: e.dma_start(out=o_uc[ct - 20, :, t0:t0 + SEG], in_=of[:, :]), Bouc, reads=[Bof])
                    elif ct < 36:
                        P.op("act", lambda e, sg=sg, pt=pt, t0=t0: e.activation(out=sg[:, t0:t0 + SEG], in_=pt[:, 0:SEG], func=AF.Sigmoid),
                             reads=[Bp], writes=[Bsg])
                    else:
                        ob, Bob = stg_b.next()
                        P.op("act", lambda e, ob=ob, pt=pt, ct=ct: e.activation(out=ob[:, :], in_=pt[:, 0:SEG], func=AF.Sigmoid,
                                                                               bias=bg[:, ct - 36:ct - 35]), reads=[Bp, Bbg], writes=[Bob])
                        P.dma("sp", lambda e, ob=ob, ct=ct, t0=t0: e.dma_start(out=o_g[ct - 36, :, t0:t0 + SEG], in_=ob[:, :]), Bog, reads=[Bob])
            if g0 == 10:
                vstg = cx.ring(2, [128, 256], BF16)
                for tb in range(17):
                    n = 128 if tb < 16 else 64
                    pt, Bp = ps_mm.next()
                    for kc in range(KC):
                        P.op("pe", lambda e, pt=pt, wt=wt, kc=kc, tb=tb, n=n: e.matmul(
                            pt[0:n, 0:256], lhsT=hT[:, kc, tb * 128:tb * 128 + n], rhs=wt[:, kc, 0:256],
                            start=(kc == 0), stop=(kc == KC - 1)), reads=[Bw, BhT], writes=[Bp])
                    vs, Bvs = vstg.next()
                    P.op("act", lambda e, vs=vs, pt=pt, n=n: e.activation(out=vs[0:n, :], in_=pt[0:n, 0:256], func=AF.Copy),
                         reads=[Bp], writes=[Bvs])
                    P.dma("sp", lambda e, vs=vs, tb=tb, n=n: e.dma_start(out=o_v[tb * 128:tb * 128 + n, :], in_=vs[0:n, :]), Bov, reads=[Bvs])
        cx.finish([Boq, Bok, Bov, Bou, Bouc, Bog])
    return nc, P


NKEY = 8448; NKT = 66

def build_stage_b1(nc=None, io=None, pool=None):
    nc = nc or bass.Bass("TRN2", target_bir_lowering=False)
    cx = Ctx(nc, io, pool); P = cx.P
    qT = cx.din("qT", [8, 128, NT], BF16)
    kT = cx.din("kT", [2, 128, NKEY], BF16)
    vv = cx.din("v", [NKEY, 256], BF16)
    o_a, Boa = cx.dout("attT", [8, 128, NT], BF16)
    scale = 1.0 / math.sqrt(128.0)
    with cx.st:
        qs, Bqs = cx.sb([128, 8, NT], BF16)
        ks, Bks = cx.sb([128, 2, NKEY], BF16)
        vs, Bvs = cx.sb([128, NKT, 256], BF16)
        ones_b, Bones = cx.sb([128, 128], BF16)
        P.op("pool", lambda e: e.memset(ones_b[:, :], 1.0), writes=[Bones])
        P.dma("sp", lambda e: e.dma_start(out=qs[:, :, :], in_=qT.rearrange("h p t -> p h t")), Bqs)
        for h in range(2):
            P.dma("sp", lambda e, h=h: e.dma_start(out=ks[:, h, :], in_=kT[h, :, :]), Bks)
        P.dma("sp", lambda e: e.dma_start(out=vs[:, :, :], in_=vv.rearrange("(kt p) c -> p kt c", p=128)), Bvs)
        ps_s = cx.ring(4, [128, 512], F32, psum=True)
        ps_o = cx.ring(2, [128, 512], F32, psum=True)
        ps_d = cx.ring(2, [128, 512], F32, psum=True)
        pT = cx.ring(4, [128, 512], BF16)
        rc = cx.ring(2, [128, 512], F32)
        ob = cx.ring(2, [128, 512], BF16)
        qtiles = [(i * 512, 512, list(range(NKT))) for i in range(4)] + [(NLAT, NCTX, [64, 65])]
        for h in range(8):
            kv = h // 4
            for (q0, ql, kts) in qtiles:
                po, Bpo = ps_o.next(); pd, Bpd = ps_d.next()
                SK = 2
                stiles = {}

                def issue_qk(i):
                    kt = kts[i]
                    pss, Bps = ps_s.next()
                    P.op("pe", lambda e, pss=pss, kt=kt, ql=ql, kv=kv, h=h, q0=q0: e.matmul(
                        pss[:, 0:ql], lhsT=ks[:, kv, kt * 128:(kt + 1) * 128], rhs=qs[:, h, q0:q0 + ql], start=True, stop=True),
                        reads=[Bks, Bqs], writes=[Bps])
                    stiles[i] = (pss, Bps)

                for i in range(min(SK, len(kts))):
                    issue_qk(i)
                for i, kt in enumerate(kts):
                    if i + SK < len(kts):
                        issue_qk(i + SK)
                    pss, Bps = stiles.pop(i)
                    pt, Bpt = pT.next()
                    P.op("act", lambda e, pt=pt, pss=pss, ql=ql: e.activation(out=pt[:, 0:ql], in_=pss[:, 0:ql], func=AF.Exp, scale=scale),
                         reads=[Bps], writes=[Bpt])
                    first = (i == 0); last = (i == len(kts) - 1)
                    P.op("pe", lambda e, pt=pt, kt=kt, first=first, last=last, po=po, ql=ql, kv=kv: e.matmul(
                        po[:, 0:ql], lhsT=vs[:, kt, kv * 128:(kv + 1) * 128], rhs=pt[:, 0:ql], start=first, stop=last),
                        reads=[Bvs, Bpt], writes=[Bpo])
                    P.op("pe", lambda e, pt=pt, first=first, last=last, pd=pd, ql=ql: e.matmul(
                        pd[:, 0:ql], lhsT=ones_b[:, :], rhs=pt[:, 0:ql], start=first, stop=last),
                        reads=[Bones, Bpt], writes=[Bpd])
                r, Br = rc.next()
                P.op("dve", lambda e, r=r, pd=pd, ql=ql: e.reciprocal(out=r[:, 0:ql], in_=pd[:, 0:ql]), reads=[Bpd], writes=[Br])
                o, Bo = ob.next()
                P.op("dve", lambda e, o=o, po=po, r=r, ql=ql: e.tensor_tensor(out=o[:, 0:ql], in0=po[:, 0:ql], in1=r[:, 0:ql], op=ALU.mult),
                     reads=[Bpo, Br], writes=[Bo])
                P.dma("sp", lambda e, o=o, h=h, q0=q0, ql=ql: e.dma_start(out=o_a[h, :, q0:q0 + ql], in_=o[:, 0:ql]), Boa, reads=[Bo])
        cx.finish([Boa])
    return nc, P


TT = 512

def build_stage_b2(nc=None, io=None, pool=None):
    nc = nc or bass.Bass("TRN2", target_bir_lowering=False)
    cx = Ctx(nc, io, pool); P = cx.P
    uT = cx.din("uT", [128, 2, NKEY])
    rowp = cx.din("rowp", [2, 3, 128, 128])
    colp = cx.din("colp", [2, 3, 128, 4])
    BT = cx.din("BT", [2, 2, 128, 128])
    CT = cx.din("CT", [2, 2, 128, 4, 128])
    dvec = cx.din("dvec", [128, 1])
    o_y, Boy = cx.dout("ysT", [128, 2, NKEY])
    with cx.st:
        halfpi, Bhp = cx.sb([128, 1], F32)
        P.op("pool", lambda e: e.memset(halfpi[:, :], math.pi / 2), writes=[Bhp])
        dv, Bdv = cx.sb([128, 1], F32)
        P.dma("sp", lambda e: e.dma_start(out=dv[:, :], in_=dvec[:, :]), Bdv)
        cnt = [0]
        def tmp(shape, dt=F32):
            return cx.sb(shape, dt)

        def ew(eng, fn, reads, writes):
            P.op(eng, fn, reads=reads, writes=writes)

        def cos_sin(theta, Bth, W):
            s, Bs = tmp([128, W]); c, Bc = tmp([128, W]); t1, Bt1 = tmp([128, W]); t2, Bt2 = tmp([128, W])
            ew("act", lambda e: e.activation(out=s[:, :], in_=theta[:, :], func=AF.Sin, scale=1.0 / 16), [Bth], [Bs])
            ew("act", lambda e: e.activation(out=c[:, :], in_=theta[:, :], func=AF.Sin, scale=1.0 / 16, bias=halfpi[:, 0:1]), [Bth, Bhp], [Bc])
            for _ in range(4):
                ew("dve", lambda e: e.tensor_tensor(out=t1[:, :], in0=c[:, :], in1=c[:, :], op=ALU.mult), [Bc], [Bt1])
                ew("dve", lambda e: e.tensor_tensor(out=t2[:, :], in0=s[:, :], in1=s[:, :], op=ALU.mult), [Bs], [Bt2])
                ew("dve", lambda e: e.scalar_tensor_tensor(out=s[:, :], in0=s[:, :], scalar=2.0, in1=c[:, :], op0=ALU.mult, op1=ALU.mult),
                   [Bs, Bc], [Bs])
                ew("dve", lambda e: e.tensor_tensor(out=c[:, :], in0=t1[:, :], in1=t2[:, :], op=ALU.subtract), [Bt1, Bt2], [Bc])
            return (c, Bc), (s, Bs)

        WBre = []; WBim = []; CTre = []; CTimN = []; Ec = []; Es = []; rcol = []
        for d in range(2):
            pr_, Bpr_ = tmp([128, 3, 128])
            P.dma("sp", lambda e, d=d, pr_=pr_: e.dma_start(out=pr_[:, :, :], in_=rowp[d].rearrange("k p n -> p k n")), Bpr_)
            bt, Bbt = tmp([128, 2, 128])
            P.dma("sp", lambda e, d=d, bt=bt: e.dma_start(out=bt[:, :, :], in_=BT[d].rearrange("k p n -> p k n")), Bbt)
            dt_, Bdt = tmp([128, 128]); ard, Bard = tmp([128, 128]); th, Bth = tmp([128, 128]); mag, Bmag = tmp([128, 128])
            ew("act", lambda e, dt_=dt_, pr_=pr_: e.activation(out=dt_[:, :], in_=pr_[:, 2, :], func=AF.Exp), [Bpr_], [Bdt])
            ew("dve", lambda e, ard=ard, pr_=pr_, dt_=dt_: e.tensor_tensor(out=ard[:, :], in0=pr_[:, 0, :], in1=dt_[:, :], op=ALU.mult), [Bpr_, Bdt], [Bard])
            ew("dve", lambda e, th=th, pr_=pr_, dt_=dt_: e.tensor_tensor(out=th[:, :], in0=pr_[:, 1, :], in1=dt_[:, :], op=ALU.mult), [Bpr_, Bdt], [Bth])
            ew("act", lambda e, mag=mag, ard=ard: e.activation(out=mag[:, :], in_=ard[:, :], func=AF.Exp), [Bard], [Bmag])
            (c, Bc), (s, Bs) = cos_sin(th, Bth, 128)
            lr, Blr = tmp([128, 128]); li, Bli = tmp([128, 128]); den, Bden = tmp([128, 128]); t1, Bt1 = tmp([128, 128]); t2, Bt2 = tmp([128, 128])
            fre, Bfre = tmp([128, 128]); fim, Bfim = tmp([128, 128])
            ew("dve", lambda e, lr=lr, mag=mag, c=c: e.tensor_tensor(out=lr[:, :], in0=mag[:, :], in1=c[:, :], op=ALU.mult), [Bmag, Bc], [Blr])
            ew("dve", lambda e, li=li, mag=mag, s=s: e.tensor_tensor(out=li[:, :], in0=mag[:, :], in1=s[:, :], op=ALU.mult), [Bmag, Bs], [Bli])
            ew("dve", lambda e, lr=lr: e.tensor_scalar(out=lr[:, :], in0=lr[:, :], scalar1=-1.0, scalar2=None, op0=ALU.add), [Blr], [Blr])
            ew("dve", lambda e, t1=t1, pr_=pr_: e.tensor_tensor(out=t1[:, :], in0=pr_[:, 0, :], in1=pr_[:, 0, :], op=ALU.mult), [Bpr_], [Bt1])
            ew("dve", lambda e, t2=t2, pr_=pr_: e.tensor_tensor(out=t2[:, :], in0=pr_[:, 1, :], in1=pr_[:, 1, :], op=ALU.mult), [Bpr_], [Bt2])
            ew("dve", lambda e, den=den, t1=t1, t2=t2: e.tensor_tensor(out=den[:, :], in0=t1[:, :], in1=t2[:, :], op=ALU.add), [Bt1, Bt2], [Bden])
            ew("dve", lambda e, den=den: e.reciprocal(out=den[:, :], in_=den[:, :]), [Bden], [Bden])
            ew("dve", lambda e, t1=t1, lr=lr, pr_=pr_: e.tensor_tensor(out=t1[:, :], in0=lr[:, :], in1=pr_[:, 0, :], op=ALU.mult), [Blr, Bpr_], [Bt1])
            ew("dve", lambda e, t2=t2, li=li, pr_=pr_: e.tensor_tensor(out=t2[:, :], in0=li[:, :], in1=pr_[:, 1, :], op=ALU.mult), [Bli, Bpr_], [Bt2])
            ew("dve", lambda e, t1=t1, t2=t2: e.tensor_tensor(out=t1[:, :], in0=t1[:, :], in1=t2[:, :], op=ALU.add), [Bt1, Bt2], [Bt1])
            ew("dve", lambda e, fre=fre, t1=t1, den=den: e.tensor_tensor(out=fre[:, :], in0=t1[:, :], in1=den[:, :], op=ALU.mult), [Bt1, Bden], [Bfre])
            ew("dve", lambda e, t1=t1, li=li, pr_=pr_: e.tensor_tensor(out=t1[:, :], in0=li[:, :], in1=pr_[:, 0, :], op=ALU.mult), [Bli, Bpr_], [Bt1])
            ew("dve", lambda e, t2=t2, lr=lr, pr_=pr_: e.tensor_tensor(out=t2[:, :], in0=lr[:, :], in1=pr_[:, 1, :], op=ALU.mult), [Blr, Bpr_], [Bt2])
            ew("dve", lambda e, t1=t1, t2=t2: e.tensor_tensor(out=t1[:, :], in0=t1[:, :], in1=t2[:, :], op=ALU.subtract), [Bt1, Bt2], [Bt1])
            ew("dve", lambda e, fim=fim, t1=t1, den=den: e.tensor_tensor(out=fim[:, :], in0=t1[:, :], in1=den[:, :], op=ALU.mult), [Bt1, Bden], [Bfim])
            wre, Bwre = tmp([128, 128], BF16); wim, Bwim = tmp([128, 128], BF16)
            ew("dve", lambda e, t1=t1, fre=fre, bt=bt: e.tensor_tensor(out=t1[:, :], in0=fre[:, :], in1=bt[:, 0, :], op=ALU.mult), [Bfre, Bbt], [Bt1])
            ew("dve", lambda e, t2=t2, fim=fim, bt=bt: e.tensor_tensor(out=t2[:, :], in0=fim[:, :], in1=bt[:, 1, :], op=ALU.mult), [Bfim, Bbt], [Bt2])
            ew("dve", lambda e, wre=wre, t1=t1, t2=t2: e.tensor_tensor(out=wre[:, :], in0=t1[:, :], in1=t2[:, :], op=ALU.subtract), [Bt1, Bt2], [Bwre])
            ew("dve", lambda e, t1=t1, fre=fre, bt=bt: e.tensor_tensor(out=t1[:, :], in0=fre[:, :], in1=bt[:, 1, :], op=ALU.mult), [Bfre, Bbt], [Bt1])
            ew("dve", lambda e, t2=t2, fim=fim, bt=bt: e.tensor_tensor(out=t2[:, :], in0=fim[:, :], in1=bt[:, 0, :], op=ALU.mult), [Bfim, Bbt], [Bt2])
            ew("dve", lambda e, wim=wim, t1=t1, t2=t2: e.tensor_tensor(out=wim[:, :], in0=t1[:, :], in1=t2[:, :], op=ALU.add), [Bt1, Bt2], [Bwim])
            WBre.append((wre, Bwre)); WBim.append((wim, Bwim))
            ctf, Bctf = tmp([128, 2, 4, 128])
            P.dma("sp", lambda e, d=d, ctf=ctf: e.dma_start(out=ctf[:, :, :, :], in_=CT[d].rearrange("k p q n -> p k q n")), Bctf)
            cre, Bcre = tmp([128, 4, 128], BF16); cim, Bcim = tmp([128, 4, 128], BF16)
            ew("dve", lambda e, cre=cre, ctf=ctf: e.tensor_copy(out=cre[:, :, :], in_=ctf[:, 0, :, :]), [Bctf], [Bcre])
            ew("dve", lambda e, cim=cim, ctf=ctf: e.tensor_scalar(out=cim[:, :, :], in0=ctf[:, 1, :, :], scalar1=-1.0, scalar2=None, op0=ALU.mult), [Bctf], [Bcim])
            CTre.append((cre, Bcre)); CTimN.append((cim, Bcim))
            pc, Bpc = tmp([128, 3, 4])
            P.dma("sp", lambda e, d=d, pc=pc: e.dma_start(out=pc[:, :, :], in_=colp[d].rearrange("k p q -> p k q")), Bpc)
            dtc, Bdtc = tmp([128, 4]); a2, Ba2 = tmp([128, 4]); thc, Bthc = tmp([128, 4]); r_, Br_ = tmp([128, 4])
            ew("act", lambda e, dtc=dtc, pc=pc: e.activation(out=dtc[:, :], in_=pc[:, 2, :], func=AF.Exp), [Bpc], [Bdtc])
            ew("dve", lambda e, a2=a2, pc=pc, dtc=dtc: e.tensor_tensor(out=a2[:, :], in0=pc[:, 0, :], in1=dtc[:, :], op=ALU.mult), [Bpc, Bdtc], [Ba2])
            ew("dve", lambda e, thc=thc, pc=pc, dtc=dtc: e.tensor_tensor(out=thc[:, :], in0=pc[:, 1, :], in1=dtc[:, :], op=ALU.mult), [Bpc, Bdtc], [Bthc])
            ew("act", lambda e, r_=r_, a2=a2: e.activation(out=r_[:, :], in_=a2[:, :], func=AF.Exp), [Ba2], [Br_])
            rcol.append((r_, Br_))
            (cc, Bcc), (sc, Bsc) = cos_sin(thc, Bthc, 4)
            ec, Bec = tmp([128, 4, TT]); es, Bes = tmp([128, 4, TT]); tt, Btt = tmp([128, TT])
            for q in range(4):
                ew("dve", lambda e, ec=ec, cc=cc, q=q: e.tensor_copy(out=ec[:, q, 0:1], in_=cc[:, q:q + 1]), [Bcc], [Bec])
                ew("dve", lambda e, es=es, sc=sc, q=q: e.tensor_scalar(out=es[:, q, 0:1], in0=sc[:, q:q + 1], scalar1=-1.0, scalar2=None, op0=ALU.mult), [Bsc], [Bes])
                m = 1
                while m < TT:
                    ew("dve", lambda e, q=q, m=m, tt=tt, es=es, ec=ec: e.tensor_scalar(out=tt[:, 0:m], in0=es[:, q, 0:m], scalar1=es[:, q, m - 1:m], scalar2=None, op0=ALU.mult), [Bes], [Btt])
                    ew("dve", lambda e, q=q, m=m, tt=tt, es=es, ec=ec: e.scalar_tensor_tensor(out=ec[:, q, m:2 * m], in0=ec[:, q, 0:m], scalar=ec[:, q, m - 1:m], in1=tt[:, 0:m],
                                                                         op0=ALU.mult, op1=ALU.subtract), [Bec, Btt], [Bec])
                    ew("dve", lambda e, q=q, m=m, tt=tt, es=es, ec=ec: e.tensor_scalar(out=tt[:, 0:m], in0=es[:, q, 0:m], scalar1=ec[:, q, m - 1:m], scalar2=None, op0=ALU.mult), [Bes, Bec], [Btt])
                    ew("dve", lambda e, q=q, m=m, tt=tt, es=es, ec=ec: e.scalar_tensor_tensor(out=es[:, q, m:2 * m], in0=ec[:, q, 0:m], scalar=es[:, q, m - 1:m], in1=tt[:, 0:m],
                                                                         op0=ALU.mult, op1=ALU.add), [Bec, Bes, Btt], [Bes])
                    m *= 2
            Ec.append((ec, Bec)); Es.append((es, Bes))
        yf, Byf = cx.sb([128, NKEY], F32)
        u_ring = cx.ring(2, [128, TT], F32); ub_ring = cx.ring(2, [128, TT], BF16)
        ps_r = cx.ring(2, [128, 512], F32, psum=True); ps_i = cx.ring(2, [128, 512], F32, psum=True)
        ps_y = cx.ring(2, [128, 512], F32, psum=True)
        tr = {k: cx.ring(2, [128, TT], F32) for k in ("m1", "m2", "m3", "m4", "gpr", "gpi", "gr", "gi")}
        hb_r = cx.ring(2, [128, TT], BF16); hb_i = cx.ring(2, [128, TT], BF16)
        ostg = cx.ring(2, [128, TT], F32)
        tiles = [(0, 256)] + [(256 + 512 * i, 512) for i in range(16)]
        for b in range(2):
            for d in range(2):
                car_r, Bcr = cx.sb([128, 4], F32); car_i, Bci = cx.sb([128, 4], F32)
                P.op("pool", lambda e, car_r=car_r: e.memset(car_r[:, :], 0.0), writes=[Bcr])
                P.op("pool", lambda e, car_i=car_i: e.memset(car_i[:, :], 0.0), writes=[Bci])
                order = tiles if d == 0 else [tiles[0]] + tiles[:0:-1]
                (wre, Bwre), (wim, Bwim) = WBre[d], WBim[d]
                (cre, Bcre), (cim, Bcim) = CTre[d], CTimN[d]
                (ec, Bec), (es, Bes) = Ec[d], Es[d]
                (r_, Br_) = rcol[d]
                for (t0, n) in order:
                    uf, Buf_ = u_ring.next()
                    P.dma("sp", lambda e, uf=uf, b=b, t0=t0, n=n: e.dma_start(out=uf[:, 0:n], in_=uT[:, b, t0:t0 + n]), Buf_)
                    ub, Bub = ub_ring.next()
                    if d == 0:
                        P.op("act", lambda e, ub=ub, uf=uf, n=n: e.activation(out=ub[:, 0:n], in_=uf[:, 0:n], func=AF.Copy), reads=[Buf_], writes=[Bub])
                    else:
                        P.op("dve", lambda e, ub=ub, uf=uf, n=n: e.tensor_copy(out=ub[:, 0:n], in_=uf[:, n - 1::-1] if False else uf[:, 0:n][:, ::-1]),
                             reads=[Buf_], writes=[Bub])
                    py, Bpy = ps_y.next()
                    def tt2(eng, o, a_, tab, op, n=n):
                        (ot, Bo), (at, Ba), (tb, Bt) = o, a_, tab
                        P.op(eng, lambda e: e.tensor_tensor(out=ot[:, 0:n], in0=at[:, 0:n], in1=tb, op=op), reads=[Ba, Bt], writes=[Bo])

                    def tt3(eng, o, a_, bsrc, op, n=n):
                        (ot, Bo), (at, Ba), (bt2, Bb2) = o, a_, bsrc
                        P.op(eng, lambda e: e.tensor_tensor(out=ot[:, 0:n], in0=at[:, 0:n], in1=bt2[:, 0:n], op=op), reads=[Ba, Bb2], writes=[Bo])

                    for g0 in (0, 2):
                        S = {}
                        for q in (g0, g0 + 1):
                            pr, Bpr = ps_r.next(); pi, Bpi = ps_i.next()
                            P.op("pe", lambda e, pr=pr, q=q, ub=ub, n=n, wre=wre: e.matmul(pr[:, 0:n], lhsT=wre[32 * q:32 * q + 32, :], rhs=ub[32 * q:32 * q + 32, 0:n],
                                                                                       start=True, stop=True, tile_position=(32 * q, 0)), reads=[Bwre, Bub], writes=[Bpr])
                            P.op("pe", lambda e, pi=pi, q=q, ub=ub, n=n, wim=wim: e.matmul(pi[:, 0:n], lhsT=wim[32 * q:32 * q + 32, :], rhs=ub[32 * q:32 * q + 32, 0:n],
                                                                                       start=True, stop=True, tile_position=(32 * q, 0)), reads=[Bwim, Bub], writes=[Bpi])
                            S[q] = dict(pr=(pr, Bpr), pi=(pi, Bpi), T={k: tr[k].next() for k in tr}, ecq=(ec[:, q, 0:n], Bec), esq=(es[:, q, 0:n], Bes),
                                        hr=hb_r.next(), hi=hb_i.next())
                        for q in (g0, g0 + 1):
                            T = S[q]["T"]
                            tt2("dve", T["m1"], S[q]["pr"], S[q]["ecq"], ALU.mult)
                            tt2("dve", T["m2"], S[q]["pi"], S[q]["esq"], ALU.mult)
                            tt2("dve", T["m3"], S[q]["pi"], S[q]["ecq"], ALU.mult)
                            tt2("dve", T["m4"], S[q]["pr"], S[q]["esq"], ALU.mult)
                        for q in (g0, g0 + 1):
                            T = S[q]["T"]
                            tt3("pool", T["gpr"], T["m1"], T["m2"], ALU.subtract)
                            tt3("pool", T["gpi"], T["m3"], T["m4"], ALU.add)
                        for q in (g0, g0 + 1):
                            T = S[q]["T"]
                            for (go, gp, car, Bcar) in ((T["gr"], T["gpr"], car_r, Bcr), (T["gi"], T["gpi"], car_i, Bci)):
                                (got, Bgo), (gpt, Bgp) = go, gp
                                P.op("dve", lambda e, got=got, gpt=gpt, car=car, q=q, n=n, r_=r_: e.tensor_tensor_scan(
                                    out=got[:, 0:n], data0=r_[:, q:q + 1].to_broadcast([128, n]), data1=gpt[:, 0:n], initial=car[:, q:q + 1],
                                    op0=ALU.mult, op1=ALU.add), reads=[Br_, Bgp, Bcar], writes=[Bgo])
                        for q in (g0, g0 + 1):
                            T = S[q]["T"]
                            tt2("pool", T["m1"], T["gr"], S[q]["ecq"], ALU.mult)
                            tt2("dve", T["m2"], T["gi"], S[q]["esq"], ALU.mult)
                            tt2("pool", T["m3"], T["gi"], S[q]["ecq"], ALU.mult)
                            tt2("pool", T["m4"], T["gr"], S[q]["esq"], ALU.mult)
                        for q in (g0, g0 + 1):
                            T = S[q]["T"]; hr, Bhr = S[q]["hr"]; hi, Bhi = S[q]["hi"]
                            tt3("dve", (hr, Bhr), T["m1"], T["m2"], ALU.add)
                            tt3("pool", (hi, Bhi), T["m3"], T["m4"], ALU.subtract)
                            (m1, Bm1), (m2, Bm2), (m3, Bm3), (m4, Bm4) = T["m1"], T["m2"], T["m3"], T["m4"]
                            P.op("pool", lambda e, car_r=car_r, m1=m1, m2=m2, q=q, n=n: e.tensor_tensor(out=car_r[:, q:q + 1], in0=m1[:, n - 1:n], in1=m2[:, n - 1:n], op=ALU.add),
                                 reads=[Bm1, Bm2], writes=[Bcr])
                            P.op("pool", lambda e, car_i=car_i, m3=m3, m4=m4, q=q, n=n: e.tensor_tensor(out=car_i[:, q:q + 1], in0=m3[:, n - 1:n], in1=m4[:, n - 1:n], op=ALU.subtract),
                                 reads=[Bm3, Bm4], writes=[Bci])
                        for q in (g0, g0 + 1):
                            hr, Bhr = S[q]["hr"]; hi, Bhi = S[q]["hi"]
                            P.op("pe", lambda e, py=py, cre=cre, hr=hr, q=q, n=n: e.matmul(py[:, 0:n], lhsT=cre[:, q, :], rhs=hr[:, 0:n], start=(q == 0), stop=False),
                                 reads=[Bcre, Bhr], writes=[Bpy])
                            P.op("pe", lambda e, py=py, cim=cim, hi=hi, q=q, n=n: e.matmul(py[:, 0:n], lhsT=cim[:, q, :], rhs=hi[:, 0:n], start=False, stop=(q == 3)),
                                 reads=[Bcim, Bhi], writes=[Bpy])
                    if d == 0:
                        P.op("act", lambda e, py=py, t0=t0, n=n: e.activation(out=yf[:, t0:t0 + n], in_=py[:, 0:n], func=AF.Copy), reads=[Bpy], writes=[Byf])
                    else:
                        o1, Bo1 = ostg.next()
                        P.op("dve", lambda e, o1=o1, uf=uf, t0=t0, n=n: e.scalar_tensor_tensor(out=o1[:, 0:n], in0=uf[:, 0:n], scalar=dv[:, 0:1], in1=yf[:, t0:t0 + n],
                                                                                            op0=ALU.mult, op1=ALU.add), reads=[Buf_, Bdv, Byf], writes=[Bo1])
                        P.op("dve", lambda e, o1=o1, py=py, n=n: e.tensor_tensor(out=o1[:, 0:n], in0=o1[:, 0:n], in1=py[:, 0:n][:, ::-1], op=ALU.add),
                             reads=[Bo1, Bpy], writes=[Bo1])
                        P.dma("sp", lambda e, o1=o1, b=b, t0=t0, n=n: e.dma_start(out=o_y[:, b, t0:t0 + n], in_=o1[:, 0:n]), Boy, reads=[Bo1])
        cx.finish([Boy])
    return nc, P


CONV_PIECES = [(i * SEG, SEG) for i in range(NSEG - 1)] + [((NSEG - 1) * SEG, NLAT - (NSEG - 1) * SEG), (NLAT, NCTX)]
CONV_WOFF = []
_o = 0
for (_s, _n) in CONV_PIECES:
    CONV_WOFF.append(_o); _o += _n + 30
WTOT = _o


def build_stage_c(final=False, debug_seg=None, nc=None, io=None, pool=None):
    nc = nc or bass.Bass("TRN2", target_bir_lowering=False)
    cx = Ctx(nc, io, pool); P = cx.P
    xT = cx.din("xT", [KC, 128, NT])
    attT = cx.din("attT", [8, 128, NT], BF16)
    ysT = cx.din("ysT", [8, 128, NT])
    ucH = cx.din("ucH", [8, 128, WTOT])
    gT = cx.din("gT", [48, 128, NT], BF16)
    w_glu = cx.din("w_glu", [1024, 1024]); w_ssm_o = cx.din("w_ssm_o", [1024, D]); w_conv_o = cx.din("w_conv_o", [1024, D])
    w_attn_o = cx.din("w_attn_o", [1024, D]); w_out = cx.din("w_out", [D, D]); w1 = cx.din("w_mlp1", [D, 4 * D]); w2 = cx.din("w_mlp2", [4 * D, D])
    vec8 = cx.din("vec8", [128, 8, 4])
    convw = cx.din("convw", [128, 8, 31])
    g2T = cx.din("g2T", [128, KC])
    fgT = cx.din("fgT", [128, KC])
    mod2 = cx.din("mod2", [128, KC, 2, 4])
    o_x, Box = cx.dout("xT_out", [KC, 128, NT])
    if debug_seg is not None:
        dbg = {n: cx.dout("dbg_" + n, [128, k, SEG], dt) for n, k, dt in (("cacc", 8, F32), ("cv", 8, BF16), ("gs", 8, BF16), ("hT", 16, BF16), ("g32", 8, F32))}
    wv = lambda w: w.rearrange("(kc p) n -> p kc n", p=128)
    with cx.st:
        ones_f, Bones = cx.sb([128, 128], F32)
        P.op("pool", lambda e: e.memset(ones_f[:, :], 1.0), writes=[Bones])
        eps_t, Beps = cx.sb([128, 1], F32)
        P.op("pool", lambda e: e.memset(eps_t[:, :], EPS), writes=[Beps])
        v8, Bv8 = cx.sb([128, 8, 4], F32); cw, Bcw = cx.sb([128, 8, 31], F32)
        g2, Bg2 = cx.sb([128, KC], F32); fg, Bfg = cx.sb([128, KC], F32); m2, Bm2 = cx.sb([128, KC, 2, 4], F32)
        sc2, Bsc2 = cx.sb([128, KC, 2, 2], F32)
        for (t, B_, src) in ((v8, Bv8, vec8), (cw, Bcw, convw), (g2, Bg2, g2T), (fg, Bfg, fgT), (m2, Bm2, mod2)):
            P.dma("sp", lambda e, t=t, src=src: e.dma_start(out=t[:], in_=src), B_)
        for var in range(2):
            P.op("dve", lambda e, var=var: e.scalar_tensor_tensor(out=sc2[:, :, var, 0], in0=m2[:, :, var, 2], scalar=1.0, in1=g2[:, :],
                                                                  op0=ALU.add, op1=ALU.mult), reads=[Bm2, Bg2], writes=[Bsc2])
            P.op("dve", lambda e, var=var: e.tensor_copy(out=sc2[:, :, var, 1], in_=m2[:, :, var, 1]), reads=[Bm2], writes=[Bsc2])
        if final:
            scf, Bscf = cx.sb([128, KC, 2, 2], F32)
            P.op("pool", lambda e: e.memset(scf[:, :, :, :], 0.0), writes=[Bscf])
            for var in range(2):
                P.op("dve", lambda e, var=var: e.tensor_copy(out=scf[:, :, var, 0], in_=fg[:, :]), reads=[Bfg, Bscf], writes=[Bscf])
        xs, Bxs = cx.sb([128, KC, SEG], F32)
        hT, BhT = cx.sb([128, KC, SEG], BF16)
        hh, Bhh = cx.sb([128, 64, SEG], BF16)
        ys, Bys = cx.sb([128, 8, SEG], F32)
        gb, Bgb = cx.sb([128, 8, SEG], BF16); gs, Bgs = cx.sb([128, 8, SEG], BF16)
        cv, Bcv = cx.sb([128, 8, SEG], BF16); at, Bat = cx.sb([128, 8, SEG], BF16)
        cacc, Bcacc = cx.sb([128, 8, SEG], F32)
        ucw_ring = cx.ring(2, [128, SEG + 30], F32)
        gate_ring = cx.ring(6, [128, SEG], BF16)
        w_ring = cx.ring(4, [128, 16, 256], BF16)
        tmp_ring = cx.ring(4, [128, SEG], F32)
        rstd_ring = cx.ring(2, [128, SEG], F32)
        ps_mm = cx.ring(6, [128, 512], F32, psum=True)
        ps_aux = cx.ring(2, [128, 512], F32, psum=True)
        ostg = cx.ring(2, [128, SEG], F32) if final else None

        WSPEC = [("w_glu", w_glu, 8, 8), ("w_attn_o", w_attn_o, 8, 16), ("w_ssm_o", w_ssm_o, 8, 16), ("w_conv_o", w_conv_o, 8, 16),
                 ("w_out", w_out, 16, 16), ("w_mlp1", w1, 16, 64), ("w_mlp2", w2, 64, 16)]
        wsc = {}; BW = {}
        for (wn, W, nK, n_oc) in WSPEC:
            npc = (nK + 15) // 16
            cx.n += 1
            wsc[wn] = nc.dram_tensor("wsc_%s_%d" % (wn, cx.n), [n_oc // 2, npc, 128, 16 * 256], BF16).ap()
            BW[wn] = Buf("wsc_" + wn)
        WNAME = {id(w_glu): "w_glu", id(w_attn_o): "w_attn_o", id(w_ssm_o): "w_ssm_o", id(w_conv_o): "w_conv_o", id(w_out): "w_out", id(w1): "w_mlp1", id(w2): "w_mlp2"}

        first_seg = [True]

        def load_w(wt, Bw, W, oc2, k0, nk):
            wn = WNAME[id(W)]
            if first_seg[0]:
                P.dma("pool", lambda e: e.dma_start(out=wt[:, 0:nk, :], in_=wv(W)[:, k0:k0 + nk, oc2 * 128:oc2 * 128 + 256]), Bw)
                P.dma("sp", lambda e: e.dma_start(out=wsc[wn][oc2 // 2, k0 // 16, :, 0:nk * 256], in_=wt[:, 0:nk, :].rearrange("p k c -> p (k c)")), BW[wn], reads=[Bw])
            else:
                P.dma("sp", lambda e: e.dma_start(out=wt[:, 0:nk, :].rearrange("p k c -> p (k c)"), in_=wsc[wn][oc2 // 2, k0 // 16, :, 0:nk * 256]), Bw, reads=[BW[wn]])

        def mm_group(W, nK, n_oc, rhs_of, Brhs, epilogue):
            for oc2 in range(0, n_oc, 2):
                pss = [ps_mm.next(), ps_mm.next()]
                for k0 in range(0, nK, 16):
                    nk = min(16, nK - k0)
                    wt, Bw = w_ring.next()
                    load_w(wt, Bw, W, oc2, k0, nk)
                    for j in range(2):
                        pt, Bp = pss[j]
                        for kc in range(nk):
                            P.op("pe", lambda e, pt=pt, wt=wt, j=j, kc=kc, k0=k0: e.matmul(
                                pt[:, 0:SEG], lhsT=wt[:, kc, j * 128:(j + 1) * 128], rhs=rhs_of(k0 + kc),
                                start=(k0 + kc == 0), stop=(k0 + kc == nK - 1)), reads=[Bw, Brhs], writes=[Bp])
                for j in range(2):
                    epilogue(oc2 + j, pss[j][0], pss[j][1])

        for seg in (range(NSEG) if debug_seg is None else [debug_seg]):
            t0 = seg * SEG
            parts = [(0, NLAT - t0, 0), (NLAT - t0, SEG, 1)] if seg == NSEG - 1 else [(0, SEG, 0)]
            P.dma("sp", lambda e, t0=t0: e.dma_start(out=xs[:, :, :], in_=xT[:, :, t0:t0 + SEG].rearrange("k p t -> p k t")), Bxs)
            P.dma("sp", lambda e, t0=t0: e.dma_start(out=ys[:, :, :], in_=ysT[:, :, t0:t0 + SEG].rearrange("k p t -> p k t")), Bys)
            P.dma("sp", lambda e, t0=t0: e.dma_start(out=at[:, :, :], in_=attT[:, :, t0:t0 + SEG].rearrange("k p t -> p k t")), Bat)
            for c in range(8):
                t1, Bt1 = tmp_ring.next()
                P.op("dve", lambda e, t1=t1, c=c: e.tensor_tensor(out=t1[:, :], in0=ys[:, c, :], in1=ys[:, c, :], op=ALU.mult), reads=[Bys], writes=[Bt1])
                P.op("dve", lambda e, t1=t1: e.tensor_scalar(out=t1[:, :], in0=t1[:, :], scalar1=0.044715, scalar2=1.0, op0=ALU.mult, op1=ALU.add),
                     reads=[Bt1], writes=[Bt1])
                P.op("pool", lambda e, t1=t1, c=c: e.tensor_tensor(out=t1[:, :], in0=t1[:, :], in1=ys[:, c, :], op=ALU.mult), reads=[Bt1, Bys], writes=[Bt1])
                P.op("act", lambda e, t1=t1: e.activation(out=t1[:, :], in_=t1[:, :], func=AF.Sigmoid, scale=1.5957691216), reads=[Bt1], writes=[Bt1])
                P.op("dve", lambda e, t1=t1, c=c: e.tensor_tensor(out=ys[:, c, :], in0=ys[:, c, :], in1=t1[:, :], op=ALU.mult), reads=[Bys, Bt1], writes=[Bys])
                P.op("act", lambda e, c=c: e.activation(out=gb[:, c, :], in_=ys[:, c, :], func=AF.Copy), reads=[Bys], writes=[Bgb])
            def ep_glu(oc, pt, Bp):
                t1, Bt1 = tmp_ring.next()
                P.op("act", lambda e: e.activation(out=t1[:, :], in_=pt[:, 0:SEG], func=AF.Sigmoid, bias=v8[:, oc, 0:1]), reads=[Bp, Bv8], writes=[Bt1])
                P.op("dve", lambda e: e.tensor_tensor(out=gs[:, oc, :], in0=ys[:, oc, :], in1=t1[:, :], op=ALU.mult), reads=[Bys, Bt1], writes=[Bgs])
            mm_group(w_glu, 8, 8, lambda k: gb[:, k, :], Bgb, ep_glu)
            for c in range(8):
                for pi_, (ps0, pn) in enumerate(CONV_PIECES):
                    if not (t0 <= ps0 < t0 + SEG):
                        continue
                    o0 = ps0 - t0
                    win, Bwin = ucw_ring.next()
                    P.dma("sp", lambda e, win=win, c=c, pi_=pi_, pn=pn: e.dma_start(out=win[:, 0:pn + 30], in_=ucH[c, :, CONV_WOFF[pi_]:CONV_WOFF[pi_] + pn + 30]), Bwin)
                    a2, Ba2 = tmp_ring.next()
                    NDVE = 31
                    P.op("dve", lambda e, win=win, c=c, o0=o0, pn=pn: e.tensor_scalar(out=cacc[:, c, o0:o0 + pn], in0=win[:, 0:pn], scalar1=cw[:, c, 0:1],
                                                                                     scalar2=v8[:, c, 1:2], op0=ALU.mult, op1=ALU.add), reads=[Bwin, Bcw, Bv8], writes=[Bcacc])
                    for k in range(1, NDVE):
                        P.op("dve", lambda e, win=win, c=c, o0=o0, pn=pn, k=k: e.scalar_tensor_tensor(
                            out=cacc[:, c, o0:o0 + pn], in0=win[:, k:k + pn], scalar=cw[:, c, k:k + 1], in1=cacc[:, c, o0:o0 + pn],
                            op0=ALU.mult, op1=ALU.add), reads=[Bwin, Bcw, Bcacc], writes=[Bcacc])
            p1, Bp1 = ps_aux.next(); p2, Bp2 = ps_aux.next()
            for c in range(8):
                P.op("pe", lambda e, c=c, p1=p1: e.matmul(p1[:, 0:SEG], lhsT=ones_f[:, :], rhs=cacc[:, c, :], start=(c == 0), stop=(c == 7)), reads=[Bones, Bcacc], writes=[Bp1])
            for c in range(8):
                sq, Bsq = tmp_ring.next()
                P.op("act", lambda e, sq=sq, c=c: e.activation(out=sq[:, :], in_=cacc[:, c, :], func=AF.Square), reads=[Bcacc], writes=[Bsq])
                P.op("pe", lambda e, sq=sq, c=c, p2=p2: e.matmul(p2[:, 0:SEG], lhsT=ones_f[:, :], rhs=sq[:, :], start=(c == 0), stop=(c == 7)), reads=[Bones, Bsq], writes=[Bp2])
            mean, Bmean = rstd_ring.next(); rstd, Brstd = rstd_ring.next(); msq, Bmsq = tmp_ring.next()
            P.op("dve", lambda e, mean=mean, p1=p1: e.tensor_scalar(out=mean[:, :], in0=p1[:, 0:SEG], scalar1=1.0 / 1024, scalar2=None, op0=ALU.mult), reads=[Bp1], writes=[Bmean])
            P.op("pool", lambda e, msq=msq, mean=mean: e.tensor_tensor(out=msq[:, :], in0=mean[:, :], in1=mean[:, :], op=ALU.mult), reads=[Bmean], writes=[Bmsq])
            P.op("dve", lambda e, rstd=rstd, p2=p2, msq=msq: e.scalar_tensor_tensor(out=rstd[:, :], in0=p2[:, 0:SEG], scalar=1.0 / 1024, in1=msq[:, :], op0=ALU.mult, op1=ALU.subtract),
                 reads=[Bp2, Bmsq], writes=[Brstd])
            P.op("act", lambda e, rstd=rstd: e.activation(out=rstd[:, :], in_=rstd[:, :], func=AF.Sqrt, bias=eps_t[:, 0:1]), reads=[Brstd, Beps], writes=[Brstd])
            P.op("dve", lambda e, rstd=rstd: e.reciprocal(out=rstd[:, :], in_=rstd[:, :]), reads=[Brstd], writes=[Brstd])
            for c in range(8):
                t1, Bt1 = tmp_ring.next()
                P.op("dve", lambda e, t1=t1, c=c, mean=mean: e.tensor_tensor(out=t1[:, :], in0=cacc[:, c, :], in1=mean[:, :], op=ALU.subtract), reads=[Bcacc, Bmean], writes=[Bt1])
                P.op("pool", lambda e, t1=t1, rstd=rstd: e.tensor_tensor(out=t1[:, :], in0=t1[:, :], in1=rstd[:, :], op=ALU.mult), reads=[Bt1, Brstd], writes=[Bt1])
                P.op("pool", lambda e, t1=t1, c=c: e.tensor_scalar(out=t1[:, :], in0=t1[:, :], scalar1=v8[:, c, 2:3], scalar2=v8[:, c, 3:4], op0=ALU.mult, op1=ALU.add),
                     reads=[Bt1, Bv8], writes=[Bt1])
                P.op("act", lambda e, t1=t1, c=c: e.activation(out=cv[:, c, :], in_=t1[:, :], func=AF.Silu), reads=[Bt1], writes=[Bcv])
            branches = [(w_attn_o, at, Bat, 0), (w_ssm_o, gs, Bgs, 16), (w_conv_o, cv, Bcv, 32)]
            for oc2 in range(0, 16, 2):
                wts = []
                for (W, src, Bsrc, goff) in branches:
                    wt, Bw = w_ring.next()
                    load_w(wt, Bw, W, oc2, 0, 8)
                    wts.append((wt, Bw))
                for j in range(2):
                    oc = oc2 + j
                    acc = None
                    for bi, (W, src, Bsrc, goff) in enumerate(branches):
                        wt, Bw = wts[bi]
                        pt, Bp = ps_mm.next()
                        for kc in range(8):
                            P.op("pe", lambda e, pt=pt, wt=wt, j=j, kc=kc, src=src: e.matmul(pt[:, 0:SEG], lhsT=wt[:, kc, j * 128:(j + 1) * 128], rhs=src[:, kc, :],
                                                                                            start=(kc == 0), stop=(kc == 7)), reads=[Bw, Bsrc], writes=[Bp])
                        gt_, Bgt = gate_ring.next()
                        P.dma("sp", lambda e, gt_=gt_, goff=goff, oc=oc, t0=t0: e.dma_start(out=gt_[:, :], in_=gT[goff + oc, :, t0:t0 + SEG]), Bgt)
                        t1, Bt1 = tmp_ring.next()
                        P.op("dve", lambda e, t1=t1, pt=pt, gt_=gt_: e.tensor_tensor(out=t1[:, :], in0=pt[:, 0:SEG], in1=gt_[:, :], op=ALU.mult), reads=[Bp, Bgt], writes=[Bt1])
                        if acc is None:
                            acc = (t1, Bt1)
                        elif bi == 1:
                            a_, Ba_ = acc
                            P.op("dve", lambda e, a_=a_, t1=t1: e.tensor_tensor(out=a_[:, :], in0=a_[:, :], in1=t1[:, :], op=ALU.add), reads=[Ba_, Bt1], writes=[Ba_])
                        else:
                            a_, Ba_ = acc
                            P.op("dve", lambda e, a_=a_, t1=t1, oc=oc: e.tensor_tensor(out=hT[:, oc, :], in0=a_[:, :], in1=t1[:, :], op=ALU.add), reads=[Ba_, Bt1], writes=[BhT])
            if debug_seg is not None:
                for n, (t_, B_) in (("cacc", (cacc, Bcacc)), ("cv", (cv, Bcv)), ("gs", (gs, Bgs)), ("hT", (hT, BhT)), ("g32", (ys, Bys))):
                    P.dma("sp", lambda e, n=n, t_=t_: e.dma_start(out=dbg[n][0][:, :, :], in_=t_[:, :, :]), dbg[n][1], reads=[B_])
            def ep_res(gidx):
                def ep(oc, pt, Bp):
                    for (a, b, var) in parts:
                        P.op("dve", lambda e, a=a, b=b, var=var: e.scalar_tensor_tensor(out=xs[:, oc, a:b], in0=pt[:, a:b], scalar=m2[:, oc, var, gidx:gidx + 1],
                                                                                       in1=xs[:, oc, a:b], op0=ALU.mult, op1=ALU.add), reads=[Bp, Bm2, Bxs], writes=[Bxs])
                return ep
            mm_group(w_out, 16, 16, lambda k: hT[:, k, :], BhT, ep_res(0))
            rmsnorm_mod_segment(cx, eps_t, xs, Bxs, ones_f, Bones, sc2, Bsc2, hT, BhT, seg, tmp_ring, ps_aux, rstd_ring, local=True, engs=("dve", "dve", "pool"))
            def ep_mlp1(oc, pt, Bp):
                t1, Bt1 = tmp_ring.next()
                P.op("act", lambda e: e.activation(out=t1[:, :], in_=pt[:, 0:SEG], func=AF.Relu), reads=[Bp], writes=[Bt1])
                P.op("act", lambda e: e.activation(out=hh[:, oc, :], in_=t1[:, :], func=AF.Square), reads=[Bt1], writes=[Bhh])
            mm_group(w1, 16, 64, lambda k: hT[:, k, :], BhT, ep_mlp1)
            mm_group(w2, 64, 16, lambda k: hh[:, k, :], Bhh, ep_res(3))
            first_seg[0] = False
            if not final:
                P.dma("sp", lambda e, t0=t0: e.dma_start(out=o_x[:, :, t0:t0 + SEG].rearrange("k p t -> p k t"), in_=xs[:, :, :]), Box, reads=[Bxs])
            else:
                pt, Bp = ps_aux.next()
                for kc in range(KC):
                    sq, Bsq = tmp_ring.next()
                    P.op("act", lambda e, sq=sq, kc=kc: e.activation(out=sq[:, :], in_=xs[:, kc, :], func=AF.Square), reads=[Bxs], writes=[Bsq])
                    P.op("pe", lambda e, sq=sq, kc=kc, pt=pt: e.matmul(pt[:, 0:SEG], lhsT=ones_f[:, :], rhs=sq[:, :], start=(kc == 0), stop=(kc == KC - 1)),
                         reads=[Bsq, Bones], writes=[Bp])
                rs, Brs = rstd_ring.next()
                P.op("act", lambda e, rs=rs, pt=pt: e.activation(out=rs[:, :], in_=pt[:, 0:SEG], func=AF.Sqrt, scale=1.0 / D, bias=eps_t[:, 0:1]), reads=[Bp, Beps], writes=[Brs])
                P.op("dve", lambda e, rs=rs: e.reciprocal(out=rs[:, :], in_=rs[:, :]), reads=[Brs], writes=[Brs])
                for kc in range(KC):
                    og, Bog_ = ostg.next()
                    P.op("dve", lambda e, og=og, kc=kc, rs=rs: e.scalar_tensor_tensor(out=og[:, :], in0=xs[:, kc, :], scalar=fg[:, kc:kc + 1], in1=rs[:, :],
                                                                                     op0=ALU.mult, op1=ALU.mult), reads=[Bxs, Bfg, Brs], writes=[Bog_])
                    P.dma("sp", lambda e, og=og, kc=kc, t0=t0: e.dma_start(out=o_x[kc, :, t0:t0 + SEG], in_=og[:, :]), Box, reads=[Bog_])
        cx.finish([Box] + ([dbg[n][1] for n in dbg] if debug_seg is not None else []))
    return nc, P


L = 8192; C = 256; B = 2


def core_tokens_T(x_lat, x_ctx, core):
    b, s = core // 4, core % 4
    t = np.concatenate([x_lat[b, s * NLAT:(s + 1) * NLAT], x_ctx[b, s * NCTX:(s + 1) * NCTX]], axis=0)
    return np.ascontiguousarray(t.T)


def rope_tables(core):
    s = core % 4
    t = np.arange(s * NLAT, (s + 1) * NLAT, dtype=np.float32)
    inv_freq = (10000.0 ** (-np.arange(0, 64, 2, dtype=np.float32) / 64)).astype(np.float32)
    d = np.arange(128)
    a = d // 64; i = d % 32; half = (d % 64) // 32
    pos = np.where(a[:, None] == 0, np.floor(t / 64)[None, :], np.mod(t, 64)[None, :]).astype(np.float32)
    ang = pos * inv_freq[i][:, None]
    cosT = np.ones((128, NT), np.float32); sinT = np.zeros((128, NT), np.float32)
    cosT[:, :NLAT] = np.cos(ang)
    sinT[:, :NLAT] = np.sin(ang) * np.where(half == 0, -1.0, 1.0)[:, None]
    return cosT, sinT


def perm_matrix():
    d = np.arange(128)
    partner = np.where((d % 64) < 32, d + 32, d - 32)
    Pm = np.zeros((128, 128), np.float32)
    Pm[partner, d] = 1.0
    return Pm


def colT(v, nchunks):
    return np.ascontiguousarray(v.reshape(nchunks, 128).T)


def ssm_params(d, l, chunk):
    rowp = np.zeros((2, 3, 128, 128), np.float32); colp = np.zeros((2, 3, 128, 4), np.float32)
    BT = np.zeros((2, 2, 128, 128), np.float32); CT = np.zeros((2, 2, 128, 4, 128), np.float32)
    for dr in range(2):
        srcs = [d["ssm_a_re"][l, dr], d["ssm_a_im"][l, dr], np.repeat(d["ssm_log_dt"][l, dr][:, None], 64, axis=1)]
        bs = [d["ssm_b_re"][l, dr], d["ssm_b_im"][l, dr]]
        cs = [d["ssm_c_re"][l, dr], d["ssm_c_im"][l, dr]]
        for q in range(4):
            gA = 8 * chunk + 2 * q; gB = gA + 1
            for k in range(3):
                row = np.concatenate([srcs[k][gA], srcs[k][gB]])
                rowp[dr, k, 32 * q:32 * q + 32, :] = row[None, :]
                colp[dr, k, :, q] = row
            for k in range(2):
                BT[dr, k, 32 * q:32 * q + 16, 0:64] = bs[k][gA].T
                BT[dr, k, 32 * q + 16:32 * q + 32, 64:128] = bs[k][gB].T
                CT[dr, k, 0:64, q, 32 * q:32 * q + 16] = cs[k][gA].T
                CT[dr, k, 64:128, q, 32 * q + 16:32 * q + 32] = cs[k][gB].T
    dvec = np.ascontiguousarray(d["ssm_d"][l][128 * chunk:128 * chunk + 128, None])
    return {"rowp": rowp, "colp": colp, "BT": BT, "CT": CT, "dvec": dvec}


def conv_windows(uc_lat, uc_ctx, core, pieces, wtot):
    b, s = core // 4, core % 4
    out = np.zeros((wtot, 1024), np.float32)
    o = 0
    for (ps0, pn) in pieces:
        if ps0 < NLAT:
            src = uc_lat[b]; g0 = s * NLAT + ps0
        else:
            src = uc_ctx[b]; g0 = s * NCTX + (ps0 - NLAT)
        lo, hi = g0 - 15, g0 + pn + 15
        a, e = max(lo, 0), min(hi, src.shape[0])
        out[o + (a - lo):o + (e - lo)] = src[a:e]
        o += pn + 30
    return np.ascontiguousarray(out.T).reshape(8, 128, wtot)


_PROGS = {}


def _prog(name):
    if name not in _PROGS:
        if name == "m":
            _PROGS[name] = build_stage_m()
        elif name == "a":
            _PROGS[name] = build_stage_a()[0]
        elif name == "b1":
            _PROGS[name] = build_stage_b1()[0]
        elif name == "b2":
            _PROGS[name] = build_stage_b2()[0]
        elif name == "c":
            _PROGS[name] = build_stage_c(False)[0]
        elif name == "cf":
            _PROGS[name] = build_stage_c(True)[0]
    return _PROGS[name]


def _run(name, in_maps):
    res = run_bass_kernel_spmd(_prog(name), in_maps, core_ids=list(range(8)))
    return [{k: np.asarray(v) for k, v in r.items()} for r in res.results]


def kernel(x, c, ctx, c_ctx, norm1_g, norm2_g, w_mod, b_mod, w_in, b_gate, q_norm_g, k_norm_g, w_attn_o,
           ssm_a_re, ssm_a_im, ssm_log_dt, ssm_b_re, ssm_b_im, ssm_c_re, ssm_c_im, ssm_d, w_glu, b_glu, w_ssm_o,
           conv_w, conv_b, conv_ln_g, conv_ln_b, w_conv_o, w_out, w_mlp1, w_mlp2, final_g):
    f32 = np.float32
    d = dict(ssm_a_re=np.asarray(ssm_a_re, f32), ssm_a_im=np.asarray(ssm_a_im, f32), ssm_log_dt=np.asarray(ssm_log_dt, f32),
             ssm_b_re=np.asarray(ssm_b_re, f32), ssm_b_im=np.asarray(ssm_b_im, f32), ssm_c_re=np.asarray(ssm_c_re, f32),
             ssm_c_im=np.asarray(ssm_c_im, f32), ssm_d=np.asarray(ssm_d, f32))
    x = np.asarray(x, f32); ctx = np.asarray(ctx, f32); w_mod = np.asarray(w_mod, f32); b_mod = np.asarray(b_mod, f32)
    DEPTH = 4
    c_all = np.concatenate([np.asarray(c, f32), np.asarray(c_ctx, f32)[None, :]], axis=0)
    cT = np.ascontiguousarray(c_all.reshape(3, 16, 128).transpose(2, 1, 0))
    in_maps = []
    for core in range(8):
        wm = np.empty((48, 128, 16, 128), f32); bm = np.empty((128, 48), f32)
        for u in range(48):
            l, j = divmod(core * 48 + u, 96)
            wm[u] = w_mod[l][:, j * 128:(j + 1) * 128].reshape(16, 128, 128).transpose(1, 0, 2)
            bm[:, u] = b_mod[l][j * 128:(j + 1) * 128]
        in_maps.append({"wm": wm, "cT": cT, "bm": bm})
    outs = _run("m", in_maps)
    mod = np.empty((DEPTH, 3, 12288), f32)
    for core in range(8):
        for u in range(48):
            l, j = divmod(core * 48 + u, 96)
            mod[l, :, j * 128:(j + 1) * 128] = outs[core]["modT"][:, u, :].T
    del in_maps
    xT = [core_tokens_T(x, ctx, core).reshape(16, 128, NT) for core in range(8)]
    Pm = perm_matrix()
    ropes = [rope_tables(core) for core in range(8)]
    for l in range(DEPTH):
        w_in_l = np.ascontiguousarray(np.asarray(w_in[l], f32))
        g1T = colT(np.asarray(norm1_g[l], f32), 16)
        qkg = np.stack([np.asarray(q_norm_g[l], f32), np.asarray(k_norm_g[l], f32)], axis=1)
        bgT = colT(np.asarray(b_gate[l], f32), 48)
        in_maps = []
        for core in range(8):
            b = core // 4
            modss = np.empty((128, 16, 2, 2), f32)
            for vi, v in enumerate((b, 2)):
                modss[:, :, vi, 0] = colT(mod[l, v, 0:2048], 16)
                modss[:, :, vi, 1] = colT(mod[l, v, 2048:4096], 16)
            in_maps.append({"xT": xT[core], "w_in": w_in_l, "g1T": g1T, "modss": modss, "qkg": qkg,
                            "cosT": ropes[core][0], "sinT": ropes[core][1], "permM": Pm, "bgT": bgT})
        oa = _run("a", in_maps)
        del in_maps, w_in_l
        kT_all = []; v_all = []
        for b in range(2):
            kk = np.empty((2, 128, NKEY), oa[0]["kT"].dtype); vv = np.empty((NKEY, 256), oa[0]["v"].dtype)
            for s in range(4):
                o = oa[4 * b + s]
                kk[:, :, s * NLAT:(s + 1) * NLAT] = o["kT"][:, :, :NLAT]; kk[:, :, L + s * NCTX:L + (s + 1) * NCTX] = o["kT"][:, :, NLAT:]
                vv[s * NLAT:(s + 1) * NLAT] = o["v"][:NLAT]; vv[L + s * NCTX:L + (s + 1) * NCTX] = o["v"][NLAT:]
            kT_all.append(kk); v_all.append(vv)
        ob1 = _run("b1", [{"qT": oa[core]["qT"], "kT": kT_all[core // 4], "v": v_all[core // 4]} for core in range(8)])
        del kT_all, v_all
        in_maps = []
        for k in range(8):
            m = ssm_params(d, l, k)
            uT = np.empty((128, 2, NKEY), f32)
            for b in range(2):
                for s in range(4):
                    o = oa[4 * b + s]["uT"][k]
                    uT[:, b, C + s * NLAT:C + (s + 1) * NLAT] = o[:, :NLAT]; uT[:, b, s * NCTX:(s + 1) * NCTX] = o[:, NLAT:]
            m["uT"] = uT
            in_maps.append(m)
        ob2 = _run("b2", in_maps)
        del in_maps
        uc_lat = np.empty((2, L, 1024), f32); uc_ctx = np.empty((2, C, 1024), f32)
        for core in range(8):
            b, s = core // 4, core % 4
            u2 = oa[core]["ucT"].reshape(1024, NT)
            uc_lat[b, s * NLAT:(s + 1) * NLAT] = u2[:, :NLAT].T; uc_ctx[b, s * NCTX:(s + 1) * NCTX] = u2[:, NLAT:].T
        Wl = {k: np.ascontiguousarray(np.asarray(v[l], f32)) for k, v in (("w_glu", w_glu), ("w_ssm_o", w_ssm_o), ("w_conv_o", w_conv_o),
                                                                          ("w_attn_o", w_attn_o), ("w_out", w_out), ("w_mlp1", w_mlp1), ("w_mlp2", w_mlp2))}
        vec8 = np.ascontiguousarray(np.stack([colT(np.asarray(t[l], f32), 8) for t in (b_glu, conv_b, conv_ln_g, conv_ln_b)], axis=2))
        convw = np.ascontiguousarray(np.asarray(conv_w[l], f32).T.reshape(8, 128, 31).transpose(1, 0, 2))
        g2T = colT(np.asarray(norm2_g[l], f32), 16); fgT = colT(np.asarray(final_g, f32), 16)
        in_maps = []
        for core in range(8):
            b, s = core // 4, core % 4
            m = dict(Wl)
            m["xT"] = xT[core]; m["attT"] = ob1[core]["attT"]; m["gT"] = oa[core]["gT"]
            ys = np.empty((8, 128, NT), f32)
            for k in range(8):
                ys[k, :, :NLAT] = ob2[k]["ysT"][:, b, C + s * NLAT:C + (s + 1) * NLAT]; ys[k, :, NLAT:] = ob2[k]["ysT"][:, b, s * NCTX:(s + 1) * NCTX]
            m["ysT"] = ys
            m["ucH"] = conv_windows(uc_lat, uc_ctx, core, CONV_PIECES, WTOT)
            m["vec8"] = vec8; m["convw"] = convw; m["g2T"] = g2T; m["fgT"] = fgT
            mod2 = np.empty((128, 16, 2, 4), f32)
            for vi, v in enumerate((b, 2)):
                for i, mi in enumerate((2, 3, 4, 5)):
                    mod2[:, :, vi, i] = colT(mod[l, v, mi * 2048:(mi + 1) * 2048], 16)
            m["mod2"] = mod2
            in_maps.append(m)
        oc = _run("cf" if l == DEPTH - 1 else "c", in_maps)
        del in_maps, oa, ob1, ob2
        xT = [oc[core]["xT_out"] for core in range(8)]
    out = np.empty((2, L, D), f32)
    for core in range(8):
        b, s = core // 4, core % 4
        out[b, s * NLAT:(s + 1) * NLAT] = xT[core].reshape(D, NT)[:, :NLAT].T
    return out
```
